# Optimizing a Trainium2 kernel written in Bass

```python
import math
import jax, jax.numpy as jnp
from jax import lax
import numpy as np

D_MODEL = 1024
BATCH = 8
SEQ = 2048
DEPTH = 4

GROUP = D_MODEL // 4
D_MIX = 4 * GROUP
A_HEADS = 4
A_QK = GROUP // (2 * A_HEADS)
A_V = GROUP // A_HEADS
B_HEADS = 4
B_HD = GROUP // B_HEADS
C_BLOCKS = 4
C_BD = GROUP // C_BLOCKS
C_CONV = 4
C_EXP = 8.0
D_HEADS = 4
D_HD = GROUP // D_HEADS
D_W_RANK = 32
D_A_RANK = 32
D_V_RANK = 32
D_G_RANK = 64
D_FF = 256 * ((8 * D_MODEL // 3 + 255) // 256)
FF_CONV = 3
Q_BLOCK = 128
RMS_EPS = 1e-6
LNX_EPS = 64e-5
ADA_CHUNKS = 6

N_A = 3 * GROUP
N_B = 3 * GROUP
N_C = 2 * GROUP
N_D = 3 * GROUP + D_W_RANK + D_A_RANK + D_G_RANK
N_IN = N_A + N_B + N_C + N_D
GROUP_SPLITS = (N_A, N_A + N_B, N_A + N_B + N_C)
D_SPLITS = (GROUP, 2 * GROUP, 3 * GROUP, 3 * GROUP + D_W_RANK, 3 * GROUP + D_W_RANK + D_A_RANK)

kernel_name = 'hybrid_parallel_head_group_trunk'


def rms_last(x, g, eps=RMS_EPS):
    xf = x.astype(jnp.float32)
    y = xf * lax.rsqrt(jnp.mean(jnp.square(xf), axis=-1, keepdims=True) + eps)
    return (y * g.astype(jnp.float32)).astype(x.dtype)


def token_shift(x):
    return jnp.pad(x, ((0, 0), (1, 0), (0, 0)))[:, :-1]


def causal_dwconv(x, w, b):
    k_w, ch = w.shape
    y = lax.conv_general_dilated(x, w[:, None, :].astype(x.dtype), window_strides=(1,),
                                 padding=[(k_w - 1, 0)], dimension_numbers=('NWC', 'WIO', 'NWC'),
                                 feature_group_count=ch)
    return y + b.astype(x.dtype)


def sweep_query_blocks(fn, q):
    bsz, seq = q.shape[:2]
    nb = seq // Q_BLOCK
    qb = jnp.moveaxis(q.reshape((bsz, nb, Q_BLOCK) + q.shape[2:]), 1, 0)
    out = lax.map(lambda args: fn(args[0], args[1]), (qb, jnp.arange(nb) * Q_BLOCK))
    out = jnp.moveaxis(out, 0, 1)
    return out.reshape((bsz, seq) + out.shape[3:])


def diff_attention(q, k, v, lam):
    kpos = jnp.arange(k.shape[1])
    scale = A_QK ** -0.5

    def block(qb, start):
        qpos = start + jnp.arange(Q_BLOCK)
        causal = kpos[None, :] <= qpos[:, None]
        s = jnp.einsum('bqhcd,bkhcd->bhcqk', qb, k).astype(jnp.float32) * scale
        p = jax.nn.softmax(jnp.where(causal, s, -jnp.inf), axis=-1)
        wts = p[:, :, 0] - lam * p[:, :, 1]
        return jnp.einsum('bhqk,bkhd->bqhd', wts.astype(v.dtype), v)

    return sweep_query_blocks(block, q)


def stick_breaking_attention(q, k, v):
    kpos = jnp.arange(k.shape[1])
    scale = B_HD ** -0.5

    def block(qb, start):
        qpos = start + jnp.arange(Q_BLOCK)
        strict = kpos[None, :] < qpos[:, None]
        z = jnp.einsum('bqhd,bkhd->bhqk', qb, k).astype(jnp.float32) * scale
        log_beta = jax.nn.log_sigmoid(z)
        log_keep = jnp.where(strict, log_beta - z, 0.0)
        log_after = lax.cumsum(log_keep, axis=3, reverse=True) - log_keep
        wts = jnp.where(strict, jnp.exp(log_beta + log_after), 0.0)
        return jnp.einsum('bhqk,bkhd->bqhd', wts.astype(v.dtype), v)

    return sweep_query_blocks(block, q)


def _linear_combine(lhs, rhs):
    a_l, b_l = lhs
    a_r, b_r = rhs
    return a_l * a_r, a_r * b_l + b_r


def rg_lru(x, ga_w, ga_b, gx_w, gx_b, lam):
    bsz, seq, wd = x.shape
    xb = x.reshape(bsz, seq, C_BLOCKS, C_BD)
    r = jax.nn.sigmoid((jnp.einsum('bsnd,nde->bsne', xb, ga_w).reshape(bsz, seq, wd) + ga_b).astype(jnp.float32))
    i = jax.nn.sigmoid((jnp.einsum('bsnd,nde->bsne', xb, gx_w).reshape(bsz, seq, wd) + gx_b).astype(jnp.float32))
    log_a = -C_EXP * r * jax.nn.softplus(-lam.astype(jnp.float32))
    a = jnp.exp(log_a)
    mult = jnp.sqrt(-jnp.expm1(2.0 * log_a))
    mult = jnp.where((jnp.arange(seq) == 0)[None, :, None], 1.0, mult)
    u = mult * i * x.astype(jnp.float32)
    _, h = lax.associative_scan(_linear_combine, (a, u), axis=1)
    return h.astype(x.dtype)


def rwkv7_scan(r, decay, k, v, kk, a):
    bsz, seq, nh, hd = r.shape
    xs = tuple(jnp.moveaxis(t, 1, 0) for t in (r, decay, k, v, kk, a))

    def step(state, inp):
        r_t, w_t, k_t, v_t, kk_t, a_t = inp
        s_kk = jnp.einsum('bhvk,bhk->bhv', state, kk_t)
        state = (state * w_t[:, :, None, :]
                 - s_kk[..., None] * (kk_t * a_t)[:, :, None, :]
                 + v_t[..., None] * k_t[:, :, None, :])
        return state, jnp.einsum('bhvk,bhk->bhv', state, r_t)

    state0 = jnp.zeros((bsz, nh, hd, hd), jnp.float32)
    _, ys = lax.scan(step, state0, xs)
    return jnp.moveaxis(ys, 0, 1)


def mixer_diff_attn(p, q_g, k_g, lam_q, lam_k, out_g, lam_init):
    bsz, seq, _ = p.shape
    q, k, v = jnp.split(p, 3, axis=-1)
    q = rms_last(q.reshape(bsz, seq, A_HEADS, 2, A_QK), q_g)
    k = rms_last(k.reshape(bsz, seq, A_HEADS, 2, A_QK), k_g)
    v = v.reshape(bsz, seq, A_HEADS, A_V)
    lq = lam_q.astype(jnp.float32)
    lk = lam_k.astype(jnp.float32)
    lam = jnp.exp(jnp.sum(lq[0] * lk[0])) - jnp.exp(jnp.sum(lq[1] * lk[1])) + lam_init
    o = diff_attention(q, k, v, lam)
    o = rms_last(o, out_g.reshape(A_HEADS, A_V)) * (1.0 - lam_init)
    return o.reshape(bsz, seq, GROUP)


def mixer_stick_breaking(p, out_g):
    bsz, seq, _ = p.shape
    q, k, v = (t.reshape(bsz, seq, B_HEADS, B_HD) for t in jnp.split(p, 3, axis=-1))
    o = stick_breaking_attention(q, k, v)
    o = rms_last(o, out_g.reshape(B_HEADS, B_HD))
    return o.reshape(bsz, seq, GROUP)


def mixer_rglru(p, conv_w, conv_b, ga_w, ga_b, gx_w, gx_b, lam, out_g):
    bsz, seq, _ = p.shape
    xr, xg = jnp.split(p, 2, axis=-1)
    xr = causal_dwconv(xr, conv_w, conv_b)
    y = rg_lru(xr, ga_w, ga_b, gx_w, gx_b, lam) * jax.nn.gelu(xg)
    y = rms_last(y.reshape(bsz, seq, C_BLOCKS, C_BD), out_g.reshape(C_BLOCKS, C_BD))
    return y.reshape(bsz, seq, GROUP)


def mixer_rwkv7(p, mu, w0, w_up, a0, a_up, g_up, k_k, k_a, r_k, lnx_w, lnx_b, v_first, v_res):
    bsz, seq, _ = p.shape
    f32 = jnp.float32
    p = p + (token_shift(p) - p) * mu
    r, k, v, w_lr, a_lr, g_lr = jnp.split(p, D_SPLITS, axis=-1)
    w = -jax.nn.softplus(-(w0 + jnp.tanh(w_lr) @ w_up).astype(f32)) - 0.5
    a = jax.nn.sigmoid((a0 + a_lr @ a_up).astype(f32))
    g = jax.nn.sigmoid(g_lr) @ g_up
    if v_res is not None:
        v0, v_down, v_up = v_res
        v = v + (v_first - v) * jax.nn.sigmoid(v0 + (v @ v_down) @ v_up)

    def heads(t):
        return t.astype(f32).reshape(bsz, seq, D_HEADS, D_HD)

    rh, kh, vh, ah = heads(r), heads(k), heads(v), heads(a)
    kk = kh * k_k.astype(f32).reshape(D_HEADS, D_HD)
    kk = kk / jnp.maximum(jnp.sqrt(jnp.sum(kk * kk, axis=-1, keepdims=True)), 1e-12)
    kh = kh * (1.0 + (ah - 1.0) * k_a.astype(f32).reshape(D_HEADS, D_HD))
    decay = jnp.exp(-jnp.exp(heads(w)))
    y = rwkv7_scan(rh, decay, kh, vh, kk, ah)
    mean = jnp.mean(y, axis=-1, keepdims=True)
    var = jnp.mean(jnp.square(y - mean), axis=-1, keepdims=True)
    y = ((y - mean) * lax.rsqrt(var + LNX_EPS)).reshape(bsz, seq, GROUP)
    y = y * lnx_w.astype(f32) + lnx_b.astype(f32)
    bonus = jnp.sum(rh * kh * r_k.astype(f32), axis=-1, keepdims=True) * vh
    y = y + bonus.reshape(bsz, seq, GROUP)
    return y.astype(p.dtype) * g, v


def setup_inputs(seed: int = 0) -> dict:
    key = jax.random.key(seed)
    keys = iter(jax.random.split(key, 48))
    f32 = jnp.float32

    def nrm(shape, std):
        return std * jax.random.normal(next(keys), shape, f32)

    def gain(shape):
        return 1.0 + nrm(shape, 0.02)

    L = DEPTH
    u = jax.random.uniform(next(keys), (L, GROUP), f32, 0.9, 0.999)
    s = u ** (1.0 / C_EXP)
    c_lambda = jnp.log(s) - jnp.log1p(-s)
    return {
        'x': nrm((BATCH, SEQ, D_MODEL), 1.0),
        'c': nrm((BATCH, D_MODEL), 1.0),
        'w_ada': nrm((L, D_MODEL, ADA_CHUNKS * D_MODEL), 0.5 * D_MODEL ** -0.5),
        'b_ada': nrm((L, ADA_CHUNKS * D_MODEL), 0.02),
        'norm1_g': gain((L, D_MODEL)),
        'norm2_g': gain((L, D_MODEL)),
        'w_in': nrm((L, D_MODEL, N_IN), D_MODEL ** -0.5),
        'w_out': nrm((L, D_MIX, D_MODEL), D_MIX ** -0.5),
        'a_qnorm_g': gain((L, A_QK)),
        'a_knorm_g': gain((L, A_QK)),
        'a_lam_q': nrm((L, 2, A_QK), 0.1),
        'a_lam_k': nrm((L, 2, A_QK), 0.1),
        'a_out_g': gain((L, GROUP)),
        'b_out_g': gain((L, GROUP)),
        'c_conv_w': nrm((L, C_CONV, GROUP), C_CONV ** -0.5),
        'c_conv_b': nrm((L, GROUP), 0.02),
        'c_gate_a_w': nrm((L, C_BLOCKS, C_BD, C_BD), C_BD ** -0.5),
        'c_gate_a_b': nrm((L, GROUP), 0.02),
        'c_gate_x_w': nrm((L, C_BLOCKS, C_BD, C_BD), C_BD ** -0.5),
        'c_gate_x_b': nrm((L, GROUP), 0.02),
        'c_lambda': c_lambda,
        'c_out_g': gain((L, GROUP)),
        'd_mu': jax.random.uniform(next(keys), (L, N_D), f32),
        'd_w0': jnp.linspace(-6.0, -1.0, GROUP, dtype=f32)[None, :] + nrm((L, GROUP), 0.1),
        'd_w_up': nrm((L, D_W_RANK, GROUP), 0.1 * D_W_RANK ** -0.5),
        'd_a0': nrm((L, GROUP), 0.1),
        'd_a_up': nrm((L, D_A_RANK, GROUP), 0.5 * D_A_RANK ** -0.5),
        'd_g_up': nrm((L, D_G_RANK, GROUP), D_G_RANK ** -0.5),
        'd_k_k': 0.85 + nrm((L, GROUP), 0.02),
        'd_k_a': gain((L, GROUP)),
        'd_r_k': nrm((L, D_HEADS, D_HD), 0.1),
        'd_lnx_w': gain((L, GROUP)),
        'd_lnx_b': nrm((L, GROUP), 0.02),
        'd_v0': nrm((L - 1, GROUP), 0.1),
        'd_v_down': nrm((L - 1, GROUP, D_V_RANK), GROUP ** -0.5),
        'd_v_up': nrm((L - 1, D_V_RANK, GROUP), 0.5 * D_V_RANK ** -0.5),
        'ff_w_up': nrm((L, D_MODEL, 2 * D_FF), D_MODEL ** -0.5),
        'ff_conv_w': nrm((L, FF_CONV, 2 * D_FF), FF_CONV ** -0.5),
        'ff_conv_b': nrm((L, 2 * D_FF), 0.02),
        'ff_w_down': nrm((L, D_FF, D_MODEL), D_FF ** -0.5),
    }


def reference(x, c, w_ada, b_ada, norm1_g, norm2_g, w_in, w_out,
              a_qnorm_g, a_knorm_g, a_lam_q, a_lam_k, a_out_g, b_out_g,
              c_conv_w, c_conv_b, c_gate_a_w, c_gate_a_b, c_gate_x_w, c_gate_x_b, c_lambda, c_out_g,
              d_mu, d_w0, d_w_up, d_a0, d_a_up, d_g_up, d_k_k, d_k_a, d_r_k, d_lnx_w, d_lnx_b,
              d_v0, d_v_down, d_v_up,
              ff_w_up, ff_conv_w, ff_conv_b, ff_w_down):
    cond = jax.nn.silu(c)
    v_first = None
    for l in range(DEPTH):
        mod = cond @ w_ada[l] + b_ada[l]
        sh1, sc1, gt1, sh2, sc2, gt2 = jnp.split(mod[:, None, :], ADA_CHUNKS, axis=-1)

        h = rms_last(x, norm1_g[l]) * (1.0 + sc1) + sh1
        p_a, p_b, p_c, p_d = jnp.split(h @ w_in[l], GROUP_SPLITS, axis=-1)
        lam_init = 0.8 - 0.6 * math.exp(-0.3 * l)
        y_a = mixer_diff_attn(p_a, a_qnorm_g[l], a_knorm_g[l], a_lam_q[l], a_lam_k[l], a_out_g[l], lam_init)
        y_b = mixer_stick_breaking(p_b, b_out_g[l])
        y_c = mixer_rglru(p_c, c_conv_w[l], c_conv_b[l], c_gate_a_w[l], c_gate_a_b[l],
                          c_gate_x_w[l], c_gate_x_b[l], c_lambda[l], c_out_g[l])
        v_res = None if l == 0 else (d_v0[l - 1], d_v_down[l - 1], d_v_up[l - 1])
        y_d, v_d = mixer_rwkv7(p_d, d_mu[l], d_w0[l], d_w_up[l], d_a0[l], d_a_up[l], d_g_up[l],
                               d_k_k[l], d_k_a[l], d_r_k[l], d_lnx_w[l], d_lnx_b[l], v_first, v_res)
        if l == 0:
            v_first = v_d
        mix = jnp.concatenate([y_a, y_b, y_c, y_d], axis=-1) @ w_out[l]
        x = x + gt1 * mix

        h = rms_last(x, norm2_g[l]) * (1.0 + sc2) + sh2
        u = causal_dwconv(h @ ff_w_up[l], ff_conv_w[l], ff_conv_b[l])
        u_g, u_v = jnp.split(u, 2, axis=-1)
        x = x + gt2 * ((jax.nn.silu(u_g) * u_v) @ ff_w_down[l])
    return x
```

```python
import math
from contextlib import ExitStack

import numpy as np
import concourse.bass as bass
import concourse.mybir as mybir
from concourse.bass_utils import run_bass_kernel_spmd

F32 = mybir.dt.float32
BF16 = mybir.dt.bfloat16
AF = mybir.ActivationFunctionType
ALU = mybir.AluOpType
AX = mybir.AxisListType

S_ = 2048
D_ = 1024
NL = 4
GROUP = 256
N_IN = 2944
D_FF = 2816
NT = 4
D_STOP = 99
D_XI = (0, 1, 2)
D_NOEV = False
D_VAR = 0
RMS_EPS = 1e-6
LNX_EPS = 64e-5

PCOLS = {}
_off = 0
for _n, _k in (("norm1_g", 8), ("norm2_g", 8), ("b_ada", 48), ("gqA", 1), ("gkA0", 1), ("gkA1", 1),
               ("lq", 64), ("lk", 64), ("a_out_g", 2), ("b_out_g", 2),
               ("c_conv_w", 8), ("c_conv_b", 2), ("c_ga_b", 2), ("c_gx_b", 2), ("c_lambda", 2), ("c_out_g", 2),
               ("d_mu", 7), ("d_w0", 2), ("d_a0", 2), ("d_k_k", 2), ("d_k_a", 2), ("d_r_k", 2),
               ("d_lnx_w", 2), ("d_lnx_b", 2), ("d_v0", 2), ("ff_conv_w", 132), ("ff_conv_b", 44)):
    PCOLS[_n] = (_off, _k)
    _off += _k
NCOL = _off

CC = {}
_off = 0
for _n, _k in (("ident", 128), ("maskLE", 128), ("maskLT", 128), ("blk32", 128), ("blk64m", 128),
               ("blk64s", 128), ("onesD", 128), ("dmask", 4 * 320), ("id64x4", 4 * 64)):
    CC[_n] = (_off, _k)
    _off += _k
CCOL = _off


def _fm(v):
    return np.ascontiguousarray(np.asarray(v, np.float32).reshape(-1, 128).T)


def _consts():
    c = np.zeros((128, CCOL), np.float32)
    i = np.arange(128)
    def put(n, a):
        o, k = CC[n]
        c[:a.shape[0], o:o + k] = a
    put("ident", np.eye(128, dtype=np.float32))
    put("maskLE", (i[None, :] >= i[:, None]).astype(np.float32))
    put("maskLT", (i[None, :] < i[:, None]).astype(np.float32))
    put("blk32", ((i[:, None] // 32) == (i[None, :] // 32)).astype(np.float32) / 32.0)
    put("blk64m", ((i[:, None] // 64) == (i[None, :] // 64)).astype(np.float32) / 64.0)
    put("blk64s", ((i[:, None] // 64) == (i[None, :] // 64)).astype(np.float32))
    put("onesD", np.full((128, 128), 1.0 / 1024.0, np.float32))
    j = np.arange(64)
    gt = (j[None, :] > j[:, None]).astype(np.float32)
    ge = (j[None, :] >= j[:, None]).astype(np.float32)
    lt = (j[None, :] < j[:, None]).astype(np.float32)
    m = np.concatenate([-gt, -ge, gt, ge, -lt], 1)
    put("dmask", np.tile(m, (1, 4)))
    put("id64x4", np.tile(np.eye(64, dtype=np.float32), (1, 4)))
    return c


class _Op:
    __slots__ = ("eng", "fn", "deps", "is_dma", "sem", "val", "waited")

    def __init__(self, eng, fn, is_dma):
        self.eng = eng
        self.fn = fn
        self.deps = []
        self.is_dma = is_dma
        self.sem = None
        self.val = 0
        self.waited = False


class Sched:
    ENGS = ("pe", "act", "dve", "pool", "sp")

    def __init__(self, nc):
        self.nc = nc
        self.streams = {e: [] for e in self.ENGS}
        self.lastw = {}
        self.readers = {}
        self.dma_sems = {}
        self.all_ops = []
        self.pending = []

    def op(self, eng, fn, reads=(), writes=(), dma_key=None):
        o = _Op(eng, fn, dma_key is not None)
        self.all_ops.append(o)
        seen = set()
        def add(d):
            if d is not None and id(d) not in seen:
                seen.add(id(d))
                o.deps.append(d)
        for k in reads:
            add(self.lastw.get(k))
        for k in writes:
            add(self.lastw.get(k))
            for r in self.readers.get(k, ()):
                add(r)
        for k in writes:
            self.lastw[k] = o
            self.readers[k] = []
        wset = set(writes)
        for k in reads:
            if k not in wset:
                self.readers.setdefault(k, []).append(o)
        if dma_key is not None:
            ent = self.dma_sems.get(dma_key)
            if ent is None:
                ent = [self.nc.alloc_semaphore("d%d" % len(self.dma_sems)), 0]
                self.dma_sems[dma_key] = ent
            ent[1] += 16
            o.sem = ent[0]
            o.val = ent[1]
        self.streams[eng].append(o)
        self.last_eng = getattr(self, "last_eng", {})
        self.last_eng[eng] = o
        if o.is_dma:
            self.pending.append(o)
        return o

    def mark(self, name):
        self.marks = getattr(self, "marks", [])
        self.marks.append((name, {e: sum(1 for o in self.streams[e] if o.fn is not None) for e in self.ENGS}))

    def barrier(self):
        prev = [self.last_eng[e] for e in self.ENGS if e != "pool" and e in self.last_eng and not self.last_eng[e].is_dma]
        prev = [p for p in prev if p.fn is not None]
        dmas = list(self.pending)
        self.pending = []
        for e in self.ENGS:
            if e == "pool":
                continue
            o = _Op(e, None, False)
            o.deps = [p for p in prev] + dmas
            self.all_ops.append(o)
            self.streams[e].append(o)

    def emit(self, final_wait_ops=()):
        nc = self.nc
        eng_sem = {e: nc.alloc_semaphore("e_" + e) for e in self.ENGS}
        for o in self.all_ops:
            for d in o.deps:
                if d.is_dma:
                    d.waited = True
                elif d.eng == o.eng and o.eng == "pe" and o.fn is not None:
                    continue
                else:
                    d.waited = True
        for e in self.ENGS:
            cnt = 0
            for o in self.streams[e]:
                if o.is_dma or o.fn is None:
                    continue
                if o.waited:
                    cnt += 1
                    o.sem = eng_sem[e]
                    o.val = cnt
        self.nwaits = {e: 0 for e in self.ENGS}
        self.ninst = {e: len(self.streams[e]) for e in self.ENGS}

        def run(engname, eng):
            known = {}
            for o in self.streams[engname]:
                need = {}
                for d in o.deps:
                    if (not d.is_dma) and d.eng == engname and engname == "pe" and o.fn is not None:
                        continue
                    if d.sem is None:
                        continue
                    key = id(d.sem)
                    if known.get(key, 0) >= d.val:
                        continue
                    if key not in need or need[key][1] < d.val:
                        need[key] = (d.sem, d.val)
                for key, (sem, val) in need.items():
                    eng.wait_ge(sem, val)
                    known[key] = val
                    self.nwaits[engname] += 1
                if o.fn is None:
                    continue
                ins = o.fn(eng)
                if o.is_dma:
                    ins.then_inc(o.sem, 16)
                elif o.waited:
                    ins.then_inc(o.sem, 1)
            if engname == "sp":
                for d in final_wait_ops:
                    eng.wait_ge(d.sem, d.val)

        with nc.Block() as block:
            @block.tensor
            def _(e):
                run("pe", e)

            @block.scalar
            def _(e):
                run("act", e)

            @block.vector
            def _(e):
                run("dve", e)

            @block.gpsimd
            def _(e):
                run("pool", e)

            @block.sync
            def _(e):
                run("sp", e)


def build_program(nl=NL, stages=("A", "B", "C", "D", "F"), dumps=()):
    nc = bass.Bass("TRN2", target_bir_lowering=False)
    dr = {}
    def din(name, shape):
        dr[name] = nc.dram_tensor(name, list(shape), F32, kind="ExternalInput").ap()
        return dr[name]
    x_d = din("x", (S_, D_))
    c_d = din("c", (128, 8))
    cst_d = din("cst", (128, CCOL))
    pv_d = din("pv", (128, NL * NCOL))
    w_ada_d = din("w_ada", (NL, D_, 6 * D_))
    w_in_d = din("w_in", (NL, D_, N_IN))
    w_out_d = din("w_out", (NL, D_, D_))
    ffu_d = din("ff_w_up", (NL, D_, 2 * D_FF))
    ffd_d = din("ff_w_down", (NL, D_FF, D_))
    gate_d = din("gatew", (NL, 2, 2, 128, 128))
    dwup_d = din("d_w_up", (NL, 32, 256))
    daup_d = din("d_a_up", (NL, 32, 256))
    dgup_d = din("d_g_up", (NL, 64, 256))
    dvdn_d = din("d_v_down", (NL - 1, 256, 32))
    dvup_d = din("d_v_up", (NL - 1, 32, 256))
    out_d = nc.dram_tensor("out", [S_, D_], F32, kind="ExternalOutput").ap()
    dump_d = {}
    for (nm, shp) in dumps:
        dump_d[nm] = nc.dram_tensor("dbg_" + nm, list(shp), F32, kind="ExternalOutput").ap()

    S = Sched(nc)
    es = ExitStack()

    _uid = [0]

    def sb(name, shape, dt, stack=None):
        _uid[0] += 1
        return (stack or es).enter_context(nc.sbuf_tensor("s%d_%s" % (_uid[0], name), list(shape), dt))

    with es:
        PS = es.enter_context(nc.psum_tensor("PS", [128, 4096], F32))
        def bank(i, w=512, off=0):
            return PS[:, i * 512 + off: i * 512 + off + w]
        def bankb(i):
            return PS[:, i * 512:(i + 1) * 512].bitcast(BF16)

        xT = sb("xT", [128, 8, S_], F32)
        hT = sb("hT", [128, 8, S_], BF16)
        yT = sb("yT", [128, 2, S_], BF16)
        pv = sb("pv", [128, NL * NCOL], F32)
        cst = sb("cst", [128, 128], F32)
        cstb = sb("cstb", [128, CCOL], BF16)
        NWB = 3
        wbufs = [sb("wbuf%d" % i, [128, 4096], BF16) for i in range(NWB)]
        modT = sb("modT", [128, 48], F32)
        lvec = sb("lvec", [128, 64], F32)
        condb = sb("condb", [128, 8], BF16)
        cin = sb("cin", [128, 8], F32)
        epsc = sb("epsc", [128, 8], F32)
        vfirst = sb("vfirst", [128, 2, S_], BF16)
        gw_p = sb("gw_p", [128, 2, 2, 128], BF16)
        dsw = {n_: sb("dsw_" + n_, [128, 256], BF16) for n_ in ("wup", "aup", "gup", "vup")}
        vdn_p = sb("vdn_p", [128, 2, 32], BF16)
        modTs = [modT, sb("modT1", [128, 48], F32)]
        lvecs = [lvec, sb("lvec1", [128, 64], F32)]

        def C(name, rows=128):
            assert name == "ident"
            return cst[0:rows, 0:128]
        def Cb(name, rows=128):
            o, k = CC[name]
            return cstb[0:rows, o:o + k]
        def P(l, name, j=0, n=1):
            o, k = PCOLS[name]
            return pv[:, l * NCOL + o + j: l * NCOL + o + j + n]

        LV_A1, LV_A2, LV_LAM, LV_CL, LV_NW0, LV_1MU, LV_TMP = 0, 8, 16, 20, 24, 28, 36

        dump_ops = []
        def dump_now(nm, ap, keys=()):
            if nm in dump_d:
                dump_ops.append(S.op("pool", lambda e: e.dma_start(out=dump_d[nm], in_=ap),
                                     reads=list(keys), dma_key=("dump", nm)))

        wstate = {"n": 0}
        def wload(srcs):
            i = wstate["n"] % NWB
            wstate["n"] += 1
            buf = wbufs[i]
            key = ("wbuf", i)
            for (src, off) in srcs:
                a, b = src.shape[1], src.shape[2]
                dst = buf[:, off:off + a * b].rearrange("p (a b) -> p a b", a=a)
                S.op("pool", lambda e, dst=dst, src=src: e.dma_start(out=dst, in_=src),
                     writes=[key], dma_key=("w", i))
            return buf, key

        def wsrc(dram2d, c0, ncols, kchunks):
            return dram2d.rearrange("(kc p) n -> p kc n", p=128)[:, 0:kchunks, c0:c0 + ncols]

        S.op("sp", lambda e: e.dma_start(out=cst[:], in_=cst_d[:, 0:128]), writes=["cst"], dma_key="cst")
        S.op("pool", lambda e: e.dma_start(out=cstb[:], in_=cst_d), writes=["cstb"], dma_key="cstb")
        S.op("sp", lambda e: e.dma_start(out=pv[:], in_=pv_d), writes=["pv"], dma_key="pv")
        S.op("sp", lambda e: e.dma_start(out=cin[:], in_=c_d), writes=["cin"], dma_key="cin")
        S.op("dve", lambda e: e.memset(epsc[:, 0:1], RMS_EPS), writes=["epsc"])
        S.op("dve", lambda e: e.memset(epsc[:, 1:2], LNX_EPS), writes=["epsc"])
        S.op("dve", lambda e: e.memset(epsc[:, 2:3], 1e-24), writes=["epsc"])
        S.op("dve", lambda e: e.memset(epsc[:, 3:4], 1.0), writes=["epsc"])
        S.op("dve", lambda e: e.memset(epsc[:, 4:5], -0.5), writes=["epsc"])
        S.op("act", lambda e: e.activation(condb[:], cin[:], AF.Silu), reads=["cin"], writes=["condb"])

        with ExitStack() as st0:
            xin = [sb("xin%d" % i, [128, D_], F32, st0) for i in range(2)]
            for tb in range(16):
                xi = xin[tb % 2]
                S.op("sp", lambda e, xi=xi, tb=tb: e.dma_start(out=xi[:], in_=x_d[tb * 128:(tb + 1) * 128, :]),
                     writes=[("xin", tb % 2)], dma_key=("xin", tb % 2))
                for g in range(2):
                    bk = (tb * 2 + g) % 8
                    for cc in range(4):
                        c = g * 4 + cc
                        S.op("pe", lambda e, xi=xi, c=c, cc=cc, bk=bk: e.transpose(
                            bank(bk, 128, cc * 128), xi[:, c * 128:(c + 1) * 128], C("ident")),
                            reads=[("xin", tb % 2), "cst"], writes=[("ps", bk)])
                    S.op("dve" if g == 0 else "act",
                         (lambda e, g=g, tb=tb, bk=bk: e.tensor_copy(
                             xT[:, g * 4:(g + 1) * 4, tb * 128:(tb + 1) * 128],
                             bank(bk).rearrange("p (a b) -> p a b", a=4))) if g == 0 else
                         (lambda e, g=g, tb=tb, bk=bk: e.activation(
                             xT[:, g * 4:(g + 1) * 4, tb * 128:(tb + 1) * 128],
                             bank(bk).rearrange("p (a b) -> p a b", a=4), AF.Copy)),
                         reads=[("ps", bk)], writes=[("xT", tb // 4)])
        S.barrier()

        def rstd_from_ms(ps_ap, out_ap, eps_col, rkeys, wkeys, tmp_ap):
            S.op("act", lambda e: e.activation(tmp_ap, ps_ap, AF.Ln, bias=epsc[0:ps_ap.shape[0], eps_col:eps_col + 1]),
                 reads=list(rkeys) + ["epsc"], writes=[("tmp", id(tmp_ap.tensor))])
            S.op("act", lambda e: e.activation(out_ap, tmp_ap, AF.Exp, scale=-0.5),
                 reads=[("tmp", id(tmp_ap.tensor))], writes=list(wkeys))

        def MT(l):
            return modTs[l % 2]

        def LVA(l):
            return lvecs[l % 2]

        def layer_mod_gen(l, bk=7):
            mt, lv, par = MT(l), LVA(l), l % 2
            for blk in range(12):
                buf, key = wload([(wsrc(w_ada_d[l], blk * 512, 512, 8), 0)])
                wv = buf[:, 0:4096].rearrange("p (a b) -> p a b", a=8)
                for j in range(4):
                    for kc in range(8):
                        S.op("pe", lambda e, wv=wv, j=j, kc=kc: e.matmul(
                            bank(bk, 1, j), wv[:, kc, j * 128:(j + 1) * 128], condb[:, kc:kc + 1],
                            start=(kc == 0), stop=(kc == 7)),
                            reads=[key, "condb"], writes=[("ps", bk)])
                S.op("dve", lambda e, blk=blk: e.tensor_tensor(mt[:, blk * 4:(blk + 1) * 4], bank(bk, 4),
                                                               P(l, "b_ada", blk * 4, 4), ALU.add),
                     reads=[("ps", bk), "pv"], writes=[("modT", par)])
                yield
            S.op("dve", lambda e: e.scalar_tensor_tensor(lv[:, LV_A1:LV_A1 + 8], mt[:, 8:16], 1.0,
                                                         P(l, "norm1_g", 0, 8), ALU.add, ALU.mult),
                 reads=[("modT", par), "pv"], writes=[("lvec_a", par)])
            S.op("dve", lambda e: e.scalar_tensor_tensor(lv[:, LV_A2:LV_A2 + 8], mt[:, 32:40], 1.0,
                                                         P(l, "norm2_g", 0, 8), ALU.add, ALU.mult),
                 reads=[("modT", par), "pv"], writes=[("lvec_a", par)])
            yield

        bgs = {"g": None}

        def bg_step():
            g = bgs["g"]
            if g is None:
                return False
            try:
                next(g)
                return True
            except StopIteration:
                bgs["g"] = None
                return False

        def bg_drain():
            while bg_step():
                pass

        def adanorm(l, which, stk):
            acol = LV_A1 if which == 1 else LV_A2
            shc = 0 if which == 1 else 24
            sq = sb("n_sq", [128, 8, 512], BF16, stk)
            rstd = sb("n_rstd", [128, 512], F32, stk)
            ltmp = sb("n_ltmp", [128, 512], F32, stk)
            u = [sb("n_u%d" % i, [128, 512], F32, stk) for i in range(2)]
            for tt in range(NT):
                ts_ = slice(tt * 512, (tt + 1) * 512)
                S.op("act", lambda e, ts_=ts_: e.activation(sq[:], xT[:, :, ts_], AF.Square),
                     reads=[("xT", tt)], writes=["n_sq"])
                bk = tt % 2
                for kc in range(8):
                    S.op("pe", lambda e, kc=kc, bk=bk: e.matmul(bank(bk), Cb("onesD"), sq[:, kc, :],
                                                                start=(kc == 0), stop=(kc == 7)),
                         reads=["n_sq", "cstb"], writes=[("ps", bk)])
                rstd_from_ms(bank(bk), rstd[:], 0, [("ps", bk)], ["n_rstd"], ltmp[:])
                for c in range(8):
                    ub = u[c % 2]
                    S.op("dve", lambda e, c=c, ub=ub, ts_=ts_: e.scalar_tensor_tensor(
                        ub[:], xT[:, c, ts_], LVA(l)[:, acol + c:acol + c + 1], rstd[:], ALU.mult, ALU.mult),
                        reads=[("xT", tt), ("lvec_a", l % 2), "n_rstd"], writes=[("n_u", c % 2)])
                    S.op("act", lambda e, c=c, ub=ub, ts_=ts_: e.activation(
                        hT[:, c, ts_], ub[:], AF.Identity, bias=MT(l)[:, shc + c:shc + c + 1]),
                        reads=[("n_u", c % 2), ("modT", l % 2)], writes=[("hT", tt)])

        def proj_fm(l_w2d, c0, mcols, tt, bk, key_extra=()):
            wv, key, coff = l_w2d
            for kc in range(8):
                S.op("pe", lambda e, kc=kc: e.matmul(
                    bank(bk)[0:mcols, :], wv[:, kc, coff + c0:coff + c0 + mcols], hT[:, kc, tt * 512:(tt + 1) * 512],
                    start=(kc == 0), stop=(kc == 7)),
                    reads=[key, ("hT", tt)], writes=[("ps", bk)])

        def resid_update(l, gidx, stk_unused=None):
            buf, key = wload([(w_out_d[l][gidx * 256:(gidx + 1) * 256, :].rearrange("(kc p) n -> p kc n", p=128)[:, :, 0:1024], 0)])
            wv = buf[:, 0:2048].rearrange("p (a b) -> p a b", a=2)
            n = 0
            for oc in range(8):
                for tt in range(NT):
                    bk = n % 4
                    n += 1
                    for kc in range(2):
                        S.op("pe", lambda e, kc=kc, oc=oc, tt=tt, bk=bk: e.matmul(
                            bank(bk), wv[:, kc, oc * 128:(oc + 1) * 128], yT[:, kc, tt * 512:(tt + 1) * 512],
                            start=(kc == 0), stop=(kc == 1)),
                            reads=[key, ("yT", kc, tt)], writes=[("ps", bk)])
                    S.op("dve", lambda e, oc=oc, tt=tt, bk=bk: e.scalar_tensor_tensor(
                        xT[:, oc, tt * 512:(tt + 1) * 512], bank(bk), MT(l)[:, 16 + oc:17 + oc],
                        xT[:, oc, tt * 512:(tt + 1) * 512], ALU.mult, ALU.add),
                        reads=[("ps", bk), ("modT", l % 2), ("xT", tt)], writes=[("xT", tt)])

        def mixer_A(l):
            lam_init = 0.8 - 0.6 * math.exp(-0.3 * l)
            with ExitStack() as stk:
                qA = sb("qA", [128, 2, S_], BF16, stk)
                kA = [sb("kA%d" % c, [128, 2, S_], BF16, stk) for c in range(2)]
                VA = sb("VA", [128, 16, 4, 65], BF16, stk)
                sqb = sb("a_sq", [128, 512], BF16, stk)
                rst = sb("a_rst", [128, 512], F32, stk)
                ltm = sb("a_ltm", [128, 512], F32, stk)
                pT = [sb("a_pT%d" % i, [128, 512], BF16, stk) for i in range(3)]
                rec = sb("a_rec", [128, 2, 4], F32, stk)
                Oa = sb("a_O", [128, 4, 64], F32, stk)
                Ob = sb("a_Ob", [128, 4, 64], F32, stk)
                ssq = sb("a_ssq", [128, 4], F32, stk)
                junk = sb("a_junk", [128, 64], F32, stk)
                ytok = sb("a_ytok", [128, 16, 256], BF16, stk)
                lqk = sb("a_lqk", [128, 64], F32, stk)
                S.op("dve", lambda e: e.tensor_tensor(lqk[:], P(l, "lq", 0, 64), P(l, "lk", 0, 64), ALU.mult),
                     reads=["pv"], writes=["a_lqk"])
                S.op("dve", lambda e: e.tensor_reduce(lvec[:, LV_LAM:LV_LAM + 2],
                                                      lqk[:].rearrange("p (a b) -> p a b", a=2), AX.X, ALU.add),
                     reads=["a_lqk"], writes=["lv_lam"])
                S.op("act", lambda e: e.activation(lvec[:, LV_LAM:LV_LAM + 2], lvec[:, LV_LAM:LV_LAM + 2], AF.Exp),
                     reads=["lv_lam"], writes=["lv_lam"])
                S.op("dve", lambda e: e.scalar_tensor_tensor(lvec[:, LV_LAM + 2:LV_LAM + 3], lvec[:, LV_LAM + 1:LV_LAM + 2],
                                                             -lam_init, lvec[:, LV_LAM:LV_LAM + 1], ALU.add, ALU.subtract),
                     reads=["lv_lam"], writes=["lv_nlam"])
                S.op("dve", lambda e: e.memset(VA[:, :, :, 64:65], 1.0), writes=["VA1"])
                buf0, key0 = wload([(wsrc(w_in_d[l], 0, 512, 8), 0)])
                wv0 = buf0[:, 0:4096].rearrange("p (a b) -> p a b", a=8)
                buf1, key1 = wload([(wsrc(w_in_d[l], 512, 256, 8), 0)])
                wv1 = buf1[:, 0:2048].rearrange("p (a b) -> p a b", a=8)
                n = 0
                for tt in range(NT):
                    ts_ = slice(tt * 512, (tt + 1) * 512)
                    for ch in range(4):
                        bk = n % 2
                        bk2 = 2 + n % 2
                        n += 1
                        proj_fm((wv0, key0, 0), ch * 128, 128, tt, bk)
                        S.op("act", lambda e, bk=bk: e.activation(sqb[:], bank(bk), AF.Square),
                             reads=[("ps", bk)], writes=["a_sq"])
                        S.op("pe", lambda e, bk2=bk2: e.matmul(bank(bk2), Cb("blk32"), sqb[:], start=True, stop=True),
                             reads=["a_sq", "cstb"], writes=[("ps", bk2)])
                        rstd_from_ms(bank(bk2), rst[:], 0, [("ps", bk2)], ["a_rst"], ltm[:])
                        if ch < 2:
                            S.op("dve", lambda e, bk=bk, ch=ch, ts_=ts_: e.scalar_tensor_tensor(
                                qA[:, ch, ts_], bank(bk), P(l, "gqA"), rst[:], ALU.mult, ALU.mult),
                                reads=[("ps", bk), "a_rst", "pv"], writes=[("qA", tt)])
                        else:
                            for c in range(2):
                                S.op("dve", lambda e, bk=bk, ch=ch, ts_=ts_, c=c: e.scalar_tensor_tensor(
                                    kA[c][:, ch - 2, ts_], bank(bk), P(l, "gkA%d" % c), rst[:], ALU.mult, ALU.mult),
                                    reads=[("ps", bk), "a_rst", "pv"], writes=[("kA", tt)])
                for tb in range(16):
                    bk = 4 + tb % 2
                    for kc in range(8):
                        S.op("pe", lambda e, kc=kc, tb=tb, bk=bk: e.matmul(
                            bank(bk, 256), hT[:, kc, tb * 128:(tb + 1) * 128], wv1[:, kc, 0:256],
                            start=(kc == 0), stop=(kc == 7)),
                            reads=[key1, ("hT", tb // 4)], writes=[("ps", bk)])
                    S.op("act", lambda e, tb=tb, bk=bk: e.activation(
                        VA[:, tb, :, 0:64], bank(bk, 256).rearrange("p (a b) -> p a b", a=4), AF.Copy),
                        reads=[("ps", bk)], writes=[("VA", tb)])
                scale = 32.0 ** -0.5
                units = [(h, t, i, c) for h in range(4) for t in range(NT) for i in range(4 * t + 4) for c in range(2)]
                LA = 2
                obk = (6, 7)

                def a_score(n):
                    h, t, i, c = units[n]
                    ch, r0 = h // 2, (h % 2) * 64
                    col0 = max(0, i - 4 * t) * 128
                    sbk = n % 4
                    S.op("pe", lambda e: e.matmul(
                        bank(sbk)[:, col0:512], kA[c][r0:r0 + 64, ch, i * 128:(i + 1) * 128],
                        qA[r0:r0 + 64, ch, t * 512 + col0:(t + 1) * 512], start=True, stop=True),
                        reads=[("kA", i // 4), ("qA", t)], writes=[("ps", sbk)])

                def a_consume(n):
                    h, t, i, c = units[n]
                    jj0 = max(0, i - 4 * t)
                    col0 = jj0 * 128
                    sbk = n % 4
                    pt = pT[n % 3]
                    ptk = ("a_pT", n % 3)
                    S.op("act", lambda e: e.activation(pt[:, col0:512], bank(sbk)[:, col0:512], AF.Exp, scale=scale),
                         reads=[("ps", sbk)], writes=[ptk])
                    if i >= 4 * t:
                        S.op("dve", lambda e: e.tensor_tensor(
                            pt[:, col0:col0 + 128], pt[:, col0:col0 + 128], Cb("maskLE"), ALU.mult),
                            reads=[ptk, "cstb"], writes=[ptk])
                    for jj in range(jj0, 4):
                        S.op("pe", lambda e, jj=jj: e.matmul(
                            bank(obk[c])[:, jj * 65:(jj + 1) * 65], pt[:, jj * 128:(jj + 1) * 128],
                            VA[:, i, h, :], start=(i == 0 and jj == 0), stop=(i == 4 * t + jj), skip_group_check=True),
                            reads=[ptk, ("VA", i), "VA1"], writes=[("ps", obk[c])])
                    if i == 4 * t + 3 and c == 1:
                        a_epilogue(h, t)

                def a_epilogue(h, t):
                    o0 = bank(6, 260).rearrange("p (a b) -> p a b", a=4)
                    o1 = bank(7, 260).rearrange("p (a b) -> p a b", a=4)
                    S.op("dve", lambda e: e.reciprocal(rec[:, 0, :], o0[:, :, 64]), reads=[("ps", 6)], writes=["a_rec"])
                    S.op("dve", lambda e: e.reciprocal(rec[:, 1, :], o1[:, :, 64]), reads=[("ps", 7)], writes=["a_rec"])
                    S.op("dve", lambda e: e.tensor_scalar(rec[:, 1, :], rec[:, 1, :], lvec[:, LV_LAM + 2:LV_LAM + 3], None, ALU.mult),
                         reads=["a_rec", "lv_nlam"], writes=["a_rec"])
                    S.op("dve", lambda e: e.tensor_tensor(
                        Oa[:], o0[:, :, 0:64], rec[:, 0, :].unsqueeze(2).to_broadcast([128, 4, 64]), ALU.mult),
                        reads=[("ps", 6), "a_rec"], writes=["a_O"])
                    S.op("dve", lambda e: e.tensor_tensor(
                        Ob[:], o1[:, :, 0:64], rec[:, 1, :].unsqueeze(2).to_broadcast([128, 4, 64]), ALU.mult),
                        reads=[("ps", 7), "a_rec"], writes=["a_Ob"])
                    S.op("dve", lambda e: e.tensor_tensor(Oa[:], Oa[:], Ob[:], ALU.add), reads=["a_O", "a_Ob"], writes=["a_O"])
                    S.op("dve", lambda e: e.tensor_tensor(Ob[:], Oa[:], Oa[:], ALU.mult), reads=["a_O", "a_Ob"], writes=["a_Ob"])
                    S.op("dve", lambda e: e.tensor_reduce(ssq[:], Ob[:], AX.X, ALU.add), reads=["a_Ob"], writes=["a_ssq"])
                    S.op("act", lambda e: e.activation(ssq[:], ssq[:], AF.Ln, bias=epsc[:, 0:1], scale=1.0 / 64.0),
                         reads=["a_ssq", "epsc"], writes=["a_ssq"])
                    S.op("act", lambda e: e.activation(ssq[:], ssq[:], AF.Exp, scale=-0.5), reads=["a_ssq"], writes=["a_ssq"])
                    S.op("dve", lambda e: e.tensor_tensor(
                        ytok[:, 4 * t:4 * t + 4, h * 64:(h + 1) * 64], Oa[:],
                        ssq[:].unsqueeze(2).to_broadcast([128, 4, 64]), ALU.mult),
                        reads=["a_O", "a_ssq"], writes=[("a_ytok", t)])

                for n in range(len(units) + LA):
                    if n < len(units):
                        a_score(n)
                    if n - LA >= 0:
                        a_consume(n - LA)
                for tb in range(16):
                    bk = tb % 2
                    for kc in range(2):
                        S.op("pe", lambda e, tb=tb, kc=kc, bk=bk: e.transpose(
                            bankb(bk)[:, kc * 128:(kc + 1) * 128], ytok[:, tb, kc * 128:(kc + 1) * 128], Cb("ident")),
                            reads=[("a_ytok", tb // 4), "cstb"], writes=[("ps", bk)])
                        S.op("dve", lambda e, tb=tb, kc=kc, bk=bk: e.tensor_scalar(
                            yT[:, kc, tb * 128:(tb + 1) * 128], bankb(bk)[:, kc * 128:(kc + 1) * 128],
                            P(l, "a_out_g", kc), 1.0 - lam_init, ALU.mult, ALU.mult),
                            reads=[("ps", bk), "pv"], writes=[("yT", kc, tb // 4)])
                S.barrier()
                if l == nl - 1:
                    dump_now("yA", yT[:], [("yT", k_, t_) for k_ in range(2) for t_ in range(4)])
                resid_update(l, 0)
            S.barrier()


        def mixer_B(l):
            with ExitStack() as stk:
                qB = sb("qB", [128, 2, S_], BF16, stk)
                kB = sb("kB", [128, 2, S_], BF16, stk)
                VB = sb("VB", [128, 16, 256], BF16, stk)
                NSL = 2
                spb = [sb("b_sp%d" % i, [128, 512], F32, stk) for i in range(NSL)]
                lkb = [sb("b_lk%d" % i, [128, 512], F32, stk) for i in range(NSL)]
                pfx = [sb("b_pf%d" % i, [128, 512], F32, stk) for i in range(NSL)]
                wb = [sb("b_w%d" % i, [128, 512], BF16, stk) for i in range(NSL)]
                wT = [sb("b_wT%d" % i, [128, 512], BF16, stk) for i in range(NSL)]
                cT = [sb("b_cT%d" % i, [128, 2], F32, stk) for i in range(NSL)]
                ytok = [sb("b_ytok%d" % i, [128, 256], BF16, stk) for i in range(2)]
                ssq = sb("b_ssq", [128, 2], F32, stk)
                junk = [sb("b_junk%d" % i, [128, 64], F32, stk) for i in range(NSL)]
                buf0, key0 = wload([(wsrc(w_in_d[l], 768, 512, 8), 0)])
                wv0 = buf0[:, 0:4096].rearrange("p (a b) -> p a b", a=8)
                buf1, key1 = wload([(wsrc(w_in_d[l], 1280, 256, 8), 0)])
                wv1 = buf1[:, 0:2048].rearrange("p (a b) -> p a b", a=8)
                n = 0
                for tt in range(NT):
                    ts_ = slice(tt * 512, (tt + 1) * 512)
                    for ch in range(4):
                        bk = n % 2
                        n += 1
                        proj_fm((wv0, key0, 0), ch * 128, 128, tt, bk)
                        dst = qB if ch < 2 else kB
                        S.op("act" if ch % 2 == 0 else "dve",
                             (lambda e, bk=bk, dst=dst, ch=ch, ts_=ts_: e.activation(dst[:, ch % 2, ts_], bank(bk), AF.Copy)) if ch % 2 == 0 else
                             (lambda e, bk=bk, dst=dst, ch=ch, ts_=ts_: e.tensor_copy(dst[:, ch % 2, ts_], bank(bk))),
                             reads=[("ps", bk)], writes=[("qB" if ch < 2 else "kB", tt)])
                for tb in range(16):
                    bk = 2 + tb % 2
                    for kc in range(8):
                        S.op("pe", lambda e, kc=kc, tb=tb, bk=bk: e.matmul(
                            bank(bk, 256), hT[:, kc, tb * 128:(tb + 1) * 128], wv1[:, kc, 0:256],
                            start=(kc == 0), stop=(kc == 7)),
                            reads=[key1, ("hT", tb // 4)], writes=[("ps", bk)])
                    S.op("act", lambda e, tb=tb, bk=bk: e.activation(VB[:, tb, :], bank(bk, 256), AF.Copy),
                         reads=[("ps", bk)], writes=[("VB", tb)])
                sc = 64.0 ** -0.5

                def head_chain(j, h, sl, yt, ytk):
                    ch, r0 = h // 2, (h % 2) * 64
                    ntile = j // 4 + 1
                    zb, tbk, ob = sl, 2 + sl, 4 + sl
                    sp_, lk_, pf_, w_, wt_, ct_ = spb[sl], lkb[sl], pfx[sl], wb[sl], wT[sl], cT[sl]
                    K = lambda nm: (nm, sl)
                    step = 0
                    for kt in range(ntile - 1, -1, -1):
                        ncols = min(512, (j + 1) * 128 - kt * 512)
                        cs = slice(kt * 512, kt * 512 + ncols)
                        cn = slice(0, ncols)
                        diag = (kt == ntile - 1)
                        S.op("pe", lambda e, cs=cs, cn=cn: e.matmul(
                            bank(zb)[:, cn], qB[r0:r0 + 64, ch, j * 128:(j + 1) * 128], kB[r0:r0 + 64, ch, cs],
                            start=True, stop=True),
                            reads=[("qB", j // 4), ("kB", kt)], writes=[("ps", zb)])
                        yield
                        S.op("act", lambda e, cn=cn: e.activation(pf_[:, cn], bank(zb)[:, cn], AF.Exp, scale=-sc),
                             reads=[("ps", zb)], writes=[K("b_pf")])
                        yield
                        S.op("act", lambda e, cn=cn: e.activation(sp_[:, cn], pf_[:, cn], AF.Ln, bias=epsc[:, 3:4]),
                             reads=[K("b_pf"), "epsc"], writes=[K("b_sp")])
                        yield
                        S.op("dve", lambda e, cn=cn: e.scalar_tensor_tensor(
                            lk_[:, cn], bank(zb)[:, cn], -sc, sp_[:, cn], ALU.mult, ALU.subtract),
                            reads=[("ps", zb), K("b_sp")], writes=[K("b_lk")])
                        yield
                        if diag:
                            dl = slice((j % 4) * 128, (j % 4) * 128 + 128)
                            S.op("dve", lambda e, dl=dl: e.tensor_tensor(lk_[:, dl], lk_[:, dl], Cb("maskLT"), ALU.mult),
                                 reads=[K("b_lk"), "cstb"], writes=[K("b_lk")])
                            yield
                        S.op("dve", lambda e, cn=cn, ncols=ncols: e.tensor_tensor_scan(
                            pf_[:, cn], epsc[:, 3:4].to_broadcast([128, ncols]), lk_[:, cn], 0.0, ALU.mult, ALU.add),
                            reads=[K("b_lk"), K("b_pf"), "epsc"], writes=[K("b_pf")])
                        yield
                        cc, cp = step % 2, (step + 1) % 2
                        if step == 0:
                            S.op("dve", lambda e, ncols=ncols, cc=cc: e.tensor_copy(ct_[:, cc:cc + 1], pf_[:, ncols - 1:ncols]),
                                 reads=[K("b_pf")], writes=[K("b_cT")])
                        else:
                            S.op("dve", lambda e, ncols=ncols, cc=cc, cp=cp: e.tensor_tensor(
                                ct_[:, cc:cc + 1], ct_[:, cp:cp + 1], pf_[:, ncols - 1:ncols], ALU.add),
                                reads=[K("b_pf"), K("b_cT")], writes=[K("b_cT")])
                        yield
                        S.op("dve", lambda e, cn=cn: e.scalar_tensor_tensor(
                            lk_[:, cn], sp_[:, cn], -1.0, pf_[:, cn], ALU.mult, ALU.subtract),
                            reads=[K("b_sp"), K("b_pf"), K("b_lk")], writes=[K("b_lk")])
                        yield
                        S.op("act", lambda e, cn=cn, cc=cc: e.activation(w_[:, cn], lk_[:, cn], AF.Exp, bias=ct_[:, cc:cc + 1]),
                             reads=[K("b_lk"), K("b_cT")], writes=[K("b_w")])
                        yield
                        if diag:
                            dl = slice((j % 4) * 128, (j % 4) * 128 + 128)
                            S.op("dve", lambda e, dl=dl: e.tensor_tensor(w_[:, dl], w_[:, dl], Cb("maskLT"), ALU.mult),
                                 reads=[K("b_w"), "cstb"], writes=[K("b_w")])
                            yield
                        nb = ncols // 128
                        for ii in range(nb):
                            S.op("pe", lambda e, ii=ii: e.transpose(
                                bankb(tbk)[:, ii * 128:(ii + 1) * 128], w_[:, ii * 128:(ii + 1) * 128], Cb("ident")),
                                reads=[K("b_w"), "cstb"], writes=[("ps", tbk)])
                        yield
                        S.op("act", lambda e, cn=cn: e.activation(wt_[:, cn], bankb(tbk)[:, cn], AF.Copy),
                             reads=[("ps", tbk)], writes=[K("b_wT")])
                        yield
                        for ii in range(nb):
                            i = kt * 4 + ii
                            st_f = (kt == ntile - 1 and ii == 0)
                            sp_f = (kt == 0 and ii == nb - 1)
                            S.op("pe", lambda e, ii=ii, i=i, st_f=st_f, sp_f=sp_f: e.matmul(
                                bank(ob)[:, 0:64], wt_[:, ii * 128:(ii + 1) * 128], VB[:, i, h * 64:(h + 1) * 64],
                                start=st_f, stop=sp_f),
                                reads=[K("b_wT"), ("VB", i)], writes=[("ps", ob)])
                        yield
                        step += 1
                    S.op("dve", lambda e: e.memset(ssq[:, sl:sl + 1], 0.0), writes=[K("b_ssq")])
                    yield
                    S.op("act", lambda e: e.activation(junk[sl][:], bank(ob)[:, 0:64], AF.Square, accum_out=ssq[:, sl:sl + 1]),
                         reads=[("ps", ob), K("b_ssq")], writes=[K("b_ssq"), K("b_junk")])
                    yield
                    S.op("act", lambda e: e.activation(ssq[:, sl:sl + 1], ssq[:, sl:sl + 1], AF.Ln, bias=epsc[:, 0:1], scale=1.0 / 64.0),
                         reads=[K("b_ssq"), "epsc"], writes=[K("b_ssq")])
                    yield
                    S.op("act", lambda e: e.activation(ssq[:, sl:sl + 1], ssq[:, sl:sl + 1], AF.Exp, scale=-0.5),
                         reads=[K("b_ssq")], writes=[K("b_ssq")])
                    yield
                    S.op("dve", lambda e: e.tensor_scalar(yt[:, h * 64:(h + 1) * 64], bank(ob)[:, 0:64], ssq[:, sl:sl + 1], None, ALU.mult),
                         reads=[("ps", ob), K("b_ssq")], writes=[ytk])
                    yield

                for j in range(16):
                    yt = ytok[j % 2]
                    ytk = ("b_ytok", j % 2)
                    for hp in ((0, 1), (2, 3)):
                        alive = [head_chain(j, h, sl, yt, ytk) for sl, h in enumerate(hp)]
                        while alive:
                            for g in list(alive):
                                try:
                                    next(g)
                                except StopIteration:
                                    alive.remove(g)
                    for kc in range(2):
                        bk = 6
                        S.op("pe", lambda e, kc=kc, bk=bk, yt=yt: e.transpose(
                            bankb(bk)[:, kc * 128:(kc + 1) * 128], yt[:, kc * 128:(kc + 1) * 128], Cb("ident")),
                            reads=[ytk, "cstb"], writes=[("ps", bk)])
                        S.op("dve", lambda e, kc=kc, bk=bk, j=j: e.tensor_scalar(
                            yT[:, kc, j * 128:(j + 1) * 128], bankb(bk)[:, kc * 128:(kc + 1) * 128], P(l, "b_out_g", kc), None, ALU.mult),
                            reads=[("ps", bk), "pv"], writes=[("yT", kc, j // 4)])
                    bg_step()
                bg_drain()
                S.barrier()
                if l == nl - 1:
                    dump_now("yB", yT[:], [("yT", k_, t_) for k_ in range(2) for t_ in range(4)])
                resid_update(l, 1)
            S.barrier()

        def mixer_C(l):
            with ExitStack() as stk:
                xrp = sb("c_xrp", [128, 3 + S_], F32, stk)
                xc = sb("c_xc", [128, S_], F32, stk)
                t1 = sb("c_t1", [128, S_], F32, stk)
                t2 = sb("c_t2", [128, S_], F32, stk)
                t3 = sb("c_t3", [128, S_], F32, stk)
                xg = sb("c_xg", [128, S_], F32, stk)
                xcb = sb("c_xcb", [128, S_], BF16, stk)
                gw = gw_p
                ltm = sb("c_ltm", [128, 512], F32, stk)
                S.op("pool", lambda e: e.dma_start(out=gw[:], in_=gate_d[l].rearrange("g c k m -> k g c m")),
                     writes=["c_gw"], dma_key="c_gw")
                buf0, key0 = wload([(wsrc(w_in_d[l], 1536, 512, 8), 0)])
                wv0 = buf0[:, 0:4096].rearrange("p (a b) -> p a b", a=8)
                S.op("dve", lambda e: e.memset(xrp[:, 0:3], 0.0), writes=["c_xrp0"])
                for cc in range(2):
                    cl = lvec[:, LV_CL + cc:LV_CL + cc + 1]
                    S.op("act", lambda e, cl=cl, cc=cc: e.activation(cl, P(l, "c_lambda", cc), AF.Exp, scale=-1.0),
                         reads=["pv"], writes=[("lv_cl", cc)])
                    S.op("act", lambda e, cl=cl: e.activation(cl, cl, AF.Ln, bias=epsc[:, 3:4]),
                         reads=[("lv_cl", cc), "epsc"], writes=[("lv_cl", cc)])
                    S.op("dve", lambda e, cl=cl: e.tensor_scalar(cl, cl, -8.0, None, ALU.mult),
                         reads=[("lv_cl", cc)], writes=[("lv_cl", cc)])
                    for tt in range(NT):
                        ts_ = slice(tt * 512, (tt + 1) * 512)
                        proj_fm((wv0, key0, 0), cc * 128, 128, tt, 0)
                        S.op("act", lambda e, tt=tt: e.activation(xrp[:, 3 + tt * 512:3 + (tt + 1) * 512], bank(0), AF.Copy),
                             reads=[("ps", 0)], writes=["c_xrp"])
                        proj_fm((wv0, key0, 0), 256 + cc * 128, 128, tt, 1)
                        S.op("dve", lambda e, ts_=ts_: e.tensor_copy(xg[:, ts_], bank(1)),
                             reads=[("ps", 1)], writes=["c_xg"])
                    cw = lambda j: P(l, "c_conv_w", cc * 4 + j)
                    S.op("dve", lambda e, cc=cc: e.tensor_scalar(xc[:], xrp[:, 3:3 + S_], P(l, "c_conv_w", cc * 4 + 3),
                                                                 P(l, "c_conv_b", cc), ALU.mult, ALU.add),
                         reads=["c_xrp", "c_xrp0", "pv"], writes=["c_xc"])
                    for j in range(3):
                        S.op("dve", lambda e, j=j, cc=cc: e.scalar_tensor_tensor(
                            xc[:], xrp[:, j:j + S_], P(l, "c_conv_w", cc * 4 + j), xc[:], ALU.mult, ALU.add),
                            reads=["c_xrp", "c_xrp0", "c_xc", "pv"], writes=["c_xc"])
                    S.op("act", lambda e: e.activation(xcb[:], xc[:], AF.Copy), reads=["c_xc"], writes=["c_xcb"])
                    for tt in range(NT):
                        ts_ = slice(tt * 512, (tt + 1) * 512)
                        S.op("pe", lambda e, ts_=ts_, cc=cc: e.matmul(bank(2), gw[:, 0, cc, :], xcb[:, ts_], start=True, stop=True),
                             reads=["c_gw", "c_xcb"], writes=[("ps", 2)])
                        S.op("act", lambda e, ts_=ts_, cc=cc: e.activation(t1[:, ts_], bank(2), AF.Sigmoid, bias=P(l, "c_ga_b", cc)),
                             reads=[("ps", 2), "pv"], writes=["c_t1"])
                        S.op("pe", lambda e, ts_=ts_, cc=cc: e.matmul(bank(3), gw[:, 1, cc, :], xcb[:, ts_], start=True, stop=True),
                             reads=["c_gw", "c_xcb"], writes=[("ps", 3)])
                        S.op("act", lambda e, ts_=ts_, cc=cc: e.activation(t2[:, ts_], bank(3), AF.Sigmoid, bias=P(l, "c_gx_b", cc)),
                             reads=[("ps", 3), "pv"], writes=["c_t2"])
                    S.op("act", lambda e, cl=cl: e.activation(t1[:], t1[:], AF.Exp, scale=cl),
                         reads=["c_t1", ("lv_cl", cc)], writes=["c_t1"])
                    S.op("act", lambda e: e.activation(t3[:], t1[:], AF.Square), reads=["c_t1"], writes=["c_t3"])
                    S.op("dve", lambda e: e.tensor_scalar(t3[:], t3[:], -1.0, 1.0, ALU.mult, ALU.add),
                         reads=["c_t3"], writes=["c_t3"])
                    S.op("act", lambda e: e.activation(t3[:], t3[:], AF.Ln), reads=["c_t3"], writes=["c_t3"])
                    S.op("act", lambda e: e.activation(t3[:], t3[:], AF.Exp, scale=0.5), reads=["c_t3"], writes=["c_t3"])
                    S.op("dve", lambda e: e.memset(t3[:, 0:1], 1.0), reads=["c_t3"], writes=["c_t3"])
                    S.op("dve", lambda e: e.tensor_tensor(t2[:], t2[:], t3[:], ALU.mult), reads=["c_t2", "c_t3"], writes=["c_t2"])
                    S.op("dve", lambda e: e.tensor_tensor(t2[:], t2[:], xc[:], ALU.mult), reads=["c_t2", "c_xc"], writes=["c_t2"])
                    S.op("dve", lambda e: e.tensor_tensor_scan(t3[:], t1[:], t2[:], 0.0, ALU.mult, ALU.add),
                         reads=["c_t1", "c_t2", "c_t3"], writes=["c_t3"])
                    S.op("act", lambda e: e.activation(xc[:], xg[:], AF.Square), reads=["c_xg", "c_xc"], writes=["c_xc"])
                    S.op("dve", lambda e: e.tensor_scalar(xc[:], xc[:], 0.044715, 1.0, ALU.mult, ALU.add),
                         reads=["c_xc"], writes=["c_xc"])
                    S.op("dve", lambda e: e.tensor_tensor(xc[:], xc[:], xg[:], ALU.mult), reads=["c_xc", "c_xg"], writes=["c_xc"])
                    S.op("act", lambda e: e.activation(xc[:], xc[:], AF.Sigmoid, scale=1.5957691216057308),
                         reads=["c_xc"], writes=["c_xc"])
                    S.op("dve", lambda e: e.tensor_tensor(xc[:], xc[:], xg[:], ALU.mult), reads=["c_xc", "c_xg"], writes=["c_xc"])
                    S.op("dve", lambda e: e.tensor_tensor(t3[:], t3[:], xc[:], ALU.mult), reads=["c_t3", "c_xc"], writes=["c_t3"])
                    S.op("act", lambda e: e.activation(xcb[:], t3[:], AF.Square), reads=["c_t3", "c_xcb"], writes=["c_xcb"])
                    for tt in range(NT):
                        ts_ = slice(tt * 512, (tt + 1) * 512)
                        bk = 2 + tt % 2
                        S.op("pe", lambda e, ts_=ts_, bk=bk: e.matmul(bank(bk), Cb("blk64m"), xcb[:, ts_], start=True, stop=True),
                             reads=["c_xcb", "cstb"], writes=[("ps", bk)])
                        rstd_from_ms(bank(bk), t1[:, ts_], 0, [("ps", bk), "c_t1"], ["c_t1"], ltm[:])
                        S.op("dve", lambda e, ts_=ts_, cc=cc: e.scalar_tensor_tensor(
                            yT[:, cc, ts_], t3[:, ts_], P(l, "c_out_g", cc), t1[:, ts_], ALU.mult, ALU.mult),
                            reads=["c_t3", "c_t1", "pv"], writes=[("yT", cc, tt)])
                S.barrier()
                if l == nl - 1:
                    dump_now("yC", yT[:], [("yT", k_, t_) for k_ in range(2) for t_ in range(4)])
                resid_update(l, 2)
            S.barrier()


        def mixer_D(l):
            NCHK = 2
            DT = 64 * NCHK
            NDT = S_ // DT
            NS = NCHK * 4
            with ExitStack() as stk:
                T = lambda name, shape, dt=F32: sb("d_" + name, shape, dt, stk)
                pdp = T("pdp", [128, 7, DT + 1])
                pl = T("pl", [128, 7, DT])
                Pst = T("Pst", [64, 4, 64])
                Pb = T("Pb", [64, 4, 64], BF16)
                ptmp = T("ptmp", [64, 4, 64])
                wup, aup, gup, vup, vdn = dsw["wup"], dsw["aup"], dsw["gup"], dsw["vup"], vdn_p
                lrb = T("lrb", [128, DT], BF16)
                vdb = T("vdb", [128, DT], BF16)
                sqb = T("sqb", [128, DT], BF16)
                e2 = T("e2", [128, 2, DT])
                av = T("av", [128, 2, DT])
                gg = T("gg", [128, 2, DT])
                kk = T("kk", [128, 2, DT])
                k2 = T("k2", [128, 2, DT])
                bb = T("bb", [128, 2, DT])
                L = T("L", [128, 2, DT])
                ta = T("ta", [128, 2, DT])
                tb_ = T("tb", [128, 2, DT])
                tc = T("tc", [128, 2, DT])
                bon = T("bon", [128, 2, DT])
                yfm = T("yfm", [128, 2, DT])
                ltm = T("ltm", [128, DT])
                FM = T("FM", [128, 2, NCHK, 4, 64], BF16)
                LOW = T("LOW", [64, 4, NCHK * 256], BF16)
                gCl = T("gCl", [64, 4, NCHK])
                NBH = T("NBH", [128, 2, DT], BF16)
                KH = T("KH", [128, 2, DT], BF16)
                VBf = T("VBf", [128, 2, DT], BF16)
                gC = T("gC", [128, 2, NCHK])
                TM = T("TM", [64, 3, NCHK, 2, 128], BF16)
                SCa = T("SCa", [64, NCHK, 4, 256], BF16)
                XN = [T("XN%d" % i, [64, NS, 128], BF16) for i in range(2)]
                TT = [T("TT%d" % i, [64, NS, 64], BF16) for i in range(2)]
                Wb = T("Wb", [64, 256], BF16)
                Ub = T("Ub", [64, 256], BF16)
                ysq = T("ysq", [64, 256])
                st = T("st", [64, 16])
                ynb = T("ynb", [64, NCHK, 256], BF16)
                dmask = Cb("dmask", 64).rearrange("p (h c) -> p h c", h=4)
                id64 = Cb("id64x4", 64).rearrange("p (h c) -> p h c", h=4)
                ones64 = epsc[:, 3:4].to_broadcast([128, 64])

                def V(eng, fn, r, w):
                    return S.op(eng, fn, reads=r, writes=w)

                for wt, nm in ((wup, "d_wup"), (aup, "d_aup"), (gup, "d_gup"), (vup, "d_vup")):
                    if l == 0:
                        V("dve", lambda e, wt=wt: e.memset(wt[:], 0.0), [], [nm])
                V("dve", lambda e: e.memset(vdb[:], 0.0), [], ["d_vdb"])
                S.op("pool", lambda e: e.dma_start(out=wup[0:32, :], in_=dwup_d[l]), writes=["d_wup"], dma_key="d_w1")
                S.op("pool", lambda e: e.dma_start(out=aup[32:64, :], in_=daup_d[l]), writes=["d_aup"], dma_key="d_w2")
                S.op("pool", lambda e: e.dma_start(out=gup[64:128, :], in_=dgup_d[l]), writes=["d_gup"], dma_key="d_w3")
                if l > 0:
                    S.op("pool", lambda e: e.dma_start(out=vdn[:], in_=dvdn_d[l - 1].rearrange("(c p) r -> p c r", p=128)),
                         writes=["d_vdn"], dma_key="d_w4")
                    S.op("pool", lambda e: e.dma_start(out=vup[0:32, :], in_=dvup_d[l - 1]), writes=["d_vup"], dma_key="d_w5")
                buf4, key4 = wload([(wsrc(w_in_d[l], 2048, 512, 8), 0)])
                wv4 = buf4[:, 0:4096].rearrange("p (a b) -> p a b", a=8)
                buf5, key5 = wload([(wsrc(w_in_d[l], 2560, 384, 8), 0)])
                wv5 = buf5[:, 0:3072].rearrange("p (a b) -> p a b", a=8)
                V("dve", lambda e: e.memset(Pst[:], 0.0), [], ["d_Pst"])
                V("dve", lambda e: e.memset(Pb[:], 0.0), [], ["d_Pb"])
                V("dve", lambda e: e.tensor_scalar(lvec[:, LV_NW0:LV_NW0 + 2], P(l, "d_w0", 0, 2), -1.0, None, ALU.mult),
                  ["pv"], ["lv_nw0"])
                nb_ = [0]
                def nbank():
                    nb_[0] += 1
                    return 2 + nb_[0] % 2

                for dt in range(NDT):
                    t0 = dt * DT
                    tcs = slice(t0, t0 + DT)
                    hk = ("hT", t0 // 512)
                    if dt == 0:
                        V("dve", lambda e: e.memset(pdp[:, :, 0:1], 0.0), [], ["d_pdp0"])
                    else:
                        V("dve", lambda e: e.tensor_copy(pdp[:, :, 0:1], pdp[:, :, DT:DT + 1]), ["d_pdp"], ["d_pdp0"])
                    for c in range(7):
                        wv, key, coff = (wv4, key4, c * 128) if c < 4 else (wv5, key5, (c - 4) * 128)
                        bk = c % 2
                        for kc in range(8):
                            S.op("pe", lambda e, kc=kc, wv=wv, coff=coff, bk=bk, tcs=tcs: e.matmul(
                                bank(bk)[:, 0:DT], wv[:, kc, coff:coff + 128], hT[:, kc, tcs],
                                start=(kc == 0), stop=(kc == 7)),
                                reads=[key, hk], writes=[("ps", bk)])
                        V("act", lambda e, c=c, bk=bk: e.activation(pdp[:, c, 1:DT + 1], bank(bk)[:, 0:DT], AF.Copy),
                          [("ps", bk), "d_pdp0"], ["d_pdp"])
                    if D_STOP < 2:
                        continue
                    mu_b = P(l, "d_mu", 0, 7).unsqueeze(2).to_broadcast([128, 7, DT])
                    V("dve", lambda e: e.tensor_tensor(pl[:], pdp[:, :, 0:DT], pdp[:, :, 1:DT + 1], ALU.subtract),
                      ["d_pdp", "d_pdp0"], ["d_pl"])
                    V("dve", lambda e, mu_b=mu_b: e.tensor_tensor(pl[:], pl[:], mu_b, ALU.mult), ["d_pl", "pv"], ["d_pl"])
                    V("dve", lambda e: e.tensor_tensor(pl[:], pl[:], pdp[:, :, 1:DT + 1], ALU.add), ["d_pl", "d_pdp"], ["d_pl"])
                    if D_STOP < 3:
                        continue
                    V("act", lambda e: e.activation(lrb[0:32, :], pl[0:32, 6, :], AF.Tanh), ["d_pl"], ["d_lrb"])
                    V("act", lambda e: e.activation(lrb[32:64, :], pl[32:64, 6, :], AF.Copy), ["d_pl"], ["d_lrb"])
                    V("act", lambda e: e.activation(lrb[64:128, :], pl[64:128, 6, :], AF.Sigmoid), ["d_pl"], ["d_lrb"])
                    for oc in range(2):
                        ocs = slice(oc * 128, (oc + 1) * 128)
                        bk = nbank()
                        V("pe", lambda e, bk=bk, ocs=ocs: e.matmul(bank(bk)[:, 0:DT], wup[:, ocs], lrb[:, :], start=True, stop=True),
                          ["d_wup", "d_lrb"], [("ps", bk)])
                        V("act", lambda e, bk=bk, oc=oc: e.activation(ta[:, oc, :], bank(bk)[:, 0:DT], AF.Exp, scale=-1.0,
                                                                      bias=lvec[:, LV_NW0 + oc:LV_NW0 + oc + 1]),
                          [("ps", bk), "lv_nw0"], ["d_ta"])
                        V("act", lambda e, oc=oc: e.activation(ta[:, oc, :], ta[:, oc, :], AF.Ln, bias=epsc[:, 3:4]),
                          ["d_ta", "epsc"], ["d_ta"])
                        V("act", lambda e, oc=oc: e.activation(e2[:, oc, :], ta[:, oc, :], AF.Exp, scale=-1.0, bias=epsc[:, 4:5]),
                          ["d_ta", "epsc"], ["d_e2"])
                        bk = nbank()
                        V("pe", lambda e, bk=bk, ocs=ocs: e.matmul(bank(bk)[:, 0:DT], aup[:, ocs], lrb[:, :], start=True, stop=True),
                          ["d_aup", "d_lrb"], [("ps", bk)])
                        V("act", lambda e, bk=bk, oc=oc: e.activation(av[:, oc, :], bank(bk)[:, 0:DT], AF.Sigmoid, bias=P(l, "d_a0", oc)),
                          [("ps", bk), "pv"], ["d_av"])
                        bk = nbank()
                        V("pe", lambda e, bk=bk, ocs=ocs: e.matmul(bank(bk)[:, 0:DT], gup[:, ocs], lrb[:, :], start=True, stop=True),
                          ["d_gup", "d_lrb"], [("ps", bk)])
                        V("act", lambda e, bk=bk, oc=oc: e.activation(gg[:, oc, :], bank(bk)[:, 0:DT], AF.Copy),
                          [("ps", bk)], ["d_gg"])
                    if D_STOP < 5:
                        continue
                    if l > 0:
                        V("act", lambda e: e.activation(VBf[:], pl[:, 4:6, :], AF.Copy), ["d_pl"], ["d_VBf"])
                        for kc in range(2):
                            V("pe", lambda e, kc=kc: e.matmul(bank(4)[0:32, 0:DT], vdn[:, kc, :], VBf[:, kc, :],
                                                              start=(kc == 0), stop=(kc == 1)),
                              ["d_vdn", "d_VBf"], [("ps", 4)])
                        V("dve", lambda e: e.tensor_copy(vdb[0:32, :], bank(4)[0:32, 0:DT]), [("ps", 4)], ["d_vdb"])
                        for oc in range(2):
                            ocs = slice(oc * 128, (oc + 1) * 128)
                            bk = nbank()
                            V("pe", lambda e, bk=bk, ocs=ocs: e.matmul(bank(bk)[:, 0:DT], vup[:, ocs], vdb[:, :], start=True, stop=True),
                              ["d_vup", "d_vdb"], [("ps", bk)])
                            V("act", lambda e, bk=bk, oc=oc: e.activation(tc[:, oc, :], bank(bk)[:, 0:DT], AF.Sigmoid, bias=P(l, "d_v0", oc)),
                              [("ps", bk), "pv"], ["d_tc"])
                            V("dve", lambda e, oc=oc, tcs=tcs: e.tensor_tensor(ta[:, oc, :], vfirst[:, oc, tcs], pl[:, 4 + oc, :], ALU.subtract),
                              ["vfirst", "d_pl"], ["d_ta"])
                            V("dve", lambda e, oc=oc: e.tensor_tensor(ta[:, oc, :], ta[:, oc, :], tc[:, oc, :], ALU.mult),
                              ["d_ta", "d_tc"], ["d_ta"])
                            V("dve", lambda e, oc=oc: e.tensor_tensor(pl[:, 4 + oc, :], pl[:, 4 + oc, :], ta[:, oc, :], ALU.add),
                              ["d_pl", "d_ta"], ["d_pl"])
                    else:
                        V("act", lambda e, tcs=tcs: e.activation(vfirst[:, :, tcs], pl[:, 4:6, :], AF.Copy), ["d_pl"], ["vfirst"])
                    if D_STOP < 6:
                        continue
                    kkk_b = P(l, "d_k_k", 0, 2).unsqueeze(2).to_broadcast([128, 2, DT])
                    kka_b = P(l, "d_k_a", 0, 2).unsqueeze(2).to_broadcast([128, 2, DT])
                    rk_b = P(l, "d_r_k", 0, 2).unsqueeze(2).to_broadcast([128, 2, DT])
                    rr, kraw, vv = pl[:, 0:2, :], pl[:, 2:4, :], pl[:, 4:6, :]
                    V("dve", lambda e: e.tensor_tensor(ta[:], kraw, kkk_b, ALU.mult), ["d_pl", "pv"], ["d_ta"])
                    for oc in range(2):
                        bk = nbank()
                        V("act", lambda e, oc=oc: e.activation(sqb[:], ta[:, oc, :], AF.Square), ["d_ta"], ["d_sqb"])
                        V("pe", lambda e, bk=bk: e.matmul(bank(bk)[:, 0:DT], Cb("blk64s"), sqb[:], start=True, stop=True),
                          ["d_sqb", "cstb"], [("ps", bk)])
                        rstd_from_ms(bank(bk)[:, 0:DT], tb_[:, oc, :], 2, [("ps", bk), "d_tb"], ["d_tb"], ltm[:])
                    V("dve", lambda e: e.tensor_tensor(kk[:], ta[:], tb_[:], ALU.mult), ["d_ta", "d_tb"], ["d_kk"])
                    V("dve", lambda e: e.scalar_tensor_tensor(ta[:], av[:], -1.0, kka_b, ALU.add, ALU.mult),
                      ["d_av", "pv", "d_ta"], ["d_ta"])
                    V("dve", lambda e: e.scalar_tensor_tensor(k2[:], ta[:], 1.0, kraw, ALU.add, ALU.mult),
                      ["d_ta", "d_pl"], ["d_k2"])
                    V("dve", lambda e: e.tensor_tensor(bb[:], kk[:], av[:], ALU.mult), ["d_kk", "d_av"], ["d_bb"])
                    for oc in range(2):
                        for ch in range(NCHK):
                            cs = slice(ch * 64, (ch + 1) * 64)
                            V("dve", lambda e, oc=oc, cs=cs: e.tensor_tensor_scan(
                                L[:, oc, cs], ones64, e2[:, oc, cs], 0.0, ALU.mult, ALU.subtract),
                              ["d_e2", "epsc"], ["d_L"])
                    V("dve", lambda e: e.tensor_tensor(ta[:], L[:], e2[:], ALU.add), ["d_L", "d_e2", "d_ta"], ["d_ta"])
                    V("act", lambda e: e.activation(ta[:], ta[:], AF.Exp), ["d_ta"], ["d_ta"])
                    c4 = lambda ap: ap.rearrange("p o (c t) -> p o c t", t=64)
                    V("dve", lambda e: e.tensor_tensor(FM[:, :, :, 0, :], c4(kk[:]), c4(ta[:]), ALU.mult),
                      ["d_kk", "d_ta"], ["d_FM"])
                    V("act", lambda e: e.activation(tb_[:], L[:], AF.Exp), ["d_L", "d_tb"], ["d_tb"])
                    V("dve", lambda e: e.tensor_tensor(FM[:, :, :, 1, :], c4(pl[:, 0:2, :]), c4(tb_[:]), ALU.mult),
                      ["d_pl", "d_tb"], ["d_FM"])
                    for oc in range(2):
                        V("dve", lambda e, oc=oc: e.tensor_copy(
                            gC[:, oc, :], tb_[:, oc, :].rearrange("p (c t) -> p c t", t=64)[:, :, 63]),
                          ["d_tb"], ["d_gC"])
                    V("act", lambda e: e.activation(tc[:], L[:], AF.Exp, scale=-1.0), ["d_L", "d_tc"], ["d_tc"])
                    V("dve", lambda e: e.tensor_tensor(FM[:, :, :, 2, :], c4(bb[:]), c4(tc[:]), ALU.mult), ["d_bb", "d_tc"], ["d_FM"])
                    V("dve", lambda e: e.tensor_tensor(FM[:, :, :, 3, :], c4(k2[:]), c4(tc[:]), ALU.mult), ["d_k2", "d_tc"], ["d_FM"])
                    for oc in range(2):
                        for ch in range(NCHK):
                            cs = slice(ch * 64, (ch + 1) * 64)
                            V("act", lambda e, oc=oc, cs=cs, ch=ch: e.activation(
                                tc[:, oc, cs], L[:, oc, cs], AF.Exp, scale=-1.0, bias=L[:, oc, ch * 64 + 63:ch * 64 + 64]),
                              ["d_L", "d_tc", "d_FM"], ["d_tc"])
                    V("dve", lambda e: e.scalar_tensor_tensor(NBH[:], bb[:], -1.0, tc[:], ALU.mult, ALU.mult),
                      ["d_bb", "d_tc"], ["d_NBH"])
                    V("dve", lambda e: e.tensor_tensor(KH[:], k2[:], tc[:], ALU.mult), ["d_k2", "d_tc"], ["d_KH"])
                    V("act", lambda e: e.activation(VBf[:], vv, AF.Copy), ["d_pl"], ["d_VBf"])
                    V("dve", lambda e: e.tensor_tensor(ta[:], rr, rk_b, ALU.mult), ["d_pl", "pv", "d_ta"], ["d_ta"])
                    V("dve", lambda e: e.tensor_tensor(ta[:], ta[:], k2[:], ALU.mult), ["d_ta", "d_k2"], ["d_ta"])
                    for oc in range(2):
                        bk = nbank()
                        V("act", lambda e, oc=oc: e.activation(sqb[:], ta[:, oc, :], AF.Copy), ["d_ta"], ["d_sqb"])
                        V("pe", lambda e, bk=bk: e.matmul(bank(bk)[:, 0:DT], Cb("blk64s"), sqb[:], start=True, stop=True),
                          ["d_sqb", "cstb"], [("ps", bk)])
                        V("dve", lambda e, bk=bk, oc=oc: e.tensor_tensor(bon[:, oc, :], bank(bk)[:, 0:DT], pl[:, 4 + oc, :], ALU.mult),
                          [("ps", bk), "d_pl"], ["d_bon"])
                    if D_STOP < 7:
                        continue
                    for xi, Xt in enumerate((VBf, NBH, KH)):
                        bk = 5 + xi
                        for ch in range(NCHK):
                            for oc in range(2):
                                slot = ch * 2 + oc
                                V("pe", lambda e, Xt=Xt, ch=ch, oc=oc, slot=slot, bk=bk: e.matmul(
                                    bank(bk)[0:64, slot * 128:(slot + 1) * 128], Xt[:, oc, ch * 64:(ch + 1) * 64], Cb("ident"),
                                    start=True, stop=True),
                                  ["d_VBf", "d_NBH", "d_KH", "cstb"], [("ps", bk)])
                        V("act" if xi != 1 else "dve",
                          (lambda e, xi=xi, bk=bk: e.activation(TM[:, xi].rearrange("p c o f -> p (c o f)"), bank(bk)[0:64, 0:NCHK * 256], AF.Copy))
                          if xi != 1 else
                          (lambda e, xi=xi, bk=bk: e.tensor_copy(TM[:, xi].rearrange("p c o f -> p (c o f)"), bank(bk)[0:64, 0:NCHK * 256])),
                          [("ps", bk)], ["d_TM"])
                    if D_STOP < 8:
                        continue
                    for oc in range(2):
                        for hh in range(2):
                            hd = oc * 2 + hh
                            bk = 3 + hd % 2
                            V("pe", lambda e, oc=oc, hh=hh, bk=bk: e.matmul(
                                bank(bk)[0:64, 0:NCHK * 256], Cb("ident")[:, hh * 64:(hh + 1) * 64],
                                FM[:, oc].rearrange("p c q t -> p (c q t)"), start=True, stop=True),
                              ["d_FM", "cstb"], [("ps", bk)])
                            if hd % 2 == 0:
                                V("act", lambda e, hd=hd, bk=bk: e.activation(LOW[:, hd, :], bank(bk)[0:64, 0:NCHK * 256], AF.Copy),
                                  [("ps", bk)], ["d_LOW"])
                            else:
                                V("dve", lambda e, hd=hd, bk=bk: e.tensor_copy(LOW[:, hd, :], bank(bk)[0:64, 0:NCHK * 256]),
                                  [("ps", bk)], ["d_LOW"])
                    for hh in range(2):
                        V("pe", lambda e, hh=hh: e.matmul(
                            bank(5)[0:64, hh * 2 * NCHK:(hh + 1) * 2 * NCHK], C("ident")[:, hh * 64:(hh + 1) * 64],
                            gC[:].rearrange("p o c -> p (o c)"), start=True, stop=True),
                          ["d_gC", "cst"], [("ps", 5)])
                    V("dve", lambda e: e.tensor_copy(
                        gCl[:].rearrange("p (o h) c -> p h o c", o=2),
                        bank(5)[0:64, 0:4 * NCHK].rearrange("p (h o c) -> p h o c", h=2, o=2)),
                      [("ps", 5)], ["d_gCl"])
                    for ch in range(NCHK):
                        cb = ch * 256
                        for hd in range(4):
                            sa = hd // 2
                            co = (hd % 2) * 256
                            V("pe", lambda e, sa=sa, co=co, hd=hd, cb=cb: e.matmul(
                                bank(sa)[0:64, co:co + 128], LOW[:, hd, cb + 128:cb + 192], LOW[:, hd, cb:cb + 128],
                                start=True, stop=True),
                              ["d_LOW"], [("ps", sa)])
                            V("pe", lambda e, sa=sa, co=co, hd=hd, cb=cb: e.matmul(
                                bank(sa)[0:64, co + 128:co + 256], LOW[:, hd, cb + 192:cb + 256], LOW[:, hd, cb:cb + 128],
                                start=True, stop=True),
                              ["d_LOW"], [("ps", sa)])
                            V("pe", lambda e, hd=hd, cb=cb: e.matmul(
                                bank(2)[0:64, hd * 64:(hd + 1) * 64], LOW[:, hd, cb:cb + 64], LOW[:, hd, cb + 128:cb + 192],
                                start=True, stop=True),
                              ["d_LOW"], [("ps", 2)])
                        for sa in range(2):
                            V("dve", lambda e, sa=sa, ch=ch: e.tensor_tensor(
                                SCa[:, ch, 2 * sa:2 * sa + 2, :], bank(sa)[0:64, :].rearrange("p (h c) -> p h c", h=2),
                                dmask[:, 2 * sa:2 * sa + 2, 0:256], ALU.mult),
                              [("ps", sa), "cstb"], ["d_SCa"])
                        V("dve", lambda e, ch=ch: e.tensor_tensor(
                            XN[0][:, ch * 4:(ch + 1) * 4, 64:128], bank(2)[0:64, 0:256].rearrange("p (h c) -> p h c", h=4),
                            dmask[:, :, 256:320], ALU.mult),
                          [("ps", 2), "cstb"], [("d_XN", 0)])
                        V("act", lambda e, ch=ch: e.activation(XN[0][:, ch * 4:(ch + 1) * 4, 0:64], SCa[:, ch, :, 0:64], AF.Copy),
                          ["d_SCa"], [("d_XN", 0)])
                        V("dve", lambda e, ch=ch: e.tensor_tensor(TT[0][:, ch * 4:(ch + 1) * 4, :], SCa[:, ch, :, 0:64], id64, ALU.add),
                          ["d_SCa", "cstb"], [("d_TT", 0)])
                    if D_STOP < 9:
                        continue
                    cur = 0
                    for lev in range(5):
                        nxt = 1 - cur
                        for s_ in range(NS):
                            bkx = 3 + s_ // 4
                            c0 = (s_ % 4) * 128
                            V("pe", lambda e, s_=s_, bkx=bkx, c0=c0, cur=cur: e.matmul(
                                bank(bkx)[0:64, c0:c0 + 64], XN[cur][:, s_, 64:128], XN[cur][:, s_, 0:64], start=True, stop=True),
                              [("d_XN", cur)], [("ps", bkx)])
                            V("pe", lambda e, s_=s_, bkx=bkx, c0=c0, cur=cur: e.matmul(
                                bank(bkx)[0:64, c0 + 64:c0 + 128], XN[cur][:, s_, 0:64], XN[cur][:, s_, 64:128], start=True, stop=True),
                              [("d_XN", cur)], [("ps", bkx)])
                        for g_ in range(NS // 4):
                            V("act" if g_ % 2 == 0 else "dve",
                              (lambda e, g_=g_, nxt=nxt: e.activation(
                                  XN[nxt][:, g_ * 4:(g_ + 1) * 4, :], bank(3 + g_)[0:64, :].rearrange("p (s c) -> p s c", s=4), AF.Copy))
                              if g_ % 2 == 0 else
                              (lambda e, g_=g_, nxt=nxt: e.tensor_copy(
                                  XN[nxt][:, g_ * 4:(g_ + 1) * 4, :], bank(3 + g_)[0:64, :].rearrange("p (s c) -> p s c", s=4))),
                              [("ps", 3 + g_)], [("d_XN", nxt)])
                        for s_ in range(NS):
                            V("pe", lambda e, s_=s_, cur=cur, nxt=nxt: e.matmul(
                                bank(5)[0:64, s_ * 64:(s_ + 1) * 64], XN[nxt][:, s_, 64:128], TT[cur][:, s_, :], start=True, stop=True),
                              [("d_XN", nxt), ("d_TT", cur)], [("ps", 5)])
                        V("dve", lambda e, cur=cur, nxt=nxt: e.tensor_tensor(
                            TT[nxt][:], bank(5)[0:64, 0:NS * 64].rearrange("p (s c) -> p s c", s=NS), TT[cur][:], ALU.add),
                          [("ps", 5), ("d_TT", cur)], [("d_TT", nxt)])
                        cur = nxt
                    TTf = TT[cur]
                    ttk = ("d_TT", cur)
                    if D_STOP < 10:
                        continue
                    for ch in range(NCHK):
                        cb = ch * 256
                        Wp, Up, Yp, Pp = bank(6)[0:64, 0:256], bank(7)[0:64, 0:256], bank(0)[0:64, 0:256], bank(1)[0:64, 0:256]
                        for hd in range(4):
                            oc, hh = hd // 2, hd % 2
                            hs = slice(hd * 64, (hd + 1) * 64)
                            vtok = TM[:, 0, ch, oc, hh * 64:(hh + 1) * 64]
                            V("pe", lambda e, hs=hs, hd=hd, cb=cb: e.matmul(
                                Wp[:, hs], LOW[:, hd, cb:cb + 64], Pb[:, hd, :], start=True, stop=False),
                              ["d_LOW", "d_Pb"], [("ps", 6)])
                            V("pe", lambda e, hs=hs, ch=ch, hd=hd, vtok=vtok: e.matmul(
                                Wp[:, hs], SCa[:, ch, hd, 128:192], vtok, start=False, stop=True),
                              ["d_SCa", "d_TM"], [("ps", 6)])
                        V("dve", lambda e: e.tensor_copy(Wb[:], bank(6)[0:64, 0:256]), [("ps", 6)], ["d_Wb"])
                        for hd in range(4):
                            hs = slice(hd * 64, (hd + 1) * 64)
                            V("pe", lambda e, hs=hs, hd=hd, ch=ch: e.matmul(
                                Up[:, hs], TTf[:, ch * 4 + hd, :], Wb[:, hs], start=True, stop=True),
                              [ttk, "d_Wb"], [("ps", 7)])
                        V("act", lambda e: e.activation(Ub[:], bank(7)[0:64, 0:256], AF.Copy), [("ps", 7)], ["d_Ub"])
                        for hd in range(4):
                            oc, hh = hd // 2, hd % 2
                            hs = slice(hd * 64, (hd + 1) * 64)
                            vtok = TM[:, 0, ch, oc, hh * 64:(hh + 1) * 64]
                            V("pe", lambda e, hs=hs, hd=hd, cb=cb: e.matmul(
                                Yp[:, hs], LOW[:, hd, cb + 64:cb + 128], Pb[:, hd, :], start=True, stop=False),
                              ["d_LOW", "d_Pb"], [("ps", 0)])
                            V("pe", lambda e, hs=hs, ch=ch, hd=hd, vtok=vtok: e.matmul(
                                Yp[:, hs], SCa[:, ch, hd, 192:256], vtok, start=False, stop=False),
                              ["d_SCa", "d_TM"], [("ps", 0)])
                            V("pe", lambda e, hs=hs, ch=ch, hd=hd: e.matmul(
                                Yp[:, hs], SCa[:, ch, hd, 64:128], Ub[:, hs], start=False, stop=True),
                              ["d_SCa", "d_Ub"], [("ps", 0)])
                        for hd in range(4):
                            oc, hh = hd // 2, hd % 2
                            hs = slice(hd * 64, (hd + 1) * 64)
                            vtok = TM[:, 0, ch, oc, hh * 64:(hh + 1) * 64]
                            V("pe", lambda e, hs=hs, ch=ch, oc=oc, hh=hh: e.matmul(
                                Pp[:, hs], TM[:, 1, ch, oc, hh * 64:(hh + 1) * 64], Ub[:, hs], start=True, stop=False),
                              ["d_TM", "d_Ub"], [("ps", 1)])
                            V("pe", lambda e, hs=hs, ch=ch, oc=oc, hh=hh, vtok=vtok: e.matmul(
                                Pp[:, hs], TM[:, 2, ch, oc, hh * 64:(hh + 1) * 64], vtok, start=False, stop=True),
                              ["d_TM"], [("ps", 1)])
                        V("dve", lambda e, ch=ch: e.tensor_tensor(
                            ptmp[:], Pst[:], gCl[:, :, ch].unsqueeze(2).to_broadcast([64, 4, 64]), ALU.mult),
                          ["d_Pst", "d_gCl"], ["d_ptmp"])
                        V("dve", lambda e: e.tensor_tensor(
                            Pst[:], ptmp[:], bank(1)[0:64, 0:256].rearrange("p (h c) -> p h c", h=4), ALU.add),
                          ["d_ptmp", ("ps", 1)], ["d_Pst"])
                        V("act", lambda e: e.activation(Pb[:], Pst[:], AF.Copy), ["d_Pst"], ["d_Pb"])
                        yv = bank(0)[0:64, 0:256].rearrange("p (h c) -> p h c", h=4)
                        V("dve", lambda e, yv=yv: e.tensor_reduce(st[:, 0:4], yv, AX.X, ALU.add), [("ps", 0)], ["d_st"])
                        V("act", lambda e: e.activation(ysq[:], bank(0)[0:64, 0:256], AF.Square), [("ps", 0)], ["d_ysq"])
                        V("dve", lambda e: e.tensor_reduce(st[:, 4:8], ysq[:].rearrange("p (h c) -> p h c", h=4), AX.X, ALU.add),
                          ["d_ysq", "d_st"], ["d_st"])
                        V("dve", lambda e: e.tensor_scalar(st[:, 8:12], st[:, 0:4], 1.0 / 64.0, None, ALU.mult), ["d_st"], ["d_st"])
                        V("dve", lambda e: e.tensor_tensor(st[:, 12:16], st[:, 8:12], st[:, 8:12], ALU.mult), ["d_st"], ["d_st"])
                        V("dve", lambda e: e.scalar_tensor_tensor(st[:, 4:8], st[:, 4:8], 1.0 / 64.0, st[:, 12:16], ALU.mult, ALU.subtract),
                          ["d_st"], ["d_st"])
                        V("act", lambda e: e.activation(st[:, 4:8], st[:, 4:8], AF.Ln, bias=epsc[0:64, 1:2]), ["d_st", "epsc"], ["d_st"])
                        V("act", lambda e: e.activation(st[:, 4:8], st[:, 4:8], AF.Exp, scale=-0.5), ["d_st"], ["d_st"])
                        V("dve", lambda e, yv=yv: e.tensor_tensor(
                            ysq[:].rearrange("p (h c) -> p h c", h=4), yv, st[:, 8:12].unsqueeze(2).to_broadcast([64, 4, 64]), ALU.subtract),
                          [("ps", 0), "d_st", "d_ysq"], ["d_ysq"])
                        V("dve", lambda e, ch=ch: e.tensor_tensor(
                            ynb[:, ch, :].rearrange("p (h c) -> p h c", h=4), ysq[:].rearrange("p (h c) -> p h c", h=4),
                            st[:, 4:8].unsqueeze(2).to_broadcast([64, 4, 64]), ALU.mult),
                          ["d_ysq", "d_st"], ["d_ynb"])
                    if D_STOP < 11:
                        continue
                    for oc in range(2):
                        for ch in range(NCHK):
                            V("pe", lambda e, oc=oc, ch=ch: e.matmul(
                                bank(2)[:, oc * DT + ch * 64:oc * DT + (ch + 1) * 64], ynb[:, ch, oc * 128:(oc + 1) * 128],
                                Cb("ident", 64)[:, 0:64], start=True, stop=True),
                              ["d_ynb", "cstb"], [("ps", 2)])
                        V("dve", lambda e, oc=oc: e.tensor_scalar(yfm[:, oc, :], bank(2)[:, oc * DT:(oc + 1) * DT],
                                                                  P(l, "d_lnx_w", oc), P(l, "d_lnx_b", oc), ALU.mult, ALU.add),
                          [("ps", 2), "pv"], ["d_yfm"])
                    V("dve", lambda e: e.tensor_tensor(yfm[:], yfm[:], bon[:], ALU.add), ["d_yfm", "d_bon"], ["d_yfm"])
                    V("dve", lambda e, tcs=tcs: e.tensor_tensor(yT[:, :, tcs], yfm[:], gg[:], ALU.mult),
                      ["d_yfm", "d_gg"], [("yT", 0, t0 // 512), ("yT", 1, t0 // 512)])
                S.barrier()
                if l == nl - 1:
                    dump_now("yD", yT[:], [("yT", k_, t_) for k_ in range(2) for t_ in range(4)])
                resid_update(l, 3)
            S.barrier()


        def ffn(l):
            with ExitStack() as stk:
                adanorm(l, 2, stk)
            S.barrier()
            with ExitStack() as stk:
                actT = sb("f_act", [128, 22, 512], BF16, stk)
                yg = [sb("f_yg%d" % i, [128, 512], F32, stk) for i in range(2)]
                yv = [sb("f_yv%d" % i, [128, 512], F32, stk) for i in range(2)]
                sg = [sb("f_sg%d" % i, [128, 512], F32, stk) for i in range(2)]
                for hb in (0, 2, 4, 6):
                    S.op("dve", lambda e, hb=hb: e.memset(bank(hb)[:, 510:512], 0.0), writes=[("ps", hb)])
                ffu = ffu_d[l]
                ffd = ffd_d[l].rearrange("(kc p) n -> p kc n", p=128)
                for tt in range(NT):
                    ts_ = slice(tt * 512, (tt + 1) * 512)
                    for f in range(22):
                        if f % 2 == 0:
                            buf, key = wload([(wsrc(ffu, f * 128, 256, 8), 0), (wsrc(ffu, D_FF + f * 128, 256, 8), 2048)])
                            wg = buf[:, 0:2048].rearrange("p (a b) -> p a b", a=8)
                            wv_ = buf[:, 2048:4096].rearrange("p (a b) -> p a b", a=8)
                        par = f % 2
                        co = (f % 2) * 128
                        for gv in range(2):
                            wv = wg if gv == 0 else wv_
                            hb = par * 4 + gv * 2
                            mb = hb + 1
                            for kc in range(8):
                                S.op("pe", lambda e, kc=kc, wv=wv, co=co, mb=mb, ts_=ts_: e.matmul(
                                    bank(mb), wv[:, kc, co:co + 128], hT[:, kc, ts_], start=(kc == 0), stop=(kc == 7)),
                                    reads=[key, ("hT", tt)], writes=[("ps", mb)])
                            if tt > 0:
                                for kc in range(8):
                                    S.op("pe", lambda e, kc=kc, wv=wv, co=co, hb=hb, tt=tt: e.matmul(
                                        bank(hb)[:, 510:512], wv[:, kc, co:co + 128], hT[:, kc, tt * 512 - 2:tt * 512],
                                        start=(kc == 0), stop=(kc == 7)),
                                        reads=[key, ("hT", tt - 1)], writes=[("ps", hb)])
                            chn = f if gv == 0 else 22 + f
                            u0 = hb * 512 + 510
                            yb = (yg if gv == 0 else yv)[par]
                            ybk = ("f_y", gv, par)
                            S.op("act", lambda e, yb=yb, u0=u0, chn=chn: e.activation(
                                yb[:], PS[:, u0 + 2:u0 + 514], AF.Identity, bias=P(l, "ff_conv_b", chn),
                                scale=P(l, "ff_conv_w", chn * 3 + 2)),
                                reads=[("ps", hb), ("ps", mb), "pv"], writes=[ybk])
                            for tap in (1, 0):
                                S.op("dve", lambda e, yb=yb, u0=u0, chn=chn, tap=tap: e.scalar_tensor_tensor(
                                    yb[:], PS[:, u0 + tap:u0 + tap + 512], P(l, "ff_conv_w", chn * 3 + tap), yb[:],
                                    ALU.mult, ALU.add),
                                    reads=[("ps", hb), ("ps", mb), "pv", ybk], writes=[ybk])
                        S.op("act", lambda e, par=par: e.activation(sg[par][:], yg[par][:], AF.Silu),
                             reads=[("f_y", 0, par)], writes=[("f_sg", par)])
                        S.op("dve", lambda e, par=par, f=f: e.tensor_tensor(actT[:, f, :], sg[par][:], yv[par][:], ALU.mult),
                             reads=[("f_sg", par), ("f_y", 1, par)], writes=[("f_act", f)])
                    for oc in range(8):
                        buf, key = wload([(ffd[:, :, oc * 128:(oc + 1) * 128], 0)])
                        wd = buf[:, 0:22 * 128].rearrange("p (a b) -> p a b", a=22)
                        bk = 1 + 2 * (oc % 4)
                        for kc in range(22):
                            S.op("pe", lambda e, kc=kc, wd=wd, bk=bk: e.matmul(
                                bank(bk), wd[:, kc, :], actT[:, kc, :], start=(kc == 0), stop=(kc == 21)),
                                reads=[key, ("f_act", kc)], writes=[("ps", bk)])
                        S.op("dve", lambda e, oc=oc, bk=bk, ts_=ts_: e.scalar_tensor_tensor(
                            xT[:, oc, ts_], bank(bk), MT(l)[:, 40 + oc:41 + oc], xT[:, oc, ts_], ALU.mult, ALU.add),
                            reads=[("ps", bk), ("modT", l % 2), ("xT", tt)], writes=[("xT", tt)])
            S.barrier()

        for l in range(nl):
            S.mark("L%d_mod" % l)
            if l == 0:
                for _ in layer_mod_gen(0, bk=0):
                    pass
            bg_drain()
            if l + 1 < nl:
                bgs["g"] = layer_mod_gen(l + 1)
            with ExitStack() as stk:
                adanorm(l, 1, stk)
            S.barrier()
            if l == nl - 1:
                dump_now("modT", MT(l)[:], [("modT", l % 2)])
                dump_now("hT", hT[:], [("hT", t_) for t_ in range(4)])
            if "A" in stages:
                S.mark("L%d_A" % l)
                mixer_A(l)
            if "B" in stages:
                S.mark("L%d_B" % l)
                mixer_B(l)
            bg_drain()
            if "C" in stages:
                S.mark("L%d_C" % l)
                mixer_C(l)
            if "D" in stages:
                S.mark("L%d_D" % l)
                mixer_D(l)
            if "F" in stages:
                S.mark("L%d_F" % l)
                ffn(l)

        S.mark("final")
        fin_ops = []
        with ExitStack() as stf:
            xo = [sb("xo%d" % i, [128, D_], F32, stf) for i in range(2)]
            for tb in range(16):
                xi = xo[tb % 2]
                for g in range(2):
                    bk = (tb * 2 + g) % 8
                    for cc in range(4):
                        c = g * 4 + cc
                        S.op("pe", lambda e, c=c, cc=cc, bk=bk, tb=tb: e.transpose(
                            bank(bk, 128, cc * 128), xT[:, c, tb * 128:(tb + 1) * 128], C("ident")),
                            reads=[("xT", tb // 4), "cst"], writes=[("ps", bk)])
                    if g == 0:
                        S.op("dve", lambda e, xi=xi, bk=bk: e.tensor_copy(xi[:, 0:512], bank(bk)),
                             reads=[("ps", bk)], writes=[("xo", tb % 2)])
                    else:
                        S.op("act", lambda e, xi=xi, bk=bk: e.activation(xi[:, 512:1024], bank(bk), AF.Copy),
                             reads=[("ps", bk)], writes=[("xo", tb % 2)])
                fin_ops.append(S.op("sp", lambda e, xi=xi, tb=tb: e.dma_start(
                    out=out_d[tb * 128:(tb + 1) * 128, :], in_=xi[:]),
                    reads=[("xo", tb % 2)], dma_key=("xo", tb % 2)))
        S.emit(final_wait_ops=fin_ops + dump_ops)
    nc._sched_stats = (S.ninst, S.nwaits)
    nc._marks = getattr(S, 'marks', [])
    return nc


def prep_shared(inp):
    f = lambda k: np.asarray(inp[k], np.float32)
    pvs = np.zeros((128, NL * NCOL), np.float32)
    for l in range(NL):
        def put(name, arr, j=0):
            o, k = PCOLS[name]
            arr = np.asarray(arr, np.float32)
            pvs[:arr.shape[0], l * NCOL + o + j: l * NCOL + o + j + arr.shape[1]] = arr
        put("norm1_g", _fm(f("norm1_g")[l]))
        put("norm2_g", _fm(f("norm2_g")[l]))
        put("b_ada", _fm(f("b_ada")[l]))
        gq = np.tile(f("a_qnorm_g")[l], 4)[:, None]
        gk = np.tile(f("a_knorm_g")[l], 4)[:, None]
        sub = (np.arange(128) // 32) % 2
        put("gqA", gq)
        put("gkA0", np.where(sub[:, None] == 0, gk, 0.0))
        put("gkA1", np.where(sub[:, None] == 1, gk, 0.0))
        put("lq", np.tile(f("a_lam_q")[l].reshape(1, 64), (128, 1)))
        put("lk", np.tile(f("a_lam_k")[l].reshape(1, 64), (128, 1)))
        put("a_out_g", _fm(f("a_out_g")[l]))
        put("b_out_g", _fm(f("b_out_g")[l]))
        cw = f("c_conv_w")[l]
        for cc in range(2):
            put("c_conv_w", cw[:, cc * 128:(cc + 1) * 128].T, cc * 4)
        for nm, key in (("c_conv_b", "c_conv_b"), ("c_ga_b", "c_gate_a_b"), ("c_gx_b", "c_gate_x_b"),
                        ("c_lambda", "c_lambda"), ("c_out_g", "c_out_g"), ("d_mu", "d_mu"), ("d_w0", "d_w0"),
                        ("d_a0", "d_a0"), ("d_k_k", "d_k_k"), ("d_k_a", "d_k_a"), ("d_lnx_w", "d_lnx_w"),
                        ("d_lnx_b", "d_lnx_b")):
            put(nm, _fm(f(key)[l]))
        put("d_r_k", _fm(f("d_r_k")[l].reshape(-1)))
        if l > 0:
            put("d_v0", _fm(f("d_v0")[l - 1]))
        fw = f("ff_conv_w")[l]
        for chn in range(44):
            put("ff_conv_w", fw[:, chn * 128:(chn + 1) * 128].T, chn * 3)
        put("ff_conv_b", _fm(f("ff_conv_b")[l]))
    gate = np.zeros((NL, 2, 2, 128, 128), np.float32)
    for l in range(NL):
        for gi, key in enumerate(("c_gate_a_w", "c_gate_x_w")):
            w = f(key)[l]
            for n in range(4):
                cc, hh = n // 2, n % 2
                gate[l, gi, cc, hh * 64:(hh + 1) * 64, hh * 64:(hh + 1) * 64] = w[n]
    shared = {
        "cst": _consts(), "pv": pvs,
        "w_ada": f("w_ada"), "w_in": f("w_in"), "w_out": f("w_out"),
        "ff_w_up": f("ff_w_up"), "ff_w_down": f("ff_w_down"),
        "gatew": gate, "d_w_up": f("d_w_up"), "d_a_up": f("d_a_up"), "d_g_up": f("d_g_up"),
        "d_v_down": f("d_v_down"), "d_v_up": f("d_v_up"),
    }
    return shared


def make_in_maps(inp, cores):
    shared = prep_shared(inp)
    x = np.asarray(inp["x"], np.float32)
    c = np.asarray(inp["c"], np.float32)
    maps = []
    for b in cores:
        m = dict(shared)
        m["x"] = np.ascontiguousarray(x[b])
        m["c"] = _fm(c[b])
        maps.append(m)
    return maps


_NC_CACHE = {}


def kernel(**inputs):
    if "nc" not in _NC_CACHE:
        _NC_CACHE["nc"] = build_program()
    nc = _NC_CACHE["nc"]
    maps = make_in_maps(inputs, list(range(8)))
    res = run_bass_kernel_spmd(nc, maps, core_ids=list(range(8)))
    return np.stack([np.asarray(r["out"], np.float32) for r in res.results], 0)
```

```python
import math
from contextlib import ExitStack

import numpy as np
import concourse.bass as bass
import concourse.mybir as mybir
from concourse.bass_utils import run_bass_kernel_spmd

F32 = mybir.dt.float32
BF16 = mybir.dt.bfloat16
AF = mybir.ActivationFunctionType
ALU = mybir.AluOpType
AX = mybir.AxisListType

S_ = 2048
D_ = 1024
NL = 4
GROUP = 256
N_IN = 2944
D_FF = 2816
NT = 4
D_STOP = 99
D_XI = (0, 1, 2)
D_NOEV = False
D_VAR = 0
RMS_EPS = 1e-6
LNX_EPS = 64e-5

PCOLS = {}
_off = 0
for _n, _k in (("norm1_g", 8), ("norm2_g", 8), ("b_ada", 48), ("gqA", 1), ("gkA0", 1), ("gkA1", 1),
               ("a_out_g", 2), ("b_out_g", 2),
               ("c_conv_w", 8), ("c_conv_b", 2), ("c_ga_b", 2), ("c_gx_b", 2), ("c_lambda", 2), ("c_out_g", 2),
               ("d_mu", 7), ("d_w0", 2), ("d_a0", 2), ("d_k_k", 2), ("d_k_a", 2), ("d_r_k", 2),
               ("d_lnx_w", 2), ("d_lnx_b", 2), ("d_v0", 2), ("ff_conv_w", 132), ("ff_conv_b", 44)):
    PCOLS[_n] = (_off, _k)
    _off += _k
NCOL = _off

CC = {}
_off = 0
for _n, _k in (("ident", 128), ("maskLE", 128), ("maskLT", 128), ("blk32", 128), ("blk64m", 128),
               ("blk64s", 128), ("onesD", 128), ("dmask", 4 * 320), ("id64x4", 4 * 64)):
    CC[_n] = (_off, _k)
    _off += _k
CCOL = _off


def _fm(v):
    return np.ascontiguousarray(np.asarray(v, np.float32).reshape(-1, 128).T)


def _consts():
    c = np.zeros((128, CCOL), np.float32)
    i = np.arange(128)
    def put(n, a):
        o, k = CC[n]
        c[:a.shape[0], o:o + k] = a
    put("ident", np.eye(128, dtype=np.float32))
    put("maskLE", (i[None, :] >= i[:, None]).astype(np.float32))
    put("maskLT", (i[None, :] < i[:, None]).astype(np.float32))
    put("blk32", ((i[:, None] // 32) == (i[None, :] // 32)).astype(np.float32) / 32.0)
    put("blk64m", ((i[:, None] // 64) == (i[None, :] // 64)).astype(np.float32) / 64.0)
    put("blk64s", ((i[:, None] // 64) == (i[None, :] // 64)).astype(np.float32))
    put("onesD", np.full((128, 128), 1.0 / 1024.0, np.float32))
    j = np.arange(64)
    gt = (j[None, :] > j[:, None]).astype(np.float32)
    ge = (j[None, :] >= j[:, None]).astype(np.float32)
    lt = (j[None, :] < j[:, None]).astype(np.float32)
    m = np.concatenate([-gt, -ge, gt, ge, -lt], 1)
    put("dmask", np.tile(m, (1, 4)))
    put("id64x4", np.tile(np.eye(64, dtype=np.float32), (1, 4)))
    return c


class _Op:
    __slots__ = ("eng", "fn", "deps", "is_dma", "sem", "val", "waited")

    def __init__(self, eng, fn, is_dma):
        self.eng = eng
        self.fn = fn
        self.deps = []
        self.is_dma = is_dma
        self.sem = None
        self.val = 0
        self.waited = False


class Sched:
    ENGS = ("pe", "act", "dve", "pool", "sp")

    def __init__(self, nc):
        self.nc = nc
        self.streams = {e: [] for e in self.ENGS}
        self.lastw = {}
        self.readers = {}
        self.dma_sems = {}
        self.all_ops = []
        self.pending = []

    def op(self, eng, fn, reads=(), writes=(), dma_key=None):
        o = _Op(eng, fn, dma_key is not None)
        self.all_ops.append(o)
        seen = set()
        def add(d):
            if d is not None and id(d) not in seen:
                seen.add(id(d))
                o.deps.append(d)
        for k in reads:
            add(self.lastw.get(k))
        for k in writes:
            add(self.lastw.get(k))
            for r in self.readers.get(k, ()):
                add(r)
        for k in writes:
            self.lastw[k] = o
            self.readers[k] = []
        wset = set(writes)
        for k in reads:
            if k not in wset:
                self.readers.setdefault(k, []).append(o)
        if dma_key is not None:
            ent = self.dma_sems.get(dma_key)
            if ent is None:
                ent = [self.nc.alloc_semaphore("d%d" % len(self.dma_sems)), 0]
                self.dma_sems[dma_key] = ent
            ent[1] += 16
            o.sem = ent[0]
            o.val = ent[1]
        self.streams[eng].append(o)
        self.last_eng = getattr(self, "last_eng", {})
        self.last_eng[eng] = o
        if o.is_dma:
            self.pending.append(o)
        return o

    def mark(self, name):
        self.marks = getattr(self, "marks", [])
        self.marks.append((name, {e: sum(1 for o in self.streams[e] if o.fn is not None) for e in self.ENGS}))

    def barrier(self):
        prev = [self.last_eng[e] for e in self.ENGS if e != "pool" and e in self.last_eng and not self.last_eng[e].is_dma]
        prev = [p for p in prev if p.fn is not None]
        dmas = list(self.pending)
        self.pending = []
        for e in self.ENGS:
            if e == "pool":
                continue
            o = _Op(e, None, False)
            o.deps = [p for p in prev] + dmas
            self.all_ops.append(o)
            self.streams[e].append(o)

    def emit(self, final_wait_ops=()):
        nc = self.nc
        eng_sem = {e: nc.alloc_semaphore("e_" + e) for e in self.ENGS}
        for o in self.all_ops:
            for d in o.deps:
                if d.is_dma:
                    d.waited = True
                elif d.eng == o.eng and o.eng == "pe" and o.fn is not None:
                    continue
                else:
                    d.waited = True
        for e in self.ENGS:
            cnt = 0
            for o in self.streams[e]:
                if o.is_dma or o.fn is None:
                    continue
                if o.waited:
                    cnt += 1
                    o.sem = eng_sem[e]
                    o.val = cnt
        self.nwaits = {e: 0 for e in self.ENGS}
        self.ninst = {e: len(self.streams[e]) for e in self.ENGS}

        def run(engname, eng):
            known = {}
            for o in self.streams[engname]:
                need = {}
                for d in o.deps:
                    if (not d.is_dma) and d.eng == engname and engname == "pe" and o.fn is not None:
                        continue
                    if d.sem is None:
                        continue
                    key = id(d.sem)
                    if known.get(key, 0) >= d.val:
                        continue
                    if key not in need or need[key][1] < d.val:
                        need[key] = (d.sem, d.val)
                for key, (sem, val) in need.items():
                    eng.wait_ge(sem, val)
                    known[key] = val
                    self.nwaits[engname] += 1
                if o.fn is None:
                    continue
                ins = o.fn(eng)
                if o.is_dma:
                    ins.then_inc(o.sem, 16)
                elif o.waited:
                    ins.then_inc(o.sem, 1)
            if engname == "sp":
                for d in final_wait_ops:
                    eng.wait_ge(d.sem, d.val)

        with nc.Block() as block:
            @block.tensor
            def _(e):
                run("pe", e)

            @block.scalar
            def _(e):
                run("act", e)

            @block.vector
            def _(e):
                run("dve", e)

            @block.gpsimd
            def _(e):
                run("pool", e)

            @block.sync
            def _(e):
                run("sp", e)


def build_program(nl=NL, stages=("A", "B", "C", "D", "F"), dumps=()):
    nc = bass.Bass("TRN2", target_bir_lowering=False)
    dr = {}
    def din(name, shape):
        dr[name] = nc.dram_tensor(name, list(shape), F32, kind="ExternalInput").ap()
        return dr[name]
    x_d = din("x", (S_, D_))
    c_d = din("c", (128, 8))
    cst_d = din("cst", (128, CCOL))
    pv_d = din("pv", (128, NL * NCOL))
    w_ada_d = din("w_ada", (NL, D_, 6 * D_))
    w_in_d = din("w_in", (NL, D_, N_IN))
    w_out_d = din("w_out", (NL, D_, D_))
    ffu_d = din("ff_w_up", (NL, D_, 2 * D_FF))
    ffd_d = din("ff_w_down", (NL, D_FF, D_))
    gate_d = din("gatew", (NL, 2, 2, 128, 128))
    dwup_d = din("d_w_up", (NL, 32, 256))
    daup_d = din("d_a_up", (NL, 32, 256))
    dgup_d = din("d_g_up", (NL, 64, 256))
    dvdn_d = din("d_v_down", (NL - 1, 256, 32))
    dvup_d = din("d_v_up", (NL - 1, 32, 256))
    lamqk_d = din("lamqk", (NL, 128, 128))
    out_d = nc.dram_tensor("out", [S_, D_], F32, kind="ExternalOutput").ap()
    dump_d = {}
    for (nm, shp) in dumps:
        dump_d[nm] = nc.dram_tensor("dbg_" + nm, list(shp), F32, kind="ExternalOutput").ap()

    S = Sched(nc)
    es = ExitStack()

    _uid = [0]

    def sb(name, shape, dt, stack=None):
        _uid[0] += 1
        return (stack or es).enter_context(nc.sbuf_tensor("s%d_%s" % (_uid[0], name), list(shape), dt))

    with es:
        PS = es.enter_context(nc.psum_tensor("PS", [128, 4096], F32))
        def bank(i, w=512, off=0):
            return PS[:, i * 512 + off: i * 512 + off + w]
        def bankb(i):
            return PS[:, i * 512:(i + 1) * 512].bitcast(BF16)

        xT = sb("xT", [128, 8, S_], F32)
        hT = sb("hT", [128, 8, S_], BF16)
        yT = sb("yT", [128, 2, S_], BF16)
        pv = sb("pv", [128, NL * NCOL], F32)
        cst = sb("cst", [128, 128], F32)
        cstb = sb("cstb", [128, CCOL], BF16)
        NWB = 3
        wbufs = [sb("wbuf%d" % i, [128, 4096], BF16) for i in range(NWB)]
        modT = sb("modT", [128, 48], F32)
        lvec = sb("lvec", [128, 64], F32)
        condb = sb("condb", [128, 8], BF16)
        cin = sb("cin", [128, 8], F32)
        epsc = sb("epsc", [128, 8], F32)
        vfirst = sb("vfirst", [128, 2, S_], BF16)
        gw_p = sb("gw_p", [128, 2, 2, 128], BF16)
        dsw = {n_: sb("dsw_" + n_, [128, 256], BF16) for n_ in ("wup", "aup", "gup", "vup")}
        vdn_p = sb("vdn_p", [128, 2, 32], BF16)
        modTs = [modT, sb("modT1", [128, 48], F32)]
        lvecs = [lvec, sb("lvec1", [128, 64], F32)]

        def C(name, rows=128):
            assert name == "ident"
            return cst[0:rows, 0:128]
        def Cb(name, rows=128):
            o, k = CC[name]
            return cstb[0:rows, o:o + k]
        def P(l, name, j=0, n=1):
            o, k = PCOLS[name]
            return pv[:, l * NCOL + o + j: l * NCOL + o + j + n]

        LV_A1, LV_A2, LV_LAM, LV_CL, LV_NW0, LV_1MU, LV_TMP = 0, 8, 16, 20, 24, 28, 36

        dump_ops = []
        def dump_now(nm, ap, keys=()):
            if nm in dump_d:
                dump_ops.append(S.op("pool", lambda e: e.dma_start(out=dump_d[nm], in_=ap),
                                     reads=list(keys), dma_key=("dump", nm)))

        wstate = {"n": 0}
        def wload(srcs):
            i = wstate["n"] % NWB
            wstate["n"] += 1
            buf = wbufs[i]
            key = ("wbuf", i)
            for (src, off) in srcs:
                a, b = src.shape[1], src.shape[2]
                dst = buf[:, off:off + a * b].rearrange("p (a b) -> p a b", a=a)
                S.op("pool", lambda e, dst=dst, src=src: e.dma_start(out=dst, in_=src),
                     writes=[key], dma_key=("w", i))
            return buf, key

        def wsrc(dram2d, c0, ncols, kchunks):
            return dram2d.rearrange("(kc p) n -> p kc n", p=128)[:, 0:kchunks, c0:c0 + ncols]

        S.op("sp", lambda e: e.dma_start(out=cst[:], in_=cst_d[:, 0:128]), writes=["cst"], dma_key="cst")
        S.op("pool", lambda e: e.dma_start(out=cstb[:], in_=cst_d), writes=["cstb"], dma_key="cstb")
        S.op("sp", lambda e: e.dma_start(out=pv[:], in_=pv_d), writes=["pv"], dma_key="pv")
        S.op("sp", lambda e: e.dma_start(out=cin[:], in_=c_d), writes=["cin"], dma_key="cin")
        S.op("dve", lambda e: e.memset(epsc[:, 0:1], RMS_EPS), writes=["epsc"])
        S.op("dve", lambda e: e.memset(epsc[:, 1:2], LNX_EPS), writes=["epsc"])
        S.op("dve", lambda e: e.memset(epsc[:, 2:3], 1e-24), writes=["epsc"])
        S.op("dve", lambda e: e.memset(epsc[:, 3:4], 1.0), writes=["epsc"])
        S.op("dve", lambda e: e.memset(epsc[:, 4:5], -0.5), writes=["epsc"])
        S.op("act", lambda e: e.activation(condb[:], cin[:], AF.Silu), reads=["cin"], writes=["condb"])

        with ExitStack() as st0:
            xin = [sb("xin%d" % i, [128, D_], F32, st0) for i in range(2)]
            for tb in range(16):
                xi = xin[tb % 2]
                S.op("sp", lambda e, xi=xi, tb=tb: e.dma_start(out=xi[:], in_=x_d[tb * 128:(tb + 1) * 128, :]),
                     writes=[("xin", tb % 2)], dma_key=("xin", tb % 2))
                for g in range(2):
                    bk = (tb * 2 + g) % 8
                    for cc in range(4):
                        c = g * 4 + cc
                        S.op("pe", lambda e, xi=xi, c=c, cc=cc, bk=bk: e.transpose(
                            bank(bk, 128, cc * 128), xi[:, c * 128:(c + 1) * 128], C("ident")),
                            reads=[("xin", tb % 2), "cst"], writes=[("ps", bk)])
                    S.op("dve" if g == 0 else "act",
                         (lambda e, g=g, tb=tb, bk=bk: e.tensor_copy(
                             xT[:, g * 4:(g + 1) * 4, tb * 128:(tb + 1) * 128],
                             bank(bk).rearrange("p (a b) -> p a b", a=4))) if g == 0 else
                         (lambda e, g=g, tb=tb, bk=bk: e.activation(
                             xT[:, g * 4:(g + 1) * 4, tb * 128:(tb + 1) * 128],
                             bank(bk).rearrange("p (a b) -> p a b", a=4), AF.Copy)),
                         reads=[("ps", bk)], writes=[("xT", tb // 4)])
        S.barrier()

        def rstd_from_ms(ps_ap, out_ap, eps_col, rkeys, wkeys, tmp_ap):
            S.op("act", lambda e: e.activation(tmp_ap, ps_ap, AF.Ln, bias=epsc[0:ps_ap.shape[0], eps_col:eps_col + 1]),
                 reads=list(rkeys) + ["epsc"], writes=[("tmp", id(tmp_ap.tensor))])
            S.op("act", lambda e: e.activation(out_ap, tmp_ap, AF.Exp, scale=-0.5),
                 reads=[("tmp", id(tmp_ap.tensor))], writes=list(wkeys))

        def MT(l):
            return modTs[l % 2]

        def LVA(l):
            return lvecs[l % 2]

        def layer_mod_gen(l, bk=7):
            mt, lv, par = MT(l), LVA(l), l % 2
            for blk in range(12):
                buf, key = wload([(wsrc(w_ada_d[l], blk * 512, 512, 8), 0)])
                wv = buf[:, 0:4096].rearrange("p (a b) -> p a b", a=8)
                for j in range(4):
                    for kc in range(8):
                        S.op("pe", lambda e, wv=wv, j=j, kc=kc: e.matmul(
                            bank(bk, 1, j), wv[:, kc, j * 128:(j + 1) * 128], condb[:, kc:kc + 1],
                            start=(kc == 0), stop=(kc == 7)),
                            reads=[key, "condb"], writes=[("ps", bk)])
                S.op("dve", lambda e, blk=blk: e.tensor_tensor(mt[:, blk * 4:(blk + 1) * 4], bank(bk, 4),
                                                               P(l, "b_ada", blk * 4, 4), ALU.add),
                     reads=[("ps", bk), "pv"], writes=[("modT", par)])
                yield
            S.op("dve", lambda e: e.scalar_tensor_tensor(lv[:, LV_A1:LV_A1 + 8], mt[:, 8:16], 1.0,
                                                         P(l, "norm1_g", 0, 8), ALU.add, ALU.mult),
                 reads=[("modT", par), "pv"], writes=[("lvec_a", par)])
            S.op("dve", lambda e: e.scalar_tensor_tensor(lv[:, LV_A2:LV_A2 + 8], mt[:, 32:40], 1.0,
                                                         P(l, "norm2_g", 0, 8), ALU.add, ALU.mult),
                 reads=[("modT", par), "pv"], writes=[("lvec_a", par)])
            yield

        bgs = {"g": None}

        def bg_step():
            g = bgs["g"]
            if g is None:
                return False
            try:
                next(g)
                return True
            except StopIteration:
                bgs["g"] = None
                return False

        def bg_drain():
            while bg_step():
                pass

        def adanorm(l, which, stk):
            acol = LV_A1 if which == 1 else LV_A2
            shc = 0 if which == 1 else 24
            sq = sb("n_sq", [128, 8, 512], BF16, stk)
            rstd = sb("n_rstd", [128, 512], F32, stk)
            ltmp = sb("n_ltmp", [128, 512], F32, stk)
            u = [sb("n_u%d" % i, [128, 512], F32, stk) for i in range(2)]
            for tt in range(NT):
                ts_ = slice(tt * 512, (tt + 1) * 512)
                S.op("act", lambda e, ts_=ts_: e.activation(sq[:], xT[:, :, ts_], AF.Square),
                     reads=[("xT", tt)], writes=["n_sq"])
                bk = tt % 2
                for kc in range(8):
                    S.op("pe", lambda e, kc=kc, bk=bk: e.matmul(bank(bk), Cb("onesD"), sq[:, kc, :],
                                                                start=(kc == 0), stop=(kc == 7)),
                         reads=["n_sq", "cstb"], writes=[("ps", bk)])
                rstd_from_ms(bank(bk), rstd[:], 0, [("ps", bk)], ["n_rstd"], ltmp[:])
                for c in range(8):
                    ub = u[c % 2]
                    S.op("dve", lambda e, c=c, ub=ub, ts_=ts_: e.scalar_tensor_tensor(
                        ub[:], xT[:, c, ts_], LVA(l)[:, acol + c:acol + c + 1], rstd[:], ALU.mult, ALU.mult),
                        reads=[("xT", tt), ("lvec_a", l % 2), "n_rstd"], writes=[("n_u", c % 2)])
                    S.op("act", lambda e, c=c, ub=ub, ts_=ts_: e.activation(
                        hT[:, c, ts_], ub[:], AF.Identity, bias=MT(l)[:, shc + c:shc + c + 1]),
                        reads=[("n_u", c % 2), ("modT", l % 2)], writes=[("hT", tt)])

        def proj_fm(l_w2d, c0, mcols, tt, bk, key_extra=()):
            wv, key, coff = l_w2d
            for kc in range(8):
                S.op("pe", lambda e, kc=kc: e.matmul(
                    bank(bk)[0:mcols, :], wv[:, kc, coff + c0:coff + c0 + mcols], hT[:, kc, tt * 512:(tt + 1) * 512],
                    start=(kc == 0), stop=(kc == 7)),
                    reads=[key, ("hT", tt)], writes=[("ps", bk)])

        def resid_update(l, gidx, stk_unused=None):
            buf, key = wload([(w_out_d[l][gidx * 256:(gidx + 1) * 256, :].rearrange("(kc p) n -> p kc n", p=128)[:, :, 0:1024], 0)])
            wv = buf[:, 0:2048].rearrange("p (a b) -> p a b", a=2)
            n = 0
            for oc in range(8):
                for tt in range(NT):
                    bk = n % 4
                    n += 1
                    for kc in range(2):
                        S.op("pe", lambda e, kc=kc, oc=oc, tt=tt, bk=bk: e.matmul(
                            bank(bk), wv[:, kc, oc * 128:(oc + 1) * 128], yT[:, kc, tt * 512:(tt + 1) * 512],
                            start=(kc == 0), stop=(kc == 1)),
                            reads=[key, ("yT", kc, tt)], writes=[("ps", bk)])
                    S.op("dve", lambda e, oc=oc, tt=tt, bk=bk: e.scalar_tensor_tensor(
                        xT[:, oc, tt * 512:(tt + 1) * 512], bank(bk), MT(l)[:, 16 + oc:17 + oc],
                        xT[:, oc, tt * 512:(tt + 1) * 512], ALU.mult, ALU.add),
                        reads=[("ps", bk), ("modT", l % 2), ("xT", tt)], writes=[("xT", tt)])

        def mixer_A(l):
            lam_init = 0.8 - 0.6 * math.exp(-0.3 * l)
            with ExitStack() as stk:
                qA = sb("qA", [128, 2, S_], BF16, stk)
                kA = [sb("kA%d" % c, [128, 2, S_], BF16, stk) for c in range(2)]
                VA = sb("VA", [128, 16, 4, 65], BF16, stk)
                sqb = sb("a_sq", [128, 512], BF16, stk)
                rst = sb("a_rst", [128, 512], F32, stk)
                ltm = sb("a_ltm", [128, 512], F32, stk)
                pT = [sb("a_pT%d" % i, [128, 512], BF16, stk) for i in range(3)]
                rec = sb("a_rec", [128, 2, 4], F32, stk)
                Oa = sb("a_O", [128, 4, 64], F32, stk)
                Ob = sb("a_Ob", [128, 4, 64], F32, stk)
                ssq = sb("a_ssq", [128, 4], F32, stk)
                junk = sb("a_junk", [128, 64], F32, stk)
                ytok = sb("a_ytok", [128, 16, 256], BF16, stk)
                lqk = sb("a_lqk", [128, 64], F32, stk)
                lqin = sb("a_lqin", [128, 128], F32, stk)
                S.op("sp", lambda e: e.dma_start(out=lqin[:], in_=lamqk_d[l]), writes=["a_lqin"], dma_key="a_lqin")
                S.op("dve", lambda e: e.tensor_tensor(lqk[:], lqin[:, 0:64], lqin[:, 64:128], ALU.mult),
                     reads=["a_lqin"], writes=["a_lqk"])
                S.op("dve", lambda e: e.tensor_reduce(lvec[:, LV_LAM:LV_LAM + 2],
                                                      lqk[:].rearrange("p (a b) -> p a b", a=2), AX.X, ALU.add),
                     reads=["a_lqk"], writes=["lv_lam"])
                S.op("act", lambda e: e.activation(lvec[:, LV_LAM:LV_LAM + 2], lvec[:, LV_LAM:LV_LAM + 2], AF.Exp),
                     reads=["lv_lam"], writes=["lv_lam"])
                S.op("dve", lambda e: e.scalar_tensor_tensor(lvec[:, LV_LAM + 2:LV_LAM + 3], lvec[:, LV_LAM + 1:LV_LAM + 2],
                                                             -lam_init, lvec[:, LV_LAM:LV_LAM + 1], ALU.add, ALU.subtract),
                     reads=["lv_lam"], writes=["lv_nlam"])
                S.op("dve", lambda e: e.memset(VA[:, :, :, 64:65], 1.0), writes=["VA1"])
                buf0, key0 = wload([(wsrc(w_in_d[l], 0, 512, 8), 0)])
                wv0 = buf0[:, 0:4096].rearrange("p (a b) -> p a b", a=8)
                buf1, key1 = wload([(wsrc(w_in_d[l], 512, 256, 8), 0)])
                wv1 = buf1[:, 0:2048].rearrange("p (a b) -> p a b", a=8)
                n = 0
                for tt in range(NT):
                    ts_ = slice(tt * 512, (tt + 1) * 512)
                    for ch in range(4):
                        bk = n % 2
                        bk2 = 2 + n % 2
                        n += 1
                        proj_fm((wv0, key0, 0), ch * 128, 128, tt, bk)
                        S.op("act", lambda e, bk=bk: e.activation(sqb[:], bank(bk), AF.Square),
                             reads=[("ps", bk)], writes=["a_sq"])
                        S.op("pe", lambda e, bk2=bk2: e.matmul(bank(bk2), Cb("blk32"), sqb[:], start=True, stop=True),
                             reads=["a_sq", "cstb"], writes=[("ps", bk2)])
                        rstd_from_ms(bank(bk2), rst[:], 0, [("ps", bk2)], ["a_rst"], ltm[:])
                        if ch < 2:
                            S.op("dve", lambda e, bk=bk, ch=ch, ts_=ts_: e.scalar_tensor_tensor(
                                qA[:, ch, ts_], bank(bk), P(l, "gqA"), rst[:], ALU.mult, ALU.mult),
                                reads=[("ps", bk), "a_rst", "pv"], writes=[("qA", tt)])
                        else:
                            for c in range(2):
                                S.op("dve", lambda e, bk=bk, ch=ch, ts_=ts_, c=c: e.scalar_tensor_tensor(
                                    kA[c][:, ch - 2, ts_], bank(bk), P(l, "gkA%d" % c), rst[:], ALU.mult, ALU.mult),
                                    reads=[("ps", bk), "a_rst", "pv"], writes=[("kA", tt)])
                for tb in range(16):
                    bk = 4 + tb % 2
                    for kc in range(8):
                        S.op("pe", lambda e, kc=kc, tb=tb, bk=bk: e.matmul(
                            bank(bk, 256), hT[:, kc, tb * 128:(tb + 1) * 128], wv1[:, kc, 0:256],
                            start=(kc == 0), stop=(kc == 7)),
                            reads=[key1, ("hT", tb // 4)], writes=[("ps", bk)])
                    S.op("act", lambda e, tb=tb, bk=bk: e.activation(
                        VA[:, tb, :, 0:64], bank(bk, 256).rearrange("p (a b) -> p a b", a=4), AF.Copy),
                        reads=[("ps", bk)], writes=[("VA", tb)])
                scale = 32.0 ** -0.5
                units = [(h, t, i, c) for h in range(4) for t in range(NT) for i in range(4 * t + 4) for c in range(2)]
                LA = 2
                obk = (6, 7)

                def a_score(n):
                    h, t, i, c = units[n]
                    ch, r0 = h // 2, (h % 2) * 64
                    col0 = max(0, i - 4 * t) * 128
                    sbk = n % 4
                    S.op("pe", lambda e: e.matmul(
                        bank(sbk)[:, col0:512], kA[c][r0:r0 + 64, ch, i * 128:(i + 1) * 128],
                        qA[r0:r0 + 64, ch, t * 512 + col0:(t + 1) * 512], start=True, stop=True),
                        reads=[("kA", i // 4), ("qA", t)], writes=[("ps", sbk)])

                def a_consume(n):
                    h, t, i, c = units[n]
                    jj0 = max(0, i - 4 * t)
                    col0 = jj0 * 128
                    sbk = n % 4
                    pt = pT[n % 3]
                    ptk = ("a_pT", n % 3)
                    S.op("act", lambda e: e.activation(pt[:, col0:512], bank(sbk)[:, col0:512], AF.Exp, scale=scale),
                         reads=[("ps", sbk)], writes=[ptk])
                    if i >= 4 * t:
                        S.op("dve", lambda e: e.tensor_tensor(
                            pt[:, col0:col0 + 128], pt[:, col0:col0 + 128], Cb("maskLE"), ALU.mult),
                            reads=[ptk, "cstb"], writes=[ptk])
                    for jj in range(jj0, 4):
                        S.op("pe", lambda e, jj=jj: e.matmul(
                            bank(obk[c])[:, jj * 65:(jj + 1) * 65], pt[:, jj * 128:(jj + 1) * 128],
                            VA[:, i, h, :], start=(i == 0 and jj == 0), stop=(i == 4 * t + jj), skip_group_check=True),
                            reads=[ptk, ("VA", i), "VA1"], writes=[("ps", obk[c])])
                    if i == 4 * t + 3 and c == 1:
                        a_epilogue(h, t)

                def a_epilogue(h, t):
                    o0 = bank(6, 260).rearrange("p (a b) -> p a b", a=4)
                    o1 = bank(7, 260).rearrange("p (a b) -> p a b", a=4)
                    S.op("dve", lambda e: e.reciprocal(rec[:, 0, :], o0[:, :, 64]), reads=[("ps", 6)], writes=["a_rec"])
                    S.op("dve", lambda e: e.reciprocal(rec[:, 1, :], o1[:, :, 64]), reads=[("ps", 7)], writes=["a_rec"])
                    S.op("dve", lambda e: e.tensor_scalar(rec[:, 1, :], rec[:, 1, :], lvec[:, LV_LAM + 2:LV_LAM + 3], None, ALU.mult),
                         reads=["a_rec", "lv_nlam"], writes=["a_rec"])
                    S.op("dve", lambda e: e.tensor_tensor(
                        Oa[:], o0[:, :, 0:64], rec[:, 0, :].unsqueeze(2).to_broadcast([128, 4, 64]), ALU.mult),
                        reads=[("ps", 6), "a_rec"], writes=["a_O"])
                    S.op("dve", lambda e: e.tensor_tensor(
                        Ob[:], o1[:, :, 0:64], rec[:, 1, :].unsqueeze(2).to_broadcast([128, 4, 64]), ALU.mult),
                        reads=[("ps", 7), "a_rec"], writes=["a_Ob"])
                    S.op("dve", lambda e: e.tensor_tensor(Oa[:], Oa[:], Ob[:], ALU.add), reads=["a_O", "a_Ob"], writes=["a_O"])
                    S.op("dve", lambda e: e.tensor_tensor(Ob[:], Oa[:], Oa[:], ALU.mult), reads=["a_O", "a_Ob"], writes=["a_Ob"])
                    S.op("dve", lambda e: e.tensor_reduce(ssq[:], Ob[:], AX.X, ALU.add), reads=["a_Ob"], writes=["a_ssq"])
                    S.op("act", lambda e: e.activation(ssq[:], ssq[:], AF.Ln, bias=epsc[:, 0:1], scale=1.0 / 64.0),
                         reads=["a_ssq", "epsc"], writes=["a_ssq"])
                    S.op("act", lambda e: e.activation(ssq[:], ssq[:], AF.Exp, scale=-0.5), reads=["a_ssq"], writes=["a_ssq"])
                    S.op("dve", lambda e: e.tensor_tensor(
                        ytok[:, 4 * t:4 * t + 4, h * 64:(h + 1) * 64], Oa[:],
                        ssq[:].unsqueeze(2).to_broadcast([128, 4, 64]), ALU.mult),
                        reads=["a_O", "a_ssq"], writes=[("a_ytok", t)])

                for n in range(len(units) + LA):
                    if n < len(units):
                        a_score(n)
                    if n - LA >= 0:
                        a_consume(n - LA)
                for tb in range(16):
                    bk = tb % 2
                    for kc in range(2):
                        S.op("pe", lambda e, tb=tb, kc=kc, bk=bk: e.transpose(
                            bankb(bk)[:, kc * 128:(kc + 1) * 128], ytok[:, tb, kc * 128:(kc + 1) * 128], Cb("ident")),
                            reads=[("a_ytok", tb // 4), "cstb"], writes=[("ps", bk)])
                        S.op("dve", lambda e, tb=tb, kc=kc, bk=bk: e.tensor_scalar(
                            yT[:, kc, tb * 128:(tb + 1) * 128], bankb(bk)[:, kc * 128:(kc + 1) * 128],
                            P(l, "a_out_g", kc), 1.0 - lam_init, ALU.mult, ALU.mult),
                            reads=[("ps", bk), "pv"], writes=[("yT", kc, tb // 4)])
                S.barrier()
                if l == nl - 1:
                    dump_now("yA", yT[:], [("yT", k_, t_) for k_ in range(2) for t_ in range(4)])
                resid_update(l, 0)
            S.barrier()


        def mixer_B(l):
            with ExitStack() as stk:
                qB = sb("qB", [128, 2, S_], BF16, stk)
                kB = sb("kB", [128, 2, S_], BF16, stk)
                VB = sb("VB", [128, 16, 256], BF16, stk)
                NSL = 4
                spb = [sb("b_sp%d" % i, [128, 512], F32, stk) for i in range(NSL)]
                lkb = [sb("b_lk%d" % i, [128, 512], F32, stk) for i in range(NSL)]
                pfx = [sb("b_pf%d" % i, [128, 512], F32, stk) for i in range(NSL)]
                wb = [sb("b_w%d" % i, [128, 512], BF16, stk) for i in range(NSL)]
                wT = [sb("b_wT%d" % i, [128, 512], BF16, stk) for i in range(NSL)]
                cT = [sb("b_cT%d" % i, [128, 2], F32, stk) for i in range(NSL)]
                ytok = [sb("b_ytok0", [128, 256], BF16, stk)] * 2
                ssq = sb("b_ssq", [128, 4], F32, stk)
                buf0, key0 = wload([(wsrc(w_in_d[l], 768, 512, 8), 0)])
                wv0 = buf0[:, 0:4096].rearrange("p (a b) -> p a b", a=8)
                buf1, key1 = wload([(wsrc(w_in_d[l], 1280, 256, 8), 0)])
                wv1 = buf1[:, 0:2048].rearrange("p (a b) -> p a b", a=8)
                n = 0
                for tt in range(NT):
                    ts_ = slice(tt * 512, (tt + 1) * 512)
                    for ch in range(4):
                        bk = n % 2
                        n += 1
                        proj_fm((wv0, key0, 0), ch * 128, 128, tt, bk)
                        dst = qB if ch < 2 else kB
                        S.op("act" if ch % 2 == 0 else "dve",
                             (lambda e, bk=bk, dst=dst, ch=ch, ts_=ts_: e.activation(dst[:, ch % 2, ts_], bank(bk), AF.Copy)) if ch % 2 == 0 else
                             (lambda e, bk=bk, dst=dst, ch=ch, ts_=ts_: e.tensor_copy(dst[:, ch % 2, ts_], bank(bk))),
                             reads=[("ps", bk)], writes=[("qB" if ch < 2 else "kB", tt)])
                for tb in range(16):
                    bk = 2 + tb % 2
                    for kc in range(8):
                        S.op("pe", lambda e, kc=kc, tb=tb, bk=bk: e.matmul(
                            bank(bk, 256), hT[:, kc, tb * 128:(tb + 1) * 128], wv1[:, kc, 0:256],
                            start=(kc == 0), stop=(kc == 7)),
                            reads=[key1, ("hT", tb // 4)], writes=[("ps", bk)])
                    S.op("act", lambda e, tb=tb, bk=bk: e.activation(VB[:, tb, :], bank(bk, 256), AF.Copy),
                         reads=[("ps", bk)], writes=[("VB", tb)])
                sc = 64.0 ** -0.5

                def head_chain(j, h, sl, yt, ytk):
                    ch, r0 = h // 2, (h % 2) * 64
                    ntile = j // 4 + 1
                    zb, tbk, ob = sl, 4 + sl % 2, 6
                    oc_ = slice(sl * 64, (sl + 1) * 64)
                    sp_, lk_, pf_, w_, wt_, ct_ = spb[sl], lkb[sl], pfx[sl], wb[sl], wT[sl], cT[sl]
                    K = lambda nm: (nm, sl)
                    step = 0
                    for kt in range(ntile - 1, -1, -1):
                        ncols = min(512, (j + 1) * 128 - kt * 512)
                        cs = slice(kt * 512, kt * 512 + ncols)
                        cn = slice(0, ncols)
                        diag = (kt == ntile - 1)
                        S.op("pe", lambda e, cs=cs, cn=cn: e.matmul(
                            bank(zb)[:, cn], qB[r0:r0 + 64, ch, j * 128:(j + 1) * 128], kB[r0:r0 + 64, ch, cs],
                            start=True, stop=True),
                            reads=[("qB", j // 4), ("kB", kt)], writes=[("ps", zb)])
                        yield
                        S.op("act", lambda e, cn=cn: e.activation(pf_[:, cn], bank(zb)[:, cn], AF.Exp, scale=-sc),
                             reads=[("ps", zb)], writes=[K("b_pf")])
                        yield
                        S.op("act", lambda e, cn=cn: e.activation(sp_[:, cn], pf_[:, cn], AF.Ln, bias=epsc[:, 3:4]),
                             reads=[K("b_pf"), "epsc"], writes=[K("b_sp")])
                        yield
                        S.op("dve", lambda e, cn=cn: e.scalar_tensor_tensor(
                            lk_[:, cn], bank(zb)[:, cn], -sc, sp_[:, cn], ALU.mult, ALU.subtract),
                            reads=[("ps", zb), K("b_sp")], writes=[K("b_lk")])
                        yield
                        if diag:
                            dl = slice((j % 4) * 128, (j % 4) * 128 + 128)
                            S.op("dve", lambda e, dl=dl: e.tensor_tensor(lk_[:, dl], lk_[:, dl], Cb("maskLT"), ALU.mult),
                                 reads=[K("b_lk"), "cstb"], writes=[K("b_lk")])
                            yield
                        S.op("dve", lambda e, cn=cn, ncols=ncols: e.tensor_tensor_scan(
                            pf_[:, cn], epsc[:, 3:4].to_broadcast([128, ncols]), lk_[:, cn], 0.0, ALU.mult, ALU.add),
                            reads=[K("b_lk"), K("b_pf"), "epsc"], writes=[K("b_pf")])
                        yield
                        cc, cp = step % 2, (step + 1) % 2
                        if step == 0:
                            S.op("dve", lambda e, ncols=ncols, cc=cc: e.tensor_copy(ct_[:, cc:cc + 1], pf_[:, ncols - 1:ncols]),
                                 reads=[K("b_pf")], writes=[K("b_cT")])
                        else:
                            S.op("dve", lambda e, ncols=ncols, cc=cc, cp=cp: e.tensor_tensor(
                                ct_[:, cc:cc + 1], ct_[:, cp:cp + 1], pf_[:, ncols - 1:ncols], ALU.add),
                                reads=[K("b_pf"), K("b_cT")], writes=[K("b_cT")])
                        yield
                        S.op("dve", lambda e, cn=cn: e.scalar_tensor_tensor(
                            lk_[:, cn], sp_[:, cn], -1.0, pf_[:, cn], ALU.mult, ALU.subtract),
                            reads=[K("b_sp"), K("b_pf"), K("b_lk")], writes=[K("b_lk")])
                        yield
                        S.op("act", lambda e, cn=cn, cc=cc: e.activation(w_[:, cn], lk_[:, cn], AF.Exp, bias=ct_[:, cc:cc + 1]),
                             reads=[K("b_lk"), K("b_cT")], writes=[K("b_w")])
                        yield
                        if diag:
                            dl = slice((j % 4) * 128, (j % 4) * 128 + 128)
                            S.op("dve", lambda e, dl=dl: e.tensor_tensor(w_[:, dl], w_[:, dl], Cb("maskLT"), ALU.mult),
                                 reads=[K("b_w"), "cstb"], writes=[K("b_w")])
                            yield
                        nb = ncols // 128
                        for ii in range(nb):
                            S.op("pe", lambda e, ii=ii: e.transpose(
                                bankb(tbk)[:, ii * 128:(ii + 1) * 128], w_[:, ii * 128:(ii + 1) * 128], Cb("ident")),
                                reads=[K("b_w"), "cstb"], writes=[("ps", tbk)])
                        S.op("act", lambda e, cn=cn: e.activation(wt_[:, cn], bankb(tbk)[:, cn], AF.Copy),
                             reads=[("ps", tbk)], writes=[K("b_wT")])
                        yield
                        for ii in range(nb):
                            i = kt * 4 + ii
                            st_f = (kt == ntile - 1 and ii == 0)
                            sp_f = (kt == 0 and ii == nb - 1)
                            st_f = st_f and sl == 0
                            S.op("pe", lambda e, ii=ii, i=i, st_f=st_f, sp_f=sp_f: e.matmul(
                                bank(ob)[:, oc_], wt_[:, ii * 128:(ii + 1) * 128], VB[:, i, h * 64:(h + 1) * 64],
                                start=st_f, stop=sp_f, skip_group_check=True),
                                reads=[K("b_wT"), ("VB", i)], writes=[("ps", ob)])
                        yield
                        step += 1
                    S.op("dve", lambda e: e.memset(ssq[:, sl:sl + 1], 0.0), writes=[K("b_ssq")])
                    yield
                    S.op("act", lambda e: e.activation(wt_[:, 0:64], bank(ob)[:, oc_], AF.Square, accum_out=ssq[:, sl:sl + 1]),
                         reads=[("ps", ob), K("b_ssq"), K("b_wT")], writes=[K("b_ssq"), K("b_wT")])
                    yield
                    S.op("act", lambda e: e.activation(ssq[:, sl:sl + 1], ssq[:, sl:sl + 1], AF.Ln, bias=epsc[:, 0:1], scale=1.0 / 64.0),
                         reads=[K("b_ssq"), "epsc"], writes=[K("b_ssq")])
                    yield
                    S.op("act", lambda e: e.activation(ssq[:, sl:sl + 1], ssq[:, sl:sl + 1], AF.Exp, scale=-0.5),
                         reads=[K("b_ssq")], writes=[K("b_ssq")])
                    yield
                    S.op("dve", lambda e: e.tensor_scalar(yt[:, h * 64:(h + 1) * 64], bank(ob)[:, oc_], ssq[:, sl:sl + 1], None, ALU.mult),
                         reads=[("ps", ob), K("b_ssq")], writes=[ytk])
                    yield

                for j in range(16):
                    yt = ytok[j % 2]
                    ytk = ("b_ytok", 0)
                    for hp in ((0, 1, 2, 3),):
                        alive = [head_chain(j, h, sl, yt, ytk) for sl, h in enumerate(hp)]
                        while alive:
                            for g in list(alive):
                                try:
                                    next(g)
                                except StopIteration:
                                    alive.remove(g)
                    for kc in range(2):
                        bk = 7
                        S.op("pe", lambda e, kc=kc, bk=bk, yt=yt: e.transpose(
                            bankb(bk)[:, 512 + kc * 128:512 + (kc + 1) * 128], yt[:, kc * 128:(kc + 1) * 128], Cb("ident")),
                            reads=[ytk, "cstb"], writes=[("ps", bk)])
                        S.op("dve", lambda e, kc=kc, bk=bk, j=j: e.tensor_scalar(
                            yT[:, kc, j * 128:(j + 1) * 128], bankb(bk)[:, 512 + kc * 128:512 + (kc + 1) * 128], P(l, "b_out_g", kc), None, ALU.mult),
                            reads=[("ps", bk), "pv"], writes=[("yT", kc, j // 4)])
                    bg_step()
                bg_drain()
                S.barrier()
                if l == nl - 1:
                    dump_now("yB", yT[:], [("yT", k_, t_) for k_ in range(2) for t_ in range(4)])
                resid_update(l, 1)
            S.barrier()

        def mixer_C(l):
            with ExitStack() as stk:
                xrp = sb("c_xrp", [128, 3 + S_], F32, stk)
                xc = sb("c_xc", [128, S_], F32, stk)
                t1 = sb("c_t1", [128, S_], F32, stk)
                t2 = sb("c_t2", [128, S_], F32, stk)
                t3 = sb("c_t3", [128, S_], F32, stk)
                xg = sb("c_xg", [128, S_], F32, stk)
                xcb = sb("c_xcb", [128, S_], BF16, stk)
                gw = gw_p
                ltm = sb("c_ltm", [128, 512], F32, stk)
                S.op("pool", lambda e: e.dma_start(out=gw[:], in_=gate_d[l].rearrange("g c k m -> k g c m")),
                     writes=["c_gw"], dma_key="c_gw")
                buf0, key0 = wload([(wsrc(w_in_d[l], 1536, 512, 8), 0)])
                wv0 = buf0[:, 0:4096].rearrange("p (a b) -> p a b", a=8)
                S.op("dve", lambda e: e.memset(xrp[:, 0:3], 0.0), writes=["c_xrp0"])
                for cc in range(2):
                    cl = lvec[:, LV_CL + cc:LV_CL + cc + 1]
                    S.op("act", lambda e, cl=cl, cc=cc: e.activation(cl, P(l, "c_lambda", cc), AF.Exp, scale=-1.0),
                         reads=["pv"], writes=[("lv_cl", cc)])
                    S.op("act", lambda e, cl=cl: e.activation(cl, cl, AF.Ln, bias=epsc[:, 3:4]),
                         reads=[("lv_cl", cc), "epsc"], writes=[("lv_cl", cc)])
                    S.op("dve", lambda e, cl=cl: e.tensor_scalar(cl, cl, -8.0, None, ALU.mult),
                         reads=[("lv_cl", cc)], writes=[("lv_cl", cc)])
                    for tt in range(NT):
                        ts_ = slice(tt * 512, (tt + 1) * 512)
                        proj_fm((wv0, key0, 0), cc * 128, 128, tt, 0)
                        S.op("act", lambda e, tt=tt: e.activation(xrp[:, 3 + tt * 512:3 + (tt + 1) * 512], bank(0), AF.Copy),
                             reads=[("ps", 0)], writes=["c_xrp"])
                        proj_fm((wv0, key0, 0), 256 + cc * 128, 128, tt, 1)
                        S.op("dve", lambda e, ts_=ts_: e.tensor_copy(xg[:, ts_], bank(1)),
                             reads=[("ps", 1)], writes=["c_xg"])
                    cw = lambda j: P(l, "c_conv_w", cc * 4 + j)
                    S.op("dve", lambda e, cc=cc: e.tensor_scalar(xc[:], xrp[:, 3:3 + S_], P(l, "c_conv_w", cc * 4 + 3),
                                                                 P(l, "c_conv_b", cc), ALU.mult, ALU.add),
                         reads=["c_xrp", "c_xrp0", "pv"], writes=["c_xc"])
                    for j in range(3):
                        S.op("dve", lambda e, j=j, cc=cc: e.scalar_tensor_tensor(
                            xc[:], xrp[:, j:j + S_], P(l, "c_conv_w", cc * 4 + j), xc[:], ALU.mult, ALU.add),
                            reads=["c_xrp", "c_xrp0", "c_xc", "pv"], writes=["c_xc"])
                    S.op("act", lambda e: e.activation(xcb[:], xc[:], AF.Copy), reads=["c_xc"], writes=["c_xcb"])
                    for tt in range(NT):
                        ts_ = slice(tt * 512, (tt + 1) * 512)
                        S.op("pe", lambda e, ts_=ts_, cc=cc: e.matmul(bank(2), gw[:, 0, cc, :], xcb[:, ts_], start=True, stop=True),
                             reads=["c_gw", "c_xcb"], writes=[("ps", 2)])
                        S.op("act", lambda e, ts_=ts_, cc=cc: e.activation(t1[:, ts_], bank(2), AF.Sigmoid, bias=P(l, "c_ga_b", cc)),
                             reads=[("ps", 2), "pv"], writes=["c_t1"])
                        S.op("pe", lambda e, ts_=ts_, cc=cc: e.matmul(bank(3), gw[:, 1, cc, :], xcb[:, ts_], start=True, stop=True),
                             reads=["c_gw", "c_xcb"], writes=[("ps", 3)])
                        S.op("act", lambda e, ts_=ts_, cc=cc: e.activation(t2[:, ts_], bank(3), AF.Sigmoid, bias=P(l, "c_gx_b", cc)),
                             reads=[("ps", 3), "pv"], writes=["c_t2"])
                    S.op("act", lambda e, cl=cl: e.activation(t1[:], t1[:], AF.Exp, scale=cl),
                         reads=["c_t1", ("lv_cl", cc)], writes=["c_t1"])
                    S.op("act", lambda e: e.activation(t3[:], t1[:], AF.Square), reads=["c_t1"], writes=["c_t3"])
                    S.op("dve", lambda e: e.tensor_scalar(t3[:], t3[:], -1.0, 1.0, ALU.mult, ALU.add),
                         reads=["c_t3"], writes=["c_t3"])
                    S.op("act", lambda e: e.activation(t3[:], t3[:], AF.Ln), reads=["c_t3"], writes=["c_t3"])
                    S.op("act", lambda e: e.activation(t3[:], t3[:], AF.Exp, scale=0.5), reads=["c_t3"], writes=["c_t3"])
                    S.op("dve", lambda e: e.memset(t3[:, 0:1], 1.0), reads=["c_t3"], writes=["c_t3"])
                    S.op("dve", lambda e: e.tensor_tensor(t2[:], t2[:], t3[:], ALU.mult), reads=["c_t2", "c_t3"], writes=["c_t2"])
                    S.op("dve", lambda e: e.tensor_tensor(t2[:], t2[:], xc[:], ALU.mult), reads=["c_t2", "c_xc"], writes=["c_t2"])
                    S.op("dve", lambda e: e.tensor_tensor_scan(t3[:], t1[:], t2[:], 0.0, ALU.mult, ALU.add),
                         reads=["c_t1", "c_t2", "c_t3"], writes=["c_t3"])
                    S.op("act", lambda e: e.activation(xc[:], xg[:], AF.Square), reads=["c_xg", "c_xc"], writes=["c_xc"])
                    S.op("dve", lambda e: e.tensor_scalar(xc[:], xc[:], 0.044715, 1.0, ALU.mult, ALU.add),
                         reads=["c_xc"], writes=["c_xc"])
                    S.op("dve", lambda e: e.tensor_tensor(xc[:], xc[:], xg[:], ALU.mult), reads=["c_xc", "c_xg"], writes=["c_xc"])
                    S.op("act", lambda e: e.activation(xc[:], xc[:], AF.Sigmoid, scale=1.5957691216057308),
                         reads=["c_xc"], writes=["c_xc"])
                    S.op("dve", lambda e: e.tensor_tensor(xc[:], xc[:], xg[:], ALU.mult), reads=["c_xc", "c_xg"], writes=["c_xc"])
                    S.op("dve", lambda e: e.tensor_tensor(t3[:], t3[:], xc[:], ALU.mult), reads=["c_t3", "c_xc"], writes=["c_t3"])
                    S.op("act", lambda e: e.activation(xcb[:], t3[:], AF.Square), reads=["c_t3", "c_xcb"], writes=["c_xcb"])
                    for tt in range(NT):
                        ts_ = slice(tt * 512, (tt + 1) * 512)
                        bk = 2 + tt % 2
                        S.op("pe", lambda e, ts_=ts_, bk=bk: e.matmul(bank(bk), Cb("blk64m"), xcb[:, ts_], start=True, stop=True),
                             reads=["c_xcb", "cstb"], writes=[("ps", bk)])
                        rstd_from_ms(bank(bk), t1[:, ts_], 0, [("ps", bk), "c_t1"], ["c_t1"], ltm[:])
                        S.op("dve", lambda e, ts_=ts_, cc=cc: e.scalar_tensor_tensor(
                            yT[:, cc, ts_], t3[:, ts_], P(l, "c_out_g", cc), t1[:, ts_], ALU.mult, ALU.mult),
                            reads=["c_t3", "c_t1", "pv"], writes=[("yT", cc, tt)])
                S.barrier()
                if l == nl - 1:
                    dump_now("yC", yT[:], [("yT", k_, t_) for k_ in range(2) for t_ in range(4)])
                resid_update(l, 2)
            S.barrier()


        def mixer_D(l):
            NCHK = 2
            DT = 64 * NCHK
            NDT = S_ // DT
            NS = NCHK * 4
            with ExitStack() as stk:
                T = lambda name, shape, dt=F32: sb("d_" + name, shape, dt, stk)
                pdp = T("pdp", [128, 7, DT + 1])
                pl = T("pl", [128, 7, DT])
                Pst = T("Pst", [64, 4, 64])
                Pb = T("Pb", [64, 4, 64], BF16)
                ptmp = T("ptmp", [64, 4, 64])
                wup, aup, gup, vup, vdn = dsw["wup"], dsw["aup"], dsw["gup"], dsw["vup"], vdn_p
                lrb = T("lrb", [128, DT], BF16)
                vdb = T("vdb", [128, DT], BF16)
                sqb = T("sqb", [128, DT], BF16)
                e2 = T("e2", [128, 2, DT])
                av = T("av", [128, 2, DT])
                gg = T("gg", [128, 2, DT])
                kk = T("kk", [128, 2, DT])
                k2 = T("k2", [128, 2, DT])
                bb = T("bb", [128, 2, DT])
                L = T("L", [128, 2, DT])
                ta = T("ta", [128, 2, DT])
                tb_ = T("tb", [128, 2, DT])
                tc = T("tc", [128, 2, DT])
                bon = T("bon", [128, 2, DT])
                yfm = T("yfm", [128, 2, DT])
                ltm = T("ltm", [128, DT])
                FM = T("FM", [128, 2, NCHK, 4, 64], BF16)
                LOW = T("LOW", [64, 4, NCHK * 256], BF16)
                gCl = T("gCl", [64, 4, NCHK])
                NBH = T("NBH", [128, 2, DT], BF16)
                KH = T("KH", [128, 2, DT], BF16)
                VBf = T("VBf", [128, 2, DT], BF16)
                gC = T("gC", [128, 2, NCHK])
                TM = T("TM", [64, 3, NCHK, 2, 128], BF16)
                SCa = T("SCa", [64, NCHK, 4, 256], BF16)
                XN = [T("XN%d" % i, [64, NS, 128], BF16) for i in range(2)]
                TT = [T("TT%d" % i, [64, NS, 64], BF16) for i in range(2)]
                Wb = T("Wb", [64, 256], BF16)
                Ub = T("Ub", [64, 256], BF16)
                ysq = T("ysq", [64, 256])
                st = T("st", [64, 16])
                ynb = T("ynb", [64, NCHK, 256], BF16)
                dmask = Cb("dmask", 64).rearrange("p (h c) -> p h c", h=4)
                id64 = Cb("id64x4", 64).rearrange("p (h c) -> p h c", h=4)
                ones64 = epsc[:, 3:4].to_broadcast([128, 64])

                def V(eng, fn, r, w):
                    return S.op(eng, fn, reads=r, writes=w)

                for wt, nm in ((wup, "d_wup"), (aup, "d_aup"), (gup, "d_gup"), (vup, "d_vup")):
                    if l == 0:
                        V("dve", lambda e, wt=wt: e.memset(wt[:], 0.0), [], [nm])
                V("dve", lambda e: e.memset(vdb[:], 0.0), [], ["d_vdb"])
                S.op("pool", lambda e: e.dma_start(out=wup[0:32, :], in_=dwup_d[l]), writes=["d_wup"], dma_key="d_w1")
                S.op("pool", lambda e: e.dma_start(out=aup[32:64, :], in_=daup_d[l]), writes=["d_aup"], dma_key="d_w2")
                S.op("pool", lambda e: e.dma_start(out=gup[64:128, :], in_=dgup_d[l]), writes=["d_gup"], dma_key="d_w3")
                if l > 0:
                    S.op("pool", lambda e: e.dma_start(out=vdn[:], in_=dvdn_d[l - 1].rearrange("(c p) r -> p c r", p=128)),
                         writes=["d_vdn"], dma_key="d_w4")
                    S.op("pool", lambda e: e.dma_start(out=vup[0:32, :], in_=dvup_d[l - 1]), writes=["d_vup"], dma_key="d_w5")
                buf4, key4 = wload([(wsrc(w_in_d[l], 2048, 512, 8), 0)])
                wv4 = buf4[:, 0:4096].rearrange("p (a b) -> p a b", a=8)
                buf5, key5 = wload([(wsrc(w_in_d[l], 2560, 384, 8), 0)])
                wv5 = buf5[:, 0:3072].rearrange("p (a b) -> p a b", a=8)
                V("dve", lambda e: e.memset(Pst[:], 0.0), [], ["d_Pst"])
                V("dve", lambda e: e.memset(Pb[:], 0.0), [], ["d_Pb"])
                V("dve", lambda e: e.tensor_scalar(lvec[:, LV_NW0:LV_NW0 + 2], P(l, "d_w0", 0, 2), -1.0, None, ALU.mult),
                  ["pv"], ["lv_nw0"])
                nb_ = [0]
                def nbank():
                    nb_[0] += 1
                    return 2 + nb_[0] % 2

                for dt in range(NDT):
                    t0 = dt * DT
                    tcs = slice(t0, t0 + DT)
                    hk = ("hT", t0 // 512)
                    if dt == 0:
                        V("dve", lambda e: e.memset(pdp[:, :, 0:1], 0.0), [], ["d_pdp0"])
                    else:
                        V("dve", lambda e: e.tensor_copy(pdp[:, :, 0:1], pdp[:, :, DT:DT + 1]), ["d_pdp"], ["d_pdp0"])
                    for c in range(7):
                        wv, key, coff = (wv4, key4, c * 128) if c < 4 else (wv5, key5, (c - 4) * 128)
                        bk = c % 2
                        for kc in range(8):
                            S.op("pe", lambda e, kc=kc, wv=wv, coff=coff, bk=bk, tcs=tcs: e.matmul(
                                bank(bk)[:, 0:DT], wv[:, kc, coff:coff + 128], hT[:, kc, tcs],
                                start=(kc == 0), stop=(kc == 7)),
                                reads=[key, hk], writes=[("ps", bk)])
                        V("act", lambda e, c=c, bk=bk: e.activation(pdp[:, c, 1:DT + 1], bank(bk)[:, 0:DT], AF.Copy),
                          [("ps", bk), "d_pdp0"], ["d_pdp"])
                    if D_STOP < 2:
                        continue
                    mu_b = P(l, "d_mu", 0, 7).unsqueeze(2).to_broadcast([128, 7, DT])
                    V("dve", lambda e: e.tensor_tensor(pl[:], pdp[:, :, 0:DT], pdp[:, :, 1:DT + 1], ALU.subtract),
                      ["d_pdp", "d_pdp0"], ["d_pl"])
                    V("dve", lambda e, mu_b=mu_b: e.tensor_tensor(pl[:], pl[:], mu_b, ALU.mult), ["d_pl", "pv"], ["d_pl"])
                    V("dve", lambda e: e.tensor_tensor(pl[:], pl[:], pdp[:, :, 1:DT + 1], ALU.add), ["d_pl", "d_pdp"], ["d_pl"])
                    if D_STOP < 3:
                        continue
                    V("act", lambda e: e.activation(lrb[0:32, :], pl[0:32, 6, :], AF.Tanh), ["d_pl"], ["d_lrb"])
                    V("act", lambda e: e.activation(lrb[32:64, :], pl[32:64, 6, :], AF.Copy), ["d_pl"], ["d_lrb"])
                    V("act", lambda e: e.activation(lrb[64:128, :], pl[64:128, 6, :], AF.Sigmoid), ["d_pl"], ["d_lrb"])
                    for oc in range(2):
                        ocs = slice(oc * 128, (oc + 1) * 128)
                        bk = nbank()
                        V("pe", lambda e, bk=bk, ocs=ocs: e.matmul(bank(bk)[:, 0:DT], wup[:, ocs], lrb[:, :], start=True, stop=True),
                          ["d_wup", "d_lrb"], [("ps", bk)])
                        V("act", lambda e, bk=bk, oc=oc: e.activation(ta[:, oc, :], bank(bk)[:, 0:DT], AF.Exp, scale=-1.0,
                                                                      bias=lvec[:, LV_NW0 + oc:LV_NW0 + oc + 1]),
                          [("ps", bk), "lv_nw0"], ["d_ta"])
                        V("act", lambda e, oc=oc: e.activation(ta[:, oc, :], ta[:, oc, :], AF.Ln, bias=epsc[:, 3:4]),
                          ["d_ta", "epsc"], ["d_ta"])
                        V("act", lambda e, oc=oc: e.activation(e2[:, oc, :], ta[:, oc, :], AF.Exp, scale=-1.0, bias=epsc[:, 4:5]),
                          ["d_ta", "epsc"], ["d_e2"])
                        bk = nbank()
                        V("pe", lambda e, bk=bk, ocs=ocs: e.matmul(bank(bk)[:, 0:DT], aup[:, ocs], lrb[:, :], start=True, stop=True),
                          ["d_aup", "d_lrb"], [("ps", bk)])
                        V("act", lambda e, bk=bk, oc=oc: e.activation(av[:, oc, :], bank(bk)[:, 0:DT], AF.Sigmoid, bias=P(l, "d_a0", oc)),
                          [("ps", bk), "pv"], ["d_av"])
                        bk = nbank()
                        V("pe", lambda e, bk=bk, ocs=ocs: e.matmul(bank(bk)[:, 0:DT], gup[:, ocs], lrb[:, :], start=True, stop=True),
                          ["d_gup", "d_lrb"], [("ps", bk)])
                        V("act", lambda e, bk=bk, oc=oc: e.activation(gg[:, oc, :], bank(bk)[:, 0:DT], AF.Copy),
                          [("ps", bk)], ["d_gg"])
                    if D_STOP < 5:
                        continue
                    if l > 0:
                        V("act", lambda e: e.activation(VBf[:], pl[:, 4:6, :], AF.Copy), ["d_pl"], ["d_VBf"])
                        for kc in range(2):
                            V("pe", lambda e, kc=kc: e.matmul(bank(4)[0:32, 0:DT], vdn[:, kc, :], VBf[:, kc, :],
                                                              start=(kc == 0), stop=(kc == 1)),
                              ["d_vdn", "d_VBf"], [("ps", 4)])
                        V("dve", lambda e: e.tensor_copy(vdb[0:32, :], bank(4)[0:32, 0:DT]), [("ps", 4)], ["d_vdb"])
                        for oc in range(2):
                            ocs = slice(oc * 128, (oc + 1) * 128)
                            bk = nbank()
                            V("pe", lambda e, bk=bk, ocs=ocs: e.matmul(bank(bk)[:, 0:DT], vup[:, ocs], vdb[:, :], start=True, stop=True),
                              ["d_vup", "d_vdb"], [("ps", bk)])
                            V("act", lambda e, bk=bk, oc=oc: e.activation(tc[:, oc, :], bank(bk)[:, 0:DT], AF.Sigmoid, bias=P(l, "d_v0", oc)),
                              [("ps", bk), "pv"], ["d_tc"])
                            V("dve", lambda e, oc=oc, tcs=tcs: e.tensor_tensor(ta[:, oc, :], vfirst[:, oc, tcs], pl[:, 4 + oc, :], ALU.subtract),
                              ["vfirst", "d_pl"], ["d_ta"])
                            V("dve", lambda e, oc=oc: e.tensor_tensor(ta[:, oc, :], ta[:, oc, :], tc[:, oc, :], ALU.mult),
                              ["d_ta", "d_tc"], ["d_ta"])
                            V("dve", lambda e, oc=oc: e.tensor_tensor(pl[:, 4 + oc, :], pl[:, 4 + oc, :], ta[:, oc, :], ALU.add),
                              ["d_pl", "d_ta"], ["d_pl"])
                    else:
                        V("act", lambda e, tcs=tcs: e.activation(vfirst[:, :, tcs], pl[:, 4:6, :], AF.Copy), ["d_pl"], ["vfirst"])
                    if D_STOP < 6:
                        continue
                    kkk_b = P(l, "d_k_k", 0, 2).unsqueeze(2).to_broadcast([128, 2, DT])
                    kka_b = P(l, "d_k_a", 0, 2).unsqueeze(2).to_broadcast([128, 2, DT])
                    rk_b = P(l, "d_r_k", 0, 2).unsqueeze(2).to_broadcast([128, 2, DT])
                    rr, kraw, vv = pl[:, 0:2, :], pl[:, 2:4, :], pl[:, 4:6, :]
                    V("dve", lambda e: e.tensor_tensor(ta[:], kraw, kkk_b, ALU.mult), ["d_pl", "pv"], ["d_ta"])
                    for oc in range(2):
                        bk = nbank()
                        V("act", lambda e, oc=oc: e.activation(sqb[:], ta[:, oc, :], AF.Square), ["d_ta"], ["d_sqb"])
                        V("pe", lambda e, bk=bk: e.matmul(bank(bk)[:, 0:DT], Cb("blk64s"), sqb[:], start=True, stop=True),
                          ["d_sqb", "cstb"], [("ps", bk)])
                        rstd_from_ms(bank(bk)[:, 0:DT], tb_[:, oc, :], 2, [("ps", bk), "d_tb"], ["d_tb"], ltm[:])
                    V("dve", lambda e: e.tensor_tensor(kk[:], ta[:], tb_[:], ALU.mult), ["d_ta", "d_tb"], ["d_kk"])
                    V("dve", lambda e: e.scalar_tensor_tensor(ta[:], av[:], -1.0, kka_b, ALU.add, ALU.mult),
                      ["d_av", "pv", "d_ta"], ["d_ta"])
                    V("dve", lambda e: e.scalar_tensor_tensor(k2[:], ta[:], 1.0, kraw, ALU.add, ALU.mult),
                      ["d_ta", "d_pl"], ["d_k2"])
                    V("dve", lambda e: e.tensor_tensor(bb[:], kk[:], av[:], ALU.mult), ["d_kk", "d_av"], ["d_bb"])
                    for oc in range(2):
                        for ch in range(NCHK):
                            cs = slice(ch * 64, (ch + 1) * 64)
                            V("dve", lambda e, oc=oc, cs=cs: e.tensor_tensor_scan(
                                L[:, oc, cs], ones64, e2[:, oc, cs], 0.0, ALU.mult, ALU.subtract),
                              ["d_e2", "epsc"], ["d_L"])
                    V("dve", lambda e: e.tensor_tensor(ta[:], L[:], e2[:], ALU.add), ["d_L", "d_e2", "d_ta"], ["d_ta"])
                    V("act", lambda e: e.activation(ta[:], ta[:], AF.Exp), ["d_ta"], ["d_ta"])
                    c4 = lambda ap: ap.rearrange("p o (c t) -> p o c t", t=64)
                    V("dve", lambda e: e.tensor_tensor(FM[:, :, :, 0, :], c4(kk[:]), c4(ta[:]), ALU.mult),
                      ["d_kk", "d_ta"], ["d_FM"])
                    V("act", lambda e: e.activation(tb_[:], L[:], AF.Exp), ["d_L", "d_tb"], ["d_tb"])
                    V("dve", lambda e: e.tensor_tensor(FM[:, :, :, 1, :], c4(pl[:, 0:2, :]), c4(tb_[:]), ALU.mult),
                      ["d_pl", "d_tb"], ["d_FM"])
                    for oc in range(2):
                        V("dve", lambda e, oc=oc: e.tensor_copy(
                            gC[:, oc, :], tb_[:, oc, :].rearrange("p (c t) -> p c t", t=64)[:, :, 63]),
                          ["d_tb"], ["d_gC"])
                    V("act", lambda e: e.activation(tc[:], L[:], AF.Exp, scale=-1.0), ["d_L", "d_tc"], ["d_tc"])
                    V("dve", lambda e: e.tensor_tensor(FM[:, :, :, 2, :], c4(bb[:]), c4(tc[:]), ALU.mult), ["d_bb", "d_tc"], ["d_FM"])
                    V("dve", lambda e: e.tensor_tensor(FM[:, :, :, 3, :], c4(k2[:]), c4(tc[:]), ALU.mult), ["d_k2", "d_tc"], ["d_FM"])
                    for oc in range(2):
                        for ch in range(NCHK):
                            cs = slice(ch * 64, (ch + 1) * 64)
                            V("act", lambda e, oc=oc, cs=cs, ch=ch: e.activation(
                                tc[:, oc, cs], L[:, oc, cs], AF.Exp, scale=-1.0, bias=L[:, oc, ch * 64 + 63:ch * 64 + 64]),
                              ["d_L", "d_tc", "d_FM"], ["d_tc"])
                    V("dve", lambda e: e.scalar_tensor_tensor(NBH[:], bb[:], -1.0, tc[:], ALU.mult, ALU.mult),
                      ["d_bb", "d_tc"], ["d_NBH"])
                    V("dve", lambda e: e.tensor_tensor(KH[:], k2[:], tc[:], ALU.mult), ["d_k2", "d_tc"], ["d_KH"])
                    V("act", lambda e: e.activation(VBf[:], vv, AF.Copy), ["d_pl"], ["d_VBf"])
                    V("dve", lambda e: e.tensor_tensor(ta[:], rr, rk_b, ALU.mult), ["d_pl", "pv", "d_ta"], ["d_ta"])
                    V("dve", lambda e: e.tensor_tensor(ta[:], ta[:], k2[:], ALU.mult), ["d_ta", "d_k2"], ["d_ta"])
                    for oc in range(2):
                        bk = nbank()
                        V("act", lambda e, oc=oc: e.activation(sqb[:], ta[:, oc, :], AF.Copy), ["d_ta"], ["d_sqb"])
                        V("pe", lambda e, bk=bk: e.matmul(bank(bk)[:, 0:DT], Cb("blk64s"), sqb[:], start=True, stop=True),
                          ["d_sqb", "cstb"], [("ps", bk)])
                        V("dve", lambda e, bk=bk, oc=oc: e.tensor_tensor(bon[:, oc, :], bank(bk)[:, 0:DT], pl[:, 4 + oc, :], ALU.mult),
                          [("ps", bk), "d_pl"], ["d_bon"])
                    if D_STOP < 7:
                        continue
                    for xi, Xt in enumerate((VBf, NBH, KH)):
                        bk = 5 + xi
                        for ch in range(NCHK):
                            for oc in range(2):
                                slot = ch * 2 + oc
                                V("pe", lambda e, Xt=Xt, ch=ch, oc=oc, slot=slot, bk=bk: e.matmul(
                                    bank(bk)[0:64, slot * 128:(slot + 1) * 128], Xt[:, oc, ch * 64:(ch + 1) * 64], Cb("ident"),
                                    start=True, stop=True),
                                  ["d_VBf", "d_NBH", "d_KH", "cstb"], [("ps", bk)])
                        V("act" if xi != 1 else "dve",
                          (lambda e, xi=xi, bk=bk: e.activation(TM[:, xi].rearrange("p c o f -> p (c o f)"), bank(bk)[0:64, 0:NCHK * 256], AF.Copy))
                          if xi != 1 else
                          (lambda e, xi=xi, bk=bk: e.tensor_copy(TM[:, xi].rearrange("p c o f -> p (c o f)"), bank(bk)[0:64, 0:NCHK * 256])),
                          [("ps", bk)], ["d_TM"])
                    if D_STOP < 8:
                        continue
                    for oc in range(2):
                        for hh in range(2):
                            hd = oc * 2 + hh
                            bk = 3 + hd % 2
                            V("pe", lambda e, oc=oc, hh=hh, bk=bk: e.matmul(
                                bank(bk)[0:64, 0:NCHK * 256], Cb("ident")[:, hh * 64:(hh + 1) * 64],
                                FM[:, oc].rearrange("p c q t -> p (c q t)"), start=True, stop=True),
                              ["d_FM", "cstb"], [("ps", bk)])
                            if hd % 2 == 0:
                                V("act", lambda e, hd=hd, bk=bk: e.activation(LOW[:, hd, :], bank(bk)[0:64, 0:NCHK * 256], AF.Copy),
                                  [("ps", bk)], ["d_LOW"])
                            else:
                                V("dve", lambda e, hd=hd, bk=bk: e.tensor_copy(LOW[:, hd, :], bank(bk)[0:64, 0:NCHK * 256]),
                                  [("ps", bk)], ["d_LOW"])
                    for hh in range(2):
                        V("pe", lambda e, hh=hh: e.matmul(
                            bank(5)[0:64, hh * 2 * NCHK:(hh + 1) * 2 * NCHK], C("ident")[:, hh * 64:(hh + 1) * 64],
                            gC[:].rearrange("p o c -> p (o c)"), start=True, stop=True),
                          ["d_gC", "cst"], [("ps", 5)])
                    V("dve", lambda e: e.tensor_copy(
                        gCl[:].rearrange("p (o h) c -> p h o c", o=2),
                        bank(5)[0:64, 0:4 * NCHK].rearrange("p (h o c) -> p h o c", h=2, o=2)),
                      [("ps", 5)], ["d_gCl"])
                    for ch in range(NCHK):
                        cb = ch * 256
                        for hd in range(4):
                            sa = hd // 2
                            co = (hd % 2) * 256
                            V("pe", lambda e, sa=sa, co=co, hd=hd, cb=cb: e.matmul(
                                bank(sa)[0:64, co:co + 128], LOW[:, hd, cb + 128:cb + 192], LOW[:, hd, cb:cb + 128],
                                start=True, stop=True),
                              ["d_LOW"], [("ps", sa)])
                            V("pe", lambda e, sa=sa, co=co, hd=hd, cb=cb: e.matmul(
                                bank(sa)[0:64, co + 128:co + 256], LOW[:, hd, cb + 192:cb + 256], LOW[:, hd, cb:cb + 128],
                                start=True, stop=True),
                              ["d_LOW"], [("ps", sa)])
                            V("pe", lambda e, hd=hd, cb=cb: e.matmul(
                                bank(2)[0:64, hd * 64:(hd + 1) * 64], LOW[:, hd, cb:cb + 64], LOW[:, hd, cb + 128:cb + 192],
                                start=True, stop=True),
                              ["d_LOW"], [("ps", 2)])
                        for sa in range(2):
                            V("dve", lambda e, sa=sa, ch=ch: e.tensor_tensor(
                                SCa[:, ch, 2 * sa:2 * sa + 2, :], bank(sa)[0:64, :].rearrange("p (h c) -> p h c", h=2),
                                dmask[:, 2 * sa:2 * sa + 2, 0:256], ALU.mult),
                              [("ps", sa), "cstb"], ["d_SCa"])
                        V("dve", lambda e, ch=ch: e.tensor_tensor(
                            XN[0][:, ch * 4:(ch + 1) * 4, 64:128], bank(2)[0:64, 0:256].rearrange("p (h c) -> p h c", h=4),
                            dmask[:, :, 256:320], ALU.mult),
                          [("ps", 2), "cstb"], [("d_XN", 0)])
                        V("act", lambda e, ch=ch: e.activation(XN[0][:, ch * 4:(ch + 1) * 4, 0:64], SCa[:, ch, :, 0:64], AF.Copy),
                          ["d_SCa"], [("d_XN", 0)])
                        V("dve", lambda e, ch=ch: e.tensor_tensor(TT[0][:, ch * 4:(ch + 1) * 4, :], SCa[:, ch, :, 0:64], id64, ALU.add),
                          ["d_SCa", "cstb"], [("d_TT", 0)])
                    if D_STOP < 9:
                        continue
                    cur = 0
                    for lev in range(5):
                        nxt = 1 - cur
                        for s_ in range(NS):
                            bkx = 3 + s_ // 4
                            c0 = (s_ % 4) * 128
                            V("pe", lambda e, s_=s_, bkx=bkx, c0=c0, cur=cur: e.matmul(
                                bank(bkx)[0:64, c0:c0 + 64], XN[cur][:, s_, 64:128], XN[cur][:, s_, 0:64], start=True, stop=True),
                              [("d_XN", cur)], [("ps", bkx)])
                            V("pe", lambda e, s_=s_, bkx=bkx, c0=c0, cur=cur: e.matmul(
                                bank(bkx)[0:64, c0 + 64:c0 + 128], XN[cur][:, s_, 0:64], XN[cur][:, s_, 64:128], start=True, stop=True),
                              [("d_XN", cur)], [("ps", bkx)])
                        for g_ in range(NS // 4):
                            V("act" if g_ % 2 == 0 else "dve",
                              (lambda e, g_=g_, nxt=nxt: e.activation(
                                  XN[nxt][:, g_ * 4:(g_ + 1) * 4, :], bank(3 + g_)[0:64, :].rearrange("p (s c) -> p s c", s=4), AF.Copy))
                              if g_ % 2 == 0 else
                              (lambda e, g_=g_, nxt=nxt: e.tensor_copy(
                                  XN[nxt][:, g_ * 4:(g_ + 1) * 4, :], bank(3 + g_)[0:64, :].rearrange("p (s c) -> p s c", s=4))),
                              [("ps", 3 + g_)], [("d_XN", nxt)])
                        for s_ in range(NS):
                            V("pe", lambda e, s_=s_, cur=cur, nxt=nxt: e.matmul(
                                bank(5)[0:64, s_ * 64:(s_ + 1) * 64], XN[nxt][:, s_, 64:128], TT[cur][:, s_, :], start=True, stop=True),
                              [("d_XN", nxt), ("d_TT", cur)], [("ps", 5)])
                        V("dve", lambda e, cur=cur, nxt=nxt: e.tensor_tensor(
                            TT[nxt][:], bank(5)[0:64, 0:NS * 64].rearrange("p (s c) -> p s c", s=NS), TT[cur][:], ALU.add),
                          [("ps", 5), ("d_TT", cur)], [("d_TT", nxt)])
                        cur = nxt
                    TTf = TT[cur]
                    ttk = ("d_TT", cur)
                    if D_STOP < 10:
                        continue
                    for ch in range(NCHK):
                        cb = ch * 256
                        Wp, Up, Yp, Pp = bank(6)[0:64, 0:256], bank(7)[0:64, 0:256], bank(0)[0:64, 0:256], bank(1)[0:64, 0:256]
                        for hd in range(4):
                            oc, hh = hd // 2, hd % 2
                            hs = slice(hd * 64, (hd + 1) * 64)
                            vtok = TM[:, 0, ch, oc, hh * 64:(hh + 1) * 64]
                            V("pe", lambda e, hs=hs, hd=hd, cb=cb: e.matmul(
                                Wp[:, hs], LOW[:, hd, cb:cb + 64], Pb[:, hd, :], start=True, stop=False),
                              ["d_LOW", "d_Pb"], [("ps", 6)])
                            V("pe", lambda e, hs=hs, ch=ch, hd=hd, vtok=vtok: e.matmul(
                                Wp[:, hs], SCa[:, ch, hd, 128:192], vtok, start=False, stop=True),
                              ["d_SCa", "d_TM"], [("ps", 6)])
                        V("dve", lambda e: e.tensor_copy(Wb[:], bank(6)[0:64, 0:256]), [("ps", 6)], ["d_Wb"])
                        for hd in range(4):
                            hs = slice(hd * 64, (hd + 1) * 64)
                            V("pe", lambda e, hs=hs, hd=hd, ch=ch: e.matmul(
                                Up[:, hs], TTf[:, ch * 4 + hd, :], Wb[:, hs], start=True, stop=True),
                              [ttk, "d_Wb"], [("ps", 7)])
                        V("act", lambda e: e.activation(Ub[:], bank(7)[0:64, 0:256], AF.Copy), [("ps", 7)], ["d_Ub"])
                        for hd in range(4):
                            oc, hh = hd // 2, hd % 2
                            hs = slice(hd * 64, (hd + 1) * 64)
                            vtok = TM[:, 0, ch, oc, hh * 64:(hh + 1) * 64]
                            V("pe", lambda e, hs=hs, hd=hd, cb=cb: e.matmul(
                                Yp[:, hs], LOW[:, hd, cb + 64:cb + 128], Pb[:, hd, :], start=True, stop=False),
                              ["d_LOW", "d_Pb"], [("ps", 0)])
                            V("pe", lambda e, hs=hs, ch=ch, hd=hd, vtok=vtok: e.matmul(
                                Yp[:, hs], SCa[:, ch, hd, 192:256], vtok, start=False, stop=False),
                              ["d_SCa", "d_TM"], [("ps", 0)])
                            V("pe", lambda e, hs=hs, ch=ch, hd=hd: e.matmul(
                                Yp[:, hs], SCa[:, ch, hd, 64:128], Ub[:, hs], start=False, stop=True),
                              ["d_SCa", "d_Ub"], [("ps", 0)])
                        for hd in range(4):
                            oc, hh = hd // 2, hd % 2
                            hs = slice(hd * 64, (hd + 1) * 64)
                            vtok = TM[:, 0, ch, oc, hh * 64:(hh + 1) * 64]
                            V("pe", lambda e, hs=hs, ch=ch, oc=oc, hh=hh: e.matmul(
                                Pp[:, hs], TM[:, 1, ch, oc, hh * 64:(hh + 1) * 64], Ub[:, hs], start=True, stop=False),
                              ["d_TM", "d_Ub"], [("ps", 1)])
                            V("pe", lambda e, hs=hs, ch=ch, oc=oc, hh=hh, vtok=vtok: e.matmul(
                                Pp[:, hs], TM[:, 2, ch, oc, hh * 64:(hh + 1) * 64], vtok, start=False, stop=True),
                              ["d_TM"], [("ps", 1)])
                        V("dve", lambda e, ch=ch: e.tensor_tensor(
                            ptmp[:], Pst[:], gCl[:, :, ch].unsqueeze(2).to_broadcast([64, 4, 64]), ALU.mult),
                          ["d_Pst", "d_gCl"], ["d_ptmp"])
                        V("dve", lambda e: e.tensor_tensor(
                            Pst[:], ptmp[:], bank(1)[0:64, 0:256].rearrange("p (h c) -> p h c", h=4), ALU.add),
                          ["d_ptmp", ("ps", 1)], ["d_Pst"])
                        V("act", lambda e: e.activation(Pb[:], Pst[:], AF.Copy), ["d_Pst"], ["d_Pb"])
                        yv = bank(0)[0:64, 0:256].rearrange("p (h c) -> p h c", h=4)
                        V("dve", lambda e, yv=yv: e.tensor_reduce(st[:, 0:4], yv, AX.X, ALU.add), [("ps", 0)], ["d_st"])
                        V("act", lambda e: e.activation(ysq[:], bank(0)[0:64, 0:256], AF.Square), [("ps", 0)], ["d_ysq"])
                        V("dve", lambda e: e.tensor_reduce(st[:, 4:8], ysq[:].rearrange("p (h c) -> p h c", h=4), AX.X, ALU.add),
                          ["d_ysq", "d_st"], ["d_st"])
                        V("dve", lambda e: e.tensor_scalar(st[:, 8:12], st[:, 0:4], 1.0 / 64.0, None, ALU.mult), ["d_st"], ["d_st"])
                        V("dve", lambda e: e.tensor_tensor(st[:, 12:16], st[:, 8:12], st[:, 8:12], ALU.mult), ["d_st"], ["d_st"])
                        V("dve", lambda e: e.scalar_tensor_tensor(st[:, 4:8], st[:, 4:8], 1.0 / 64.0, st[:, 12:16], ALU.mult, ALU.subtract),
                          ["d_st"], ["d_st"])
                        V("act", lambda e: e.activation(st[:, 4:8], st[:, 4:8], AF.Ln, bias=epsc[0:64, 1:2]), ["d_st", "epsc"], ["d_st"])
                        V("act", lambda e: e.activation(st[:, 4:8], st[:, 4:8], AF.Exp, scale=-0.5), ["d_st"], ["d_st"])
                        V("dve", lambda e, yv=yv: e.tensor_tensor(
                            ysq[:].rearrange("p (h c) -> p h c", h=4), yv, st[:, 8:12].unsqueeze(2).to_broadcast([64, 4, 64]), ALU.subtract),
                          [("ps", 0), "d_st", "d_ysq"], ["d_ysq"])
                        V("dve", lambda e, ch=ch: e.tensor_tensor(
                            ynb[:, ch, :].rearrange("p (h c) -> p h c", h=4), ysq[:].rearrange("p (h c) -> p h c", h=4),
                            st[:, 4:8].unsqueeze(2).to_broadcast([64, 4, 64]), ALU.mult),
                          ["d_ysq", "d_st"], ["d_ynb"])
                    if D_STOP < 11:
                        continue
                    for oc in range(2):
                        for ch in range(NCHK):
                            V("pe", lambda e, oc=oc, ch=ch: e.matmul(
                                bank(2)[:, oc * DT + ch * 64:oc * DT + (ch + 1) * 64], ynb[:, ch, oc * 128:(oc + 1) * 128],
                                Cb("ident", 64)[:, 0:64], start=True, stop=True),
                              ["d_ynb", "cstb"], [("ps", 2)])
                        V("dve", lambda e, oc=oc: e.tensor_scalar(yfm[:, oc, :], bank(2)[:, oc * DT:(oc + 1) * DT],
                                                                  P(l, "d_lnx_w", oc), P(l, "d_lnx_b", oc), ALU.mult, ALU.add),
                          [("ps", 2), "pv"], ["d_yfm"])
                    V("dve", lambda e: e.tensor_tensor(yfm[:], yfm[:], bon[:], ALU.add), ["d_yfm", "d_bon"], ["d_yfm"])
                    V("dve", lambda e, tcs=tcs: e.tensor_tensor(yT[:, :, tcs], yfm[:], gg[:], ALU.mult),
                      ["d_yfm", "d_gg"], [("yT", 0, t0 // 512), ("yT", 1, t0 // 512)])
                S.barrier()
                if l == nl - 1:
                    dump_now("yD", yT[:], [("yT", k_, t_) for k_ in range(2) for t_ in range(4)])
                resid_update(l, 3)
            S.barrier()


        def ffn(l):
            with ExitStack() as stk:
                adanorm(l, 2, stk)
            S.barrier()
            with ExitStack() as stk:
                actT = sb("f_act", [128, 22, 512], BF16, stk)
                yg = [sb("f_yg%d" % i, [128, 512], F32, stk) for i in range(2)]
                yv = [sb("f_yv%d" % i, [128, 512], F32, stk) for i in range(2)]
                sg = [sb("f_sg%d" % i, [128, 512], F32, stk) for i in range(2)]
                for hb in (0, 2, 4, 6):
                    S.op("dve", lambda e, hb=hb: e.memset(bank(hb)[:, 510:512], 0.0), writes=[("ps", hb)])
                ffu = ffu_d[l]
                ffd = ffd_d[l].rearrange("(kc p) n -> p kc n", p=128)
                for tt in range(NT):
                    ts_ = slice(tt * 512, (tt + 1) * 512)
                    for f in range(22):
                        if f % 2 == 0:
                            buf, key = wload([(wsrc(ffu, f * 128, 256, 8), 0), (wsrc(ffu, D_FF + f * 128, 256, 8), 2048)])
                            wg = buf[:, 0:2048].rearrange("p (a b) -> p a b", a=8)
                            wv_ = buf[:, 2048:4096].rearrange("p (a b) -> p a b", a=8)
                        par = f % 2
                        co = (f % 2) * 128
                        for gv in range(2):
                            wv = wg if gv == 0 else wv_
                            hb = par * 4 + gv * 2
                            mb = hb + 1
                            for kc in range(8):
                                S.op("pe", lambda e, kc=kc, wv=wv, co=co, mb=mb, ts_=ts_: e.matmul(
                                    bank(mb), wv[:, kc, co:co + 128], hT[:, kc, ts_], start=(kc == 0), stop=(kc == 7)),
                                    reads=[key, ("hT", tt)], writes=[("ps", mb)])
                            if tt > 0:
                                for kc in range(8):
                                    S.op("pe", lambda e, kc=kc, wv=wv, co=co, hb=hb, tt=tt: e.matmul(
                                        bank(hb)[:, 510:512], wv[:, kc, co:co + 128], hT[:, kc, tt * 512 - 2:tt * 512],
                                        start=(kc == 0), stop=(kc == 7)),
                                        reads=[key, ("hT", tt - 1)], writes=[("ps", hb)])
                            chn = f if gv == 0 else 22 + f
                            u0 = hb * 512 + 510
                            yb = (yg if gv == 0 else yv)[par]
                            ybk = ("f_y", gv, par)
                            S.op("act", lambda e, yb=yb, u0=u0, chn=chn: e.activation(
                                yb[:], PS[:, u0 + 2:u0 + 514], AF.Identity, bias=P(l, "ff_conv_b", chn),
                                scale=P(l, "ff_conv_w", chn * 3 + 2)),
                                reads=[("ps", hb), ("ps", mb), "pv"], writes=[ybk])
                            for tap in (1, 0):
                                S.op("dve", lambda e, yb=yb, u0=u0, chn=chn, tap=tap: e.scalar_tensor_tensor(
                                    yb[:], PS[:, u0 + tap:u0 + tap + 512], P(l, "ff_conv_w", chn * 3 + tap), yb[:],
                                    ALU.mult, ALU.add),
                                    reads=[("ps", hb), ("ps", mb), "pv", ybk], writes=[ybk])
                        S.op("act", lambda e, par=par: e.activation(sg[par][:], yg[par][:], AF.Silu),
                             reads=[("f_y", 0, par)], writes=[("f_sg", par)])
                        S.op("dve", lambda e, par=par, f=f: e.tensor_tensor(actT[:, f, :], sg[par][:], yv[par][:], ALU.mult),
                             reads=[("f_sg", par), ("f_y", 1, par)], writes=[("f_act", f)])
                    for oc in range(8):
                        buf, key = wload([(ffd[:, :, oc * 128:(oc + 1) * 128], 0)])
                        wd = buf[:, 0:22 * 128].rearrange("p (a b) -> p a b", a=22)
                        bk = 1 + 2 * (oc % 4)
                        for kc in range(22):
                            S.op("pe", lambda e, kc=kc, wd=wd, bk=bk: e.matmul(
                                bank(bk), wd[:, kc, :], actT[:, kc, :], start=(kc == 0), stop=(kc == 21)),
                                reads=[key, ("f_act", kc)], writes=[("ps", bk)])
                        S.op("dve", lambda e, oc=oc, bk=bk, ts_=ts_: e.scalar_tensor_tensor(
                            xT[:, oc, ts_], bank(bk), MT(l)[:, 40 + oc:41 + oc], xT[:, oc, ts_], ALU.mult, ALU.add),
                            reads=[("ps", bk), ("modT", l % 2), ("xT", tt)], writes=[("xT", tt)])
            S.barrier()

        for l in range(nl):
            S.mark("L%d_mod" % l)
            if l == 0:
                for _ in layer_mod_gen(0, bk=0):
                    pass
            bg_drain()
            if l + 1 < nl:
                bgs["g"] = layer_mod_gen(l + 1)
            with ExitStack() as stk:
                adanorm(l, 1, stk)
            S.barrier()
            if l == nl - 1:
                dump_now("modT", MT(l)[:], [("modT", l % 2)])
                dump_now("hT", hT[:], [("hT", t_) for t_ in range(4)])
            if "A" in stages:
                S.mark("L%d_A" % l)
                mixer_A(l)
            if "B" in stages:
                S.mark("L%d_B" % l)
                mixer_B(l)
            bg_drain()
            if "C" in stages:
                S.mark("L%d_C" % l)
                mixer_C(l)
            if "D" in stages:
                S.mark("L%d_D" % l)
                mixer_D(l)
            if "F" in stages:
                S.mark("L%d_F" % l)
                ffn(l)

        S.mark("final")
        fin_ops = []
        with ExitStack() as stf:
            xo = [sb("xo%d" % i, [128, D_], F32, stf) for i in range(2)]
            for tb in range(16):
                xi = xo[tb % 2]
                for g in range(2):
                    bk = (tb * 2 + g) % 8
                    for cc in range(4):
                        c = g * 4 + cc
                        S.op("pe", lambda e, c=c, cc=cc, bk=bk, tb=tb: e.transpose(
                            bank(bk, 128, cc * 128), xT[:, c, tb * 128:(tb + 1) * 128], C("ident")),
                            reads=[("xT", tb // 4), "cst"], writes=[("ps", bk)])
                    if g == 0:
                        S.op("dve", lambda e, xi=xi, bk=bk: e.tensor_copy(xi[:, 0:512], bank(bk)),
                             reads=[("ps", bk)], writes=[("xo", tb % 2)])
                    else:
                        S.op("act", lambda e, xi=xi, bk=bk: e.activation(xi[:, 512:1024], bank(bk), AF.Copy),
                             reads=[("ps", bk)], writes=[("xo", tb % 2)])
                fin_ops.append(S.op("sp", lambda e, xi=xi, tb=tb: e.dma_start(
                    out=out_d[tb * 128:(tb + 1) * 128, :], in_=xi[:]),
                    reads=[("xo", tb % 2)], dma_key=("xo", tb % 2)))
        S.emit(final_wait_ops=fin_ops + dump_ops)
    nc._sched_stats = (S.ninst, S.nwaits)
    nc._marks = getattr(S, 'marks', [])
    return nc


def prep_shared(inp):
    f = lambda k: np.asarray(inp[k], np.float32)
    pvs = np.zeros((128, NL * NCOL), np.float32)
    for l in range(NL):
        def put(name, arr, j=0):
            o, k = PCOLS[name]
            arr = np.asarray(arr, np.float32)
            pvs[:arr.shape[0], l * NCOL + o + j: l * NCOL + o + j + arr.shape[1]] = arr
        put("norm1_g", _fm(f("norm1_g")[l]))
        put("norm2_g", _fm(f("norm2_g")[l]))
        put("b_ada", _fm(f("b_ada")[l]))
        gq = np.tile(f("a_qnorm_g")[l], 4)[:, None]
        gk = np.tile(f("a_knorm_g")[l], 4)[:, None]
        sub = (np.arange(128) // 32) % 2
        put("gqA", gq)
        put("gkA0", np.where(sub[:, None] == 0, gk, 0.0))
        put("gkA1", np.where(sub[:, None] == 1, gk, 0.0))
        put("a_out_g", _fm(f("a_out_g")[l]))
        put("b_out_g", _fm(f("b_out_g")[l]))
        cw = f("c_conv_w")[l]
        for cc in range(2):
            put("c_conv_w", cw[:, cc * 128:(cc + 1) * 128].T, cc * 4)
        for nm, key in (("c_conv_b", "c_conv_b"), ("c_ga_b", "c_gate_a_b"), ("c_gx_b", "c_gate_x_b"),
                        ("c_lambda", "c_lambda"), ("c_out_g", "c_out_g"), ("d_mu", "d_mu"), ("d_w0", "d_w0"),
                        ("d_a0", "d_a0"), ("d_k_k", "d_k_k"), ("d_k_a", "d_k_a"), ("d_lnx_w", "d_lnx_w"),
                        ("d_lnx_b", "d_lnx_b")):
            put(nm, _fm(f(key)[l]))
        put("d_r_k", _fm(f("d_r_k")[l].reshape(-1)))
        if l > 0:
            put("d_v0", _fm(f("d_v0")[l - 1]))
        fw = f("ff_conv_w")[l]
        for chn in range(44):
            put("ff_conv_w", fw[:, chn * 128:(chn + 1) * 128].T, chn * 3)
        put("ff_conv_b", _fm(f("ff_conv_b")[l]))
    gate = np.zeros((NL, 2, 2, 128, 128), np.float32)
    for l in range(NL):
        for gi, key in enumerate(("c_gate_a_w", "c_gate_x_w")):
            w = f(key)[l]
            for n in range(4):
                cc, hh = n // 2, n % 2
                gate[l, gi, cc, hh * 64:(hh + 1) * 64, hh * 64:(hh + 1) * 64] = w[n]
    lamqk = np.stack([np.tile(np.concatenate([f("a_lam_q")[l].reshape(1, 64), f("a_lam_k")[l].reshape(1, 64)], 1), (128, 1))
                      for l in range(NL)], 0).astype(np.float32)
    shared = {
        "cst": _consts(), "pv": pvs, "lamqk": lamqk,
        "w_ada": f("w_ada"), "w_in": f("w_in"), "w_out": f("w_out"),
        "ff_w_up": f("ff_w_up"), "ff_w_down": f("ff_w_down"),
        "gatew": gate, "d_w_up": f("d_w_up"), "d_a_up": f("d_a_up"), "d_g_up": f("d_g_up"),
        "d_v_down": f("d_v_down"), "d_v_up": f("d_v_up"),
    }
    return shared


def make_in_maps(inp, cores):
    shared = prep_shared(inp)
    x = np.asarray(inp["x"], np.float32)
    c = np.asarray(inp["c"], np.float32)
    maps = []
    for b in cores:
        m = dict(shared)
        m["x"] = np.ascontiguousarray(x[b])
        m["c"] = _fm(c[b])
        maps.append(m)
    return maps


_NC_CACHE = {}


def kernel(**inputs):
    if "nc" not in _NC_CACHE:
        _NC_CACHE["nc"] = build_program()
    nc = _NC_CACHE["nc"]
    maps = make_in_maps(inputs, list(range(8)))
    res = run_bass_kernel_spmd(nc, maps, core_ids=list(range(8)))
    return np.stack([np.asarray(r["out"], np.float32) for r in res.results], 0)
```
